# Optimizing a Trainium2 kernel written in Bass

```python
import jax
import jax.numpy as jnp
from jax import lax
import numpy as np

D_MODEL = 2048
BATCH = 1
SEQ = 16384
DEPTH = 2

MLA_HEADS = D_MODEL // 256
MLA_NOPE_DIM = 128
MLA_ROPE_DIM = 64
MLA_V_DIM = 128
Q_LORA = D_MODEL // 4
KV_LORA = D_MODEL // 8
SWA_HEADS = D_MODEL // 256
SWA_KV_HEADS = SWA_HEADS // 4
SWA_HEAD_DIM = 128
WINDOW = 128
Q_BLOCK = 128
ROPE_THETA = 10000.0
MLA_OUT = MLA_HEADS * MLA_V_DIM
SWA_OUT = SWA_HEADS * SWA_HEAD_DIM
D_MIX = MLA_OUT + SWA_OUT
C_Q_END = Q_LORA
C_KV_END = C_Q_END + KV_LORA
K_PE_END = C_KV_END + MLA_ROPE_DIM
Q_S_END = K_PE_END + SWA_HEADS * SWA_HEAD_DIM
K_S_END = Q_S_END + SWA_KV_HEADS * SWA_HEAD_DIM
IN_COLS = K_S_END + SWA_KV_HEADS * SWA_HEAD_DIM
D_FF_DENSE = 5632
N_EXPERTS = 8
TOP_K = 2
D_FF_EXPERT = 7168
MOE_BLOCK = 512
N_DENSE = (DEPTH + 1) // 2
N_MOE = DEPTH // 2
ALPHA = (2 * DEPTH) ** 0.25
BETA = (8 * DEPTH) ** -0.25
LN_EPS = 1e-5
RMS_EPS = 1e-6
NEG_INF = -1e30

kernel_name = 'hybrid_mla_swa_moe_deepnorm_encoder'


def layer_norm(x, g, b):
    xf = x.astype(jnp.float32)
    mu = jnp.mean(xf, axis=-1, keepdims=True)
    var = jnp.mean(jnp.square(xf - mu), axis=-1, keepdims=True)
    y = (xf - mu) * lax.rsqrt(var + LN_EPS) * g.astype(jnp.float32) + b.astype(jnp.float32)
    return y.astype(x.dtype)


def rms_norm(x, g):
    xf = x.astype(jnp.float32)
    y = xf * lax.rsqrt(jnp.mean(jnp.square(xf), axis=-1, keepdims=True) + RMS_EPS) * g.astype(jnp.float32)
    return y.astype(x.dtype)


def rope_tables(seq, dim):
    inv_freq = ROPE_THETA ** (-jnp.arange(0, dim, 2, dtype=jnp.float32) / dim)
    ang = jnp.arange(seq, dtype=jnp.float32)[:, None] * inv_freq[None, :]
    return jnp.cos(ang), jnp.sin(ang)


def apply_rope(t, cos, sin):
    tf = t.astype(jnp.float32)
    t1, t2 = jnp.split(tf, 2, axis=-1)
    c = cos[None, :, None, :]
    s = sin[None, :, None, :]
    return jnp.concatenate([t1 * c - t2 * s, t2 * c + t1 * s], axis=-1).astype(t.dtype)


def mla_attention(q_nope, q_pe, k_nope, k_pe, v):
    B, S, H, _ = q_nope.shape
    nb = S // Q_BLOCK
    scale = (MLA_NOPE_DIM + MLA_ROPE_DIM) ** -0.5
    qn = q_nope.reshape(B, nb, Q_BLOCK, H, MLA_NOPE_DIM).transpose(1, 0, 2, 3, 4)
    qp = q_pe.reshape(B, nb, Q_BLOCK, H, MLA_ROPE_DIM).transpose(1, 0, 2, 3, 4)

    def one_block(args):
        qn_b, qp_b = args
        s = (jnp.einsum('bqhd,bkhd->bhqk', qn_b, k_nope, preferred_element_type=jnp.float32)
             + jnp.einsum('bqhr,bkr->bhqk', qp_b, k_pe, preferred_element_type=jnp.float32)) * scale
        p = jax.nn.softmax(s, axis=-1).astype(v.dtype)
        return jnp.einsum('bhqk,bkhv->bqhv', p, v)

    o = lax.map(one_block, (qn, qp))
    return o.transpose(1, 0, 2, 3, 4).reshape(B, S, H, MLA_V_DIM)


def windowed_sink_gqa(q, k, v, sink):
    B, S, Hq, D = q.shape
    Hkv = k.shape[2]
    G = Hq // Hkv
    nb = S // Q_BLOCK
    qb = q.reshape(B, nb, Q_BLOCK, Hkv, G, D)
    pad = ((0, 0), (Q_BLOCK, Q_BLOCK), (0, 0), (0, 0))
    kp = jnp.pad(k, pad).reshape(B, nb + 2, Q_BLOCK, Hkv, D)
    vp = jnp.pad(v, pad).reshape(B, nb + 2, Q_BLOCK, Hkv, D)
    kb = jnp.concatenate([kp[:, :-2], kp[:, 1:-1], kp[:, 2:]], axis=2)
    vb = jnp.concatenate([vp[:, :-2], vp[:, 1:-1], vp[:, 2:]], axis=2)
    blk = jnp.arange(nb)[:, None] * Q_BLOCK
    qpos = blk + jnp.arange(Q_BLOCK)[None, :]
    kpos = blk - Q_BLOCK + jnp.arange(3 * Q_BLOCK)[None, :]
    mask = ((jnp.abs(qpos[:, :, None] - kpos[:, None, :]) <= WINDOW)
            & (kpos[:, None, :] >= 0) & (kpos[:, None, :] < S))
    s = jnp.einsum('bnqhgd,bnkhd->bnhgqk', qb, kb, preferred_element_type=jnp.float32) * (D ** -0.5)
    s = jnp.where(mask[None, :, None, None, :, :], s, NEG_INF)
    sink_f = sink.astype(jnp.float32).reshape(Hkv, G)[None, None, :, :, None, None]
    m = jnp.maximum(jnp.max(s, axis=-1, keepdims=True), sink_f)
    e = jnp.exp(s - m)
    p = (e / (jnp.sum(e, axis=-1, keepdims=True) + jnp.exp(sink_f - m))).astype(v.dtype)
    o = jnp.einsum('bnhgqk,bnkhd->bnqhgd', p, vb)
    return o.reshape(B, S, Hq, D)


def token_mixer(x, w_in, g_cq, w_qb, g_ckv, w_kvb, sink, g_out_mla, g_out_swa, w_out,
                cos_m, sin_m, cos_s, sin_s):
    B, S, _ = x.shape
    proj = x @ w_in
    c_q, c_kv, k_pe, q_s, k_s, v_s = jnp.split(
        proj, [C_Q_END, C_KV_END, K_PE_END, Q_S_END, K_S_END], axis=-1)
    q = (rms_norm(c_q, g_cq) @ w_qb).reshape(B, S, MLA_HEADS, MLA_NOPE_DIM + MLA_ROPE_DIM)
    q_nope = q[..., :MLA_NOPE_DIM]
    q_pe = apply_rope(q[..., MLA_NOPE_DIM:], cos_m, sin_m)
    kv = (rms_norm(c_kv, g_ckv) @ w_kvb).reshape(B, S, MLA_HEADS, MLA_NOPE_DIM + MLA_V_DIM)
    k_nope = kv[..., :MLA_NOPE_DIM]
    v_m = kv[..., MLA_NOPE_DIM:]
    k_pe = apply_rope(k_pe[:, :, None, :], cos_m, sin_m)[:, :, 0, :]
    o_mla = mla_attention(q_nope, q_pe, k_nope, k_pe, v_m).reshape(B, S, MLA_OUT)
    q_s = apply_rope(q_s.reshape(B, S, SWA_HEADS, SWA_HEAD_DIM), cos_s, sin_s)
    k_s = apply_rope(k_s.reshape(B, S, SWA_KV_HEADS, SWA_HEAD_DIM), cos_s, sin_s)
    v_s = v_s.reshape(B, S, SWA_KV_HEADS, SWA_HEAD_DIM)
    o_swa = windowed_sink_gqa(q_s, k_s, v_s, sink).reshape(B, S, SWA_OUT)
    merged = jnp.concatenate([rms_norm(o_mla, g_out_mla), rms_norm(o_swa, g_out_swa)], axis=-1)
    return merged @ w_out


def swiglu(x, wg, wu, wd):
    return (jax.nn.silu(x @ wg) * (x @ wu)) @ wd


def moe_swiglu(x, w_router, wg, wu, wd):
    B, S, D = x.shape
    N = B * S
    NK = N * TOP_K
    xf = x.reshape(N, D)
    logits = (xf @ w_router).astype(jnp.float32)
    top_l, top_e = lax.top_k(logits, TOP_K)
    gates = jax.nn.softmax(top_l, axis=-1).astype(x.dtype)
    flat_e = top_e.reshape(-1)
    flat_tok = jnp.repeat(jnp.arange(N, dtype=jnp.int32), TOP_K)
    flat_g = gates.reshape(-1)
    order = jnp.argsort(flat_e)
    se, stok, sg = flat_e[order], flat_tok[order], flat_g[order]
    counts = jnp.bincount(flat_e, length=N_EXPERTS)
    offs = jnp.cumsum(counts) - counts
    padded = (counts + MOE_BLOCK - 1) // MOE_BLOCK * MOE_BLOCK
    pad_end = jnp.cumsum(padded)
    pad_offs = pad_end - padded
    dest = pad_offs[se] + jnp.arange(NK, dtype=jnp.int32) - offs[se]
    cap = -(-NK // MOE_BLOCK) * MOE_BLOCK + N_EXPERTS * MOE_BLOCK
    n_blk = cap // MOE_BLOCK
    tok_buf = jnp.zeros((cap,), jnp.int32).at[dest].set(stok)
    gate_buf = jnp.zeros((cap,), x.dtype).at[dest].set(sg)
    blk_e = jnp.minimum(jnp.searchsorted(pad_end, jnp.arange(n_blk) * MOE_BLOCK, side='right'),
                        N_EXPERTS - 1).astype(jnp.int32)
    xs = xf[tok_buf].reshape(n_blk, MOE_BLOCK, D)

    def expert_block(args):
        xb, e = args
        return (jax.nn.silu(xb @ wg[e]) * (xb @ wu[e])) @ wd[e]

    ys = lax.map(expert_block, (xs, blk_e)).reshape(cap, D)
    out = jnp.zeros((N, D), x.dtype).at[tok_buf].add(ys * gate_buf[:, None])
    return out.reshape(B, S, D)


def setup_inputs(seed: int = 0) -> dict:
    key = jax.random.key(seed)
    ks = jax.random.split(key, 24)

    def nrm(k, shape, scale):
        return jax.random.normal(k, shape, jnp.float32) * scale

    def gain(k, shape):
        return 1.0 + 0.02 * jax.random.normal(k, shape, jnp.float32)

    x = jax.random.normal(ks[0], (BATCH, SEQ, D_MODEL), jnp.float32)
    v_cols = SWA_KV_HEADS * SWA_HEAD_DIM
    in_scale = jnp.concatenate([jnp.ones((IN_COLS - v_cols,), jnp.float32),
                                jnp.full((v_cols,), BETA, jnp.float32)])
    w_in = nrm(ks[1], (DEPTH, D_MODEL, IN_COLS), D_MODEL ** -0.5) * in_scale
    g_cq = gain(ks[2], (DEPTH, Q_LORA))
    w_qb = nrm(ks[3], (DEPTH, Q_LORA, MLA_HEADS * (MLA_NOPE_DIM + MLA_ROPE_DIM)), Q_LORA ** -0.5)
    g_ckv = gain(ks[4], (DEPTH, KV_LORA))
    kv_scale = jnp.tile(jnp.concatenate([jnp.ones((MLA_NOPE_DIM,), jnp.float32),
                                         jnp.full((MLA_V_DIM,), BETA, jnp.float32)]), MLA_HEADS)
    w_kvb = nrm(ks[5], (DEPTH, KV_LORA, MLA_HEADS * (MLA_NOPE_DIM + MLA_V_DIM)), KV_LORA ** -0.5) * kv_scale
    sink = nrm(ks[6], (DEPTH, SWA_HEADS), 0.5)
    g_out_mla = gain(ks[7], (DEPTH, MLA_OUT))
    g_out_swa = gain(ks[8], (DEPTH, SWA_OUT))
    w_out = nrm(ks[9], (DEPTH, D_MIX, D_MODEL), D_MIX ** -0.5 * BETA)
    ln1_g = gain(ks[10], (DEPTH, D_MODEL))
    ln1_b = nrm(ks[11], (DEPTH, D_MODEL), 0.02)
    dense_wg = nrm(ks[12], (N_DENSE, D_MODEL, D_FF_DENSE), D_MODEL ** -0.5 * BETA)
    dense_wu = nrm(ks[13], (N_DENSE, D_MODEL, D_FF_DENSE), D_MODEL ** -0.5 * BETA)
    dense_wd = nrm(ks[14], (N_DENSE, D_FF_DENSE, D_MODEL), D_FF_DENSE ** -0.5 * BETA)
    router_w = nrm(ks[15], (N_MOE, D_MODEL, N_EXPERTS), D_MODEL ** -0.5)
    moe_wg = nrm(ks[16], (N_MOE, N_EXPERTS, D_MODEL, D_FF_EXPERT), D_MODEL ** -0.5 * BETA)
    moe_wu = nrm(ks[17], (N_MOE, N_EXPERTS, D_MODEL, D_FF_EXPERT), D_MODEL ** -0.5 * BETA)
    moe_wd = nrm(ks[18], (N_MOE, N_EXPERTS, D_FF_EXPERT, D_MODEL), D_FF_EXPERT ** -0.5 * BETA)
    ln2_g = gain(ks[19], (DEPTH, D_MODEL))
    ln2_b = nrm(ks[20], (DEPTH, D_MODEL), 0.02)
    return {'x': x, 'w_in': w_in, 'g_cq': g_cq, 'w_qb': w_qb, 'g_ckv': g_ckv, 'w_kvb': w_kvb,
            'sink': sink, 'g_out_mla': g_out_mla, 'g_out_swa': g_out_swa, 'w_out': w_out,
            'ln1_g': ln1_g, 'ln1_b': ln1_b, 'dense_wg': dense_wg, 'dense_wu': dense_wu,
            'dense_wd': dense_wd, 'router_w': router_w, 'moe_wg': moe_wg, 'moe_wu': moe_wu,
            'moe_wd': moe_wd, 'ln2_g': ln2_g, 'ln2_b': ln2_b}


def reference(x, w_in, g_cq, w_qb, g_ckv, w_kvb, sink, g_out_mla, g_out_swa, w_out,
              ln1_g, ln1_b, dense_wg, dense_wu, dense_wd, router_w, moe_wg, moe_wu, moe_wd,
              ln2_g, ln2_b):
    S = x.shape[1]
    cos_m, sin_m = rope_tables(S, MLA_ROPE_DIM)
    cos_s, sin_s = rope_tables(S, SWA_HEAD_DIM)
    for l in range(DEPTH):
        mix = token_mixer(x, w_in[l], g_cq[l], w_qb[l], g_ckv[l], w_kvb[l], sink[l],
                          g_out_mla[l], g_out_swa[l], w_out[l], cos_m, sin_m, cos_s, sin_s)
        x = layer_norm(ALPHA * x + mix, ln1_g[l], ln1_b[l])
        j = l // 2
        if l % 2 == 0:
            ffn = swiglu(x, dense_wg[j], dense_wu[j], dense_wd[j])
        else:
            ffn = moe_swiglu(x, router_w[j], moe_wg[j], moe_wu[j], moe_wd[j])
        x = layer_norm(ALPHA * x + ffn, ln2_g[l], ln2_b[l])
    return x
```

```python
import numpy as np
import concourse.bass as bass
import concourse.mybir as mybir
from concourse.bass_utils import run_bass_kernel_spmd

F32 = mybir.dt.float32
BF16 = mybir.dt.bfloat16
AF = mybir.ActivationFunctionType
ALU = mybir.AluOpType
AX = mybir.AxisListType

PE, ACT, DVE, POOL, SP = "tensor", "scalar", "vector", "gpsimd", "sync"
ENGS = (PE, ACT, DVE, POOL, SP)
SEM_ROLL = 30000

T = 2048
D = 2048
HT = 1024
NC_IN = 3712
KVROWS = 16 * 128 + 64
ALPHA = 4 ** 0.25
LN_EPS = 1e-5
RMS_EPS = 1e-6
NEG = -1e30
NCORE = 8
S = 16384
DEBUG_COUNTS = False
MOE_CAPMAX = 896


class Buf:
    __slots__ = ("name", "w", "r", "dsem")

    def __init__(self, name=""):
        self.name = name
        self.w = None
        self.r = {}
        self.dsem = None


class Prog:
    def __init__(self, nc):
        self.nc = nc
        self.streams = {e: [] for e in ENGS}
        self.esems = {e: None for e in ENGS}
        self.waited = {}
        self.all_sems = []

    def _new_sem(self, name):
        h = self.nc.alloc_semaphore(name)
        rec = [h, 0]
        self.all_sems.append(rec)
        return rec

    def _eng_sem(self, eng):
        rec = self.esems[eng]
        if rec is None or rec[1] >= SEM_ROLL:
            rec = self._new_sem(f"e_{eng}_{len(self.all_sems)}")
            self.esems[eng] = rec
        return rec

    def _collect(self, eng, reads, writes, is_dma):
        waits = {}

        def need(tok, same_ok):
            if tok is None:
                return
            rec, val, src, dma = tok
            if dma:
                val = rec[1]
            elif src == eng and not is_dma:
                if eng == PE or same_ok:
                    return
            k = id(rec)
            if waits.get(k, (None, -1))[1] < val:
                waits[k] = (rec, val)

        for b in reads:
            need(b.w, False)
        for b in writes:
            need(b.w, True)
            for t in b.r.values():
                need(t, True)
        out = []
        for k, (rec, val) in waits.items():
            key = (eng, k)
            if self.waited.get(key, -1) >= val:
                continue
            self.waited[key] = val
            out.append((rec[0], val))
        return out

    def op(self, eng, name, args, reads=(), writes=(), sig=True, own_sem_inc=None):
        def fn(e, name=name, args=args):
            return getattr(e, name)(**args)
        waits = self._collect(eng, reads, writes, False)
        tok = None
        if own_sem_inc is not None:
            rec = self._new_sem(f"own_{len(self.all_sems)}")
            rec[1] += own_sem_inc
            tok = (rec, rec[1], eng, True)
            self.streams[eng].append((waits, fn, (rec[0], own_sem_inc)))
        elif sig:
            rec = self._eng_sem(eng)
            rec[1] += 1
            tok = (rec, rec[1], eng, False)
            self.streams[eng].append((waits, fn, (rec[0], 1)))
        else:
            self.streams[eng].append((waits, fn, None))
        if tok is not None:
            for b in writes:
                b.w = tok
                b.r = {}
            for b in reads:
                b.r[id(rec)] = tok
        return tok

    def dma(self, eng, out, in_, reads=(), writes=(), sembuf=None, **kw):
        waits = self._collect(eng, reads, writes, True)
        sb = sembuf if sembuf is not None else (writes[0] if writes else reads[0])
        if sb.dsem is None:
            sb.dsem = self._new_sem(f"d_{sb.name}_{len(self.all_sems)}")
        rec = sb.dsem
        rec[1] += 16
        tok = (rec, rec[1], eng, True)

        def fn(e, out=out, in_=in_, kw=kw):
            return e.dma_start(out=out, in_=in_, **kw)
        self.streams[eng].append((waits, fn, (rec[0], 16)))
        for b in writes:
            b.w = tok
            b.r = {}
        for b in reads:
            b.r[id(rec)] = tok
        return tok

    def barrier(self):
        for eng in ENGS:
            waits = []
            for rec in self.all_sems:
                if rec[1] > 0 and self.waited.get((eng, id(rec)), -1) < rec[1]:
                    self.waited[(eng, id(rec))] = rec[1]
                    waits.append((rec[0], rec[1]))
            if waits:
                self.streams[eng].append((waits, None, None))

    def final_wait(self, eng, bufs):
        waits = self._collect(eng, bufs, (), True)
        self.streams[eng].append((waits, None, None))

    def emit(self, block):
        nc = self.nc

        def make(eng):
            stream = self.streams[eng]

            def body(e):
                for waits, fn, inc in stream:
                    for h, v in waits:
                        e.wait_ge(h, v)
                    if fn is not None:
                        ins = fn(e)
                        if inc is not None:
                            ins.then_inc(inc[0], inc[1])
            return body
        block.tensor(make(PE))
        block.scalar(make(ACT))
        block.vector(make(DVE))
        block.gpsimd(make(POOL))
        block.sync(make(SP))


SB_BASE = 16512
SB_END = 229376
_DT_BYTES = {F32: 4, BF16: 2}


class SBAlloc:
    def __init__(self, nc):
        self.nc = nc
        self.off = SB_BASE
        self.n = 0

    def alloc(self, name, shape, dtype):
        size = 1
        for d in shape[1:]:
            size *= d
        size *= _DT_BYTES[dtype]
        size = (size + 31) // 32 * 32
        assert self.off + size <= SB_END, f"SBUF overflow at {name}: {self.off + size}"
        t = self.nc.alloc_sbuf_tensor_at(f"{name}_{self.n}", list(shape), dtype, offset=self.off)
        self.n += 1
        self.off += size
        return t

    def mark(self):
        return self.off

    def reset(self, m):
        self.off = m


def mm_acc(P, out_ap, out_buf, pairs, reads):
    n = len(pairs)
    for i, (l, r) in enumerate(pairs):
        P.op(PE, "matmul", dict(out=out_ap, lhsT=l, rhs=r, start=(i == 0), stop=(i == n - 1)),
             reads=reads, writes=[out_buf], sig=(i == n - 1))


import ml_dtypes

BF = ml_dtypes.bfloat16
ROPE_THETA = 10000.0


def rope_np(dim):
    inv = np.power(np.float32(ROPE_THETA), -(np.arange(0, dim, 2, dtype=np.float32) / np.float32(dim))).astype(np.float32)
    ang = np.arange(S, dtype=np.float32)[:, None] * inv[None, :]
    return np.cos(ang).astype(np.float32), np.sin(ang).astype(np.float32)


def rope_tables_fm(dim):
    c, s = rope_np(dim)
    cf = np.concatenate([c, c], axis=1).T
    sf = np.concatenate([-s, s], axis=1).T
    return np.ascontiguousarray(cf), np.ascontiguousarray(sf)


def perm_half(w, dim):
    n = w.shape[1] // dim
    idx = np.concatenate([(np.arange(dim) + dim // 2) % dim + b * dim for b in range(n)])
    return w[:, idx]


def prep_w_in(w):
    c_q = w[:, 0:512]
    c_kv = w[:, 512:768]
    k_pe = w[:, 768:832]
    q_s = w[:, 832:1856]
    k_s = w[:, 1856:2112]
    v_s = w[:, 2112:2368]
    cols = [c_q, c_kv, k_pe, perm_half(k_pe, 64)]
    qsp = perm_half(q_s, 128)
    for h in range(8):
        cols += [q_s[:, h * 128:(h + 1) * 128], qsp[:, h * 128:(h + 1) * 128]]
    ksp = perm_half(k_s, 128)
    for h in range(2):
        cols += [k_s[:, h * 128:(h + 1) * 128], ksp[:, h * 128:(h + 1) * 128]]
    cols.append(v_s)
    out = np.ascontiguousarray(np.concatenate(cols, axis=1))
    assert out.shape[1] == 3712
    return out


def prep_w_qb(w):
    cols = []
    for h in range(8):
        blk = w[:, h * 192:(h + 1) * 192]
        pe = blk[:, 128:192]
        cols += [blk[:, :128], pe, perm_half(pe, 64)]
    return np.ascontiguousarray(np.concatenate(cols, axis=1))


def prep_w_kvb(w):
    ks = [w[:, h * 256:h * 256 + 128] for h in range(8)]
    vs = [w[:, h * 256 + 128:(h + 1) * 256] for h in range(8)]
    return np.ascontiguousarray(np.concatenate(ks + vs, axis=1))


def fm_vec(g, nchunk):
    return np.ascontiguousarray(g.reshape(nchunk, 128).T)


_TABS = {}


def tables():
    if not _TABS:
        _TABS["m"] = rope_tables_fm(64)
        _TABS["s"] = rope_tables_fm(128)
    return _TABS


def inputs_A(xs, l, inp):
    tb = tables()
    w_in = prep_w_in(inp["w_in"][l])
    w_qb = prep_w_qb(inp["w_qb"][l])
    w_kvb = prep_w_kvb(inp["w_kvb"][l])
    g_cq = fm_vec(inp["g_cq"][l], 4)
    g_ckv = fm_vec(inp["g_ckv"][l], 2)
    maps = []
    for c in range(NCORE):
        sl = slice(c * T, (c + 1) * T)
        maps.append({
            "x": np.ascontiguousarray(xs[c]), "w_in": w_in, "w_qb": w_qb, "w_kvb": w_kvb,
            "g_cq": g_cq, "g_ckv": g_ckv,
            "cos_m": np.ascontiguousarray(tb["m"][0][:, sl]), "sin_m": np.ascontiguousarray(tb["m"][1][:, sl]),
            "cos_s": np.ascontiguousarray(tb["s"][0][:, sl]), "sin_s": np.ascontiguousarray(tb["s"][1][:, sl]),
        })
    return maps


def swa_masks(core):
    qi = np.arange(128)[:, None]
    ki = np.arange(128)[None, :]
    prev = np.where(qi <= ki, 0.0, NEG).astype(np.float32)
    mid = np.zeros((128, 128), np.float32)
    nxt = np.where(ki <= qi, 0.0, NEG).astype(np.float32)
    full = np.concatenate([prev, mid, nxt], axis=1)
    allneg = np.full((128, 128), NEG, np.float32)
    first = full.copy()
    last = full.copy()
    if core == 0:
        first[:, :128] = allneg
    if core == NCORE - 1:
        last[:, 256:] = allneg
    return np.ascontiguousarray(np.stack([first, full, last], axis=1))


def inputs_Bb(l, inp, resBa):
    j = l // 2
    ln = np.stack([inp["ln1_g"][l], inp["ln1_b"][l], inp["ln2_g"][l], inp["ln2_b"][l]], 0)
    ln_bc = np.ascontiguousarray(np.broadcast_to(ln[None], (128, 4, 2048))).astype(np.float32)
    iota_row = np.ascontiguousarray(np.broadcast_to(np.arange(MOE_CAPMAX, dtype=np.float32)[None, :], (128, MOE_CAPMAX)))
    maps = []
    for c in range(NCORE):
        maps.append({"ln_bc": ln_bc, "iota_row": iota_row,
                     "wg": inp["moe_wg"][j], "wu": inp["moe_wu"][j], "wd": inp["moe_wd"][j],
                     "x1s_in": resBa[c]["x1s"], "xb_in": resBa[c]["xb_d"],
                     "gates_in": resBa[c]["gates_o"], "rank_in": resBa[c]["rank_o"]})
    return maps


def inputs_B(xs, l, inp, resA, moe):
    kv_all = np.ascontiguousarray(np.concatenate([resA[c]["kv_out"] for c in range(NCORE)], axis=0))
    sink_bc = np.ascontiguousarray(np.broadcast_to(inp["sink"][l][None, :], (128, 8))).astype(np.float32)
    g_mla = fm_vec(inp["g_out_mla"][l], 8)
    g_swa = np.ascontiguousarray(np.broadcast_to(inp["g_out_swa"][l][None, :], (128, 1024))).astype(np.float32)
    ln = np.stack([inp["ln1_g"][l], inp["ln1_b"][l], inp["ln2_g"][l], inp["ln2_b"][l]], 0)
    ln_bc = np.ascontiguousarray(np.broadcast_to(ln[None], (128, 4, 2048))).astype(np.float32)
    j = l // 2
    maps = []
    for c in range(NCORE):
        ks = resA[c]["ks_out"]
        vs = resA[c]["vs_out"]
        kprev = resA[c - 1]["ks_out"][:, -128:] if c > 0 else np.zeros((256, 128), ks.dtype)
        knext = resA[c + 1]["ks_out"][:, :128] if c < NCORE - 1 else np.zeros((256, 128), ks.dtype)
        vprev = resA[c - 1]["vs_out"][-128:] if c > 0 else np.zeros((128, 256), vs.dtype)
        vnext = resA[c + 1]["vs_out"][:128] if c < NCORE - 1 else np.zeros((128, 256), vs.dtype)
        m = {
            "x": np.ascontiguousarray(xs[c]), "kv_all": kv_all,
            "qn": resA[c]["qn_out"], "qpe": resA[c]["qpe_out"], "qs": resA[c]["qs_out"],
            "ks_ext": np.ascontiguousarray(np.concatenate([kprev, ks, knext], axis=1)),
            "vs_ext": np.ascontiguousarray(np.concatenate([vprev, vs, vnext], axis=0)),
            "masks": swa_masks(c), "sink_bc": sink_bc, "g_mla": g_mla, "g_swa_bc": g_swa,
            "w_out": inp["w_out"][l], "ln_bc": ln_bc,
        }
        if moe:
            lmat = (np.arange(128)[:, None] < np.arange(128)[None, :]).astype(np.float32).astype(BF)
            m.update({"w_router": inp["router_w"][j], "lmat": lmat})
        else:
            m.update({"wg": inp["dense_wg"][j], "wu": inp["dense_wu"][j], "wd": inp["dense_wd"][j]})
        maps.append(m)
    return maps


def build_A():
    nc = bass.Bass("TRN2", target_bir_lowering=False)
    x = nc.dram_tensor("x", [T, D], F32, kind="ExternalInput").ap()
    w_in = nc.dram_tensor("w_in", [D, NC_IN], F32, kind="ExternalInput").ap()
    w_qb = nc.dram_tensor("w_qb", [512, 2048], F32, kind="ExternalInput").ap()
    w_kvb = nc.dram_tensor("w_kvb", [256, 2048], F32, kind="ExternalInput").ap()
    g_cq = nc.dram_tensor("g_cq", [128, 4], F32, kind="ExternalInput").ap()
    g_ckv = nc.dram_tensor("g_ckv", [128, 2], F32, kind="ExternalInput").ap()
    cos_m = nc.dram_tensor("cos_m", [64, T], F32, kind="ExternalInput").ap()
    sin_m = nc.dram_tensor("sin_m", [64, T], F32, kind="ExternalInput").ap()
    cos_s = nc.dram_tensor("cos_s", [128, T], F32, kind="ExternalInput").ap()
    sin_s = nc.dram_tensor("sin_s", [128, T], F32, kind="ExternalInput").ap()
    kv_out = nc.dram_tensor("kv_out", [16 * 128 + 64, T], BF16, kind="ExternalOutput").ap()
    qn_out = nc.dram_tensor("qn_out", [8 * 128, T], BF16, kind="ExternalOutput").ap()
    qpe_out = nc.dram_tensor("qpe_out", [8 * 64, T], BF16, kind="ExternalOutput").ap()
    qs_out = nc.dram_tensor("qs_out", [8 * 128, T], BF16, kind="ExternalOutput").ap()
    ks_out = nc.dram_tensor("ks_out", [2 * 128, T], BF16, kind="ExternalOutput").ap()
    vs_out = nc.dram_tensor("vs_out", [T, 256], BF16, kind="ExternalOutput").ap()

    P = Prog(nc)
    sb = SBAlloc(nc)
    ps = nc.alloc_psum_tensor("ps", [128, 8, 512], F32)
    psb = [Buf(f"ps{i}") for i in range(8)]
    pctr = [0]

    def bank():
        i = pctr[0] % 8
        pctr[0] += 1
        return ps[:, i, :], psb[i]

    b_out = Buf("out")

    ident = sb.alloc("ident", [128, 128], BF16)
    ones = sb.alloc("ones", [128, 128], BF16)
    gq = sb.alloc("gq", [128, 4], F32)
    gkv = sb.alloc("gkv", [128, 2], F32)
    b_c = Buf("consts")
    P.op(POOL, "memset", dict(ap=ident[:], constant=0.0), writes=[b_c])
    P.op(POOL, "affine_select", dict(out=ident[:], in_=ident[:], pattern=[[-1, 128]],
                                     compare_op=ALU.not_equal, fill=1.0, base=0,
                                     channel_multiplier=1), reads=[b_c], writes=[b_c])
    P.op(POOL, "memset", dict(ap=ones[:], constant=1.0), reads=[b_c], writes=[b_c])
    b_g = Buf("g")
    P.dma(SP, gq[:], g_cq[:, :], writes=[b_g])
    P.dma(SP, gkv[:], g_ckv[:, :], writes=[b_g])

    cosm = sb.alloc("cosm", [64, HT], F32)
    sinm = sb.alloc("sinm", [64, HT], F32)
    coss = sb.alloc("coss", [128, HT], F32)
    sins = sb.alloc("sins", [128, HT], F32)
    cqn = sb.alloc("cqn", [128, 4, HT], BF16)
    ckvn = sb.alloc("ckvn", [128, 2, HT], BF16)
    kpeT = sb.alloc("kpeT", [64, HT], BF16)
    b_tab = Buf("tab")
    b_cqn = [Buf(f"cqn{i}") for i in range(2)]
    b_ckvn = [Buf(f"ckvn{i}") for i in range(2)]
    b_kpe = Buf("kpe")
    m_phase = sb.mark()

    w_in_v = w_in.rearrange("(k p) c -> p k c", p=128)
    groups = [(0, 512), (512, 896)] + [(896 + 512 * i, 896 + 512 * (i + 1)) for i in range(4)] + \
             [(2944, 3456), (3456, 3712)]

    for half in range(2):
        t0 = half * HT
        sb.reset(m_phase)
        xT = sb.alloc("xT", [128, 16, HT], BF16)
        b_xT = [Buf(f"xT{i}") for i in range(HT // 128)]
        xt = [sb.alloc(f"xt{r}", [128, D], BF16) for r in range(2)]
        b_xt = [Buf(f"xt{r}") for r in range(2)]
        wring = [sb.alloc(f"wr{r}", [128, 16, 512], BF16) for r in range(2)]
        b_wr = [Buf(f"wr{r}") for r in range(2)]
        sq = sb.alloc("sq", [128, 4, 512], BF16)
        b_sq = Buf("sq")
        rstd = sb.alloc("rstd", [128, 512], F32)
        b_rstd = Buf("rstd")
        tmp = [sb.alloc(f"tmp{r}", [128, 512], F32) for r in range(4)]
        b_tmp = [Buf(f"tmp{r}") for r in range(4)]
        stg = [sb.alloc(f"stg{r}", [128, HT], BF16) for r in range(4)]
        b_stg = [Buf(f"stg{r}") for r in range(4)]
        vstg = sb.alloc("vstg", [128, HT // 128, 256], BF16)
        b_vstg = Buf("vstg")

        P.dma(SP, cosm[:], cos_m[:, t0:t0 + HT], writes=[b_tab])
        P.dma(SP, sinm[:], sin_m[:, t0:t0 + HT], writes=[b_tab])
        P.dma(SP, coss[:], cos_s[:, t0:t0 + HT], writes=[b_tab])
        P.dma(SP, sins[:], sin_s[:, t0:t0 + HT], writes=[b_tab])

        for i in range(HT // 128):
            r = i % 2
            P.dma(POOL, xt[r][:], x[t0 + i * 128:t0 + (i + 1) * 128, :], writes=[b_xt[r]])
            j = (pctr[0] // 2) % 4
            pctr[0] += 2
            pT = ps[:, 2 * j:2 * j + 2, :].bitcast(BF16).rearrange("p a (b c) -> p (a b) c", c=128)
            pbufs = [psb[2 * j], psb[2 * j + 1]]
            for k in range(16):
                P.op(PE, "transpose", dict(out=pT[:, k, :], in_=xt[r][:, k * 128:(k + 1) * 128], identity=ident[:]),
                     reads=[b_xt[r], b_c], writes=pbufs, sig=(k == 15))
            eng = ACT if i % 2 == 0 else DVE
            if eng == ACT:
                P.op(ACT, "copy", dict(out=xT[:, :, i * 128:(i + 1) * 128], in_=pT[:, :, :]),
                     reads=pbufs, writes=[b_xT[i]])
            else:
                P.op(DVE, "tensor_copy", dict(out=xT[:, :, i * 128:(i + 1) * 128], in_=pT[:, :, :]),
                     reads=pbufs, writes=[b_xT[i]])

        def rms_block(chunks, nfeat, gvec, dst, dstbuf, tb):
            n = len(chunks)
            for c, (pa, pb) in enumerate(chunks):
                P.op(ACT, "activation", dict(out=sq[:, c, :], in_=pa, func=AF.Square),
                     reads=[pb], writes=[b_sq])
            sa, sbf = bank()
            mm_acc(P, sa, sbf, [(ones[:], sq[:, c, :]) for c in range(n)], [b_sq, b_c])
            P.op(DVE, "tensor_scalar", dict(out=rstd[:], in0=sa, scalar1=1.0 / nfeat, scalar2=RMS_EPS,
                                            op0=ALU.mult, op1=ALU.add), reads=[sbf], writes=[b_rstd])
            P.op(ACT, "activation", dict(out=rstd[:], in_=rstd[:], func=AF.Sqrt), reads=[b_rstd], writes=[b_rstd])
            P.op(DVE, "reciprocal", dict(out=rstd[:], in_=rstd[:]), reads=[b_rstd], writes=[b_rstd])
            for c, (pa, pb) in enumerate(chunks):
                P.op(DVE, "scalar_tensor_tensor", dict(
                    out=dst[:, c, tb * 512:(tb + 1) * 512], in0=pa, scalar=gvec[:, c:c + 1], in1=rstd[:],
                    op0=ALU.mult, op1=ALU.mult), reads=[pb, b_rstd, b_g], writes=[dstbuf])

        tctr = [0]

        def rope_block(pa, pb_a, pu, pb_u, cosT, sinT, nrow, dst_ap, dst_buf, tb):
            i0 = tctr[0] % 4
            i1 = (tctr[0] + 1) % 4
            tctr[0] += 2
            cs = slice(tb * 512, (tb + 1) * 512)
            P.op(DVE, "tensor_tensor", dict(out=tmp[i0][:nrow, :], in0=pa, in1=cosT[:nrow, cs], op=ALU.mult),
                 reads=[pb_a, b_tab], writes=[b_tmp[i0]])
            P.op(DVE, "tensor_tensor", dict(out=tmp[i1][:nrow, :], in0=pu, in1=sinT[:nrow, cs], op=ALU.mult),
                 reads=[pb_u, b_tab], writes=[b_tmp[i1]])
            P.op(POOL, "tensor_tensor", dict(out=dst_ap, in0=tmp[i0][:nrow, :], in1=tmp[i1][:nrow, :], op=ALU.add),
                 reads=[b_tmp[i0], b_tmp[i1]], writes=[dst_buf])

        sctr = [0]
        for gi, (c0, c1) in enumerate(groups):
            r = gi % 2
            ncol = c1 - c0
            P.dma(POOL, wring[r][:, :, :ncol], w_in_v[:, :, c0:c1], writes=[b_wr[r]])
            W = wring[r]
            if gi == 7:
                for i in range(HT // 128):
                    pa, pb = bank()
                    mm_acc(P, pa[:, :256], pb,
                           [(xT[:, k, i * 128:(i + 1) * 128], W[:, k, 0:256]) for k in range(16)],
                           [b_xT[i], b_wr[r]])
                    P.op(ACT, "copy", dict(out=vstg[:, i, :], in_=pa[:, :256]),
                         reads=[pb], writes=[b_vstg])
                P.dma(SP, vs_out[t0:t0 + HT, :].rearrange("(i p) c -> p i c", p=128), vstg[:],
                      reads=[b_vstg], writes=[b_out])
                continue
            if gi >= 2:
                sidx = [sctr[0] % 4, (sctr[0] + 1) % 4]
                sctr[0] += 2
            for tb in range(HT // 512):
                xb = [b_xT[4 * tb + q] for q in range(4)]
                cs = slice(tb * 512, (tb + 1) * 512)

                def chunk(col0, ncols_):
                    pa, pb = bank()
                    mm_acc(P, pa[:ncols_, :], pb,
                           [(W[:, k, col0:col0 + ncols_], xT[:, k, cs]) for k in range(16)],
                           xb + [b_wr[r]])
                    return pa, pb
                if gi == 0:
                    chunks = [chunk(c * 128, 128) for c in range(4)]
                    rms_block(chunks, 512, gq, cqn, b_cqn[tb], tb)
                elif gi == 1:
                    chunks = [chunk(c * 128, 128) for c in range(2)]
                    rms_block(chunks, 256, gkv, ckvn, b_ckvn[tb], tb)
                    pa, pb = chunk(256, 64)
                    pu, pbu = chunk(320, 64)
                    rope_block(pa[:64, :], pb, pu[:64, :], pbu, cosm, sinm, 64, kpeT[:, cs], b_kpe, tb)
                else:
                    for hh in range(2):
                        pa, pb = chunk(hh * 256, 128)
                        pu, pbu = chunk(hh * 256 + 128, 128)
                        rope_block(pa, pb, pu, pbu, coss, sins, 128, stg[sidx[hh]][:, cs], b_stg[sidx[hh]], tb)
            if gi >= 2:
                for hh in range(2):
                    if gi <= 5:
                        h = (gi - 2) * 2 + hh
                        dst = qs_out[h * 128:(h + 1) * 128, t0:t0 + HT]
                    else:
                        dst = ks_out[hh * 128:(hh + 1) * 128, t0:t0 + HT]
                    P.dma(SP, dst, stg[sidx[hh]][:], reads=[b_stg[sidx[hh]]], writes=[b_out])

        P.dma(SP, kv_out[16 * 128:16 * 128 + 64, t0:t0 + HT], kpeT[:], reads=[b_kpe], writes=[b_out])

        P.barrier()
        sb.reset(m_phase)
        wqb = sb.alloc("wqb", [128, 4, 2048], BF16)
        wkvb = sb.alloc("wkvb", [128, 2, 2048], BF16)
        b_wqb, b_wkvb = Buf("wqb"), Buf("wkvb")
        qn = sb.alloc("qn", [128, 8, HT], BF16)
        qpe = sb.alloc("qpe", [64, 8, HT], BF16)
        KT = sb.alloc("KT", [128, 8, HT], BF16)
        Vt = sb.alloc("Vt", [128, HT // 128, 1024], BF16)
        b_qn, b_qpe, b_KT, b_Vt = Buf("qn"), Buf("qpe"), Buf("KT"), Buf("Vt")
        tmp = [sb.alloc(f"tmpb{r}", [128, 512], F32) for r in range(4)]
        b_tmp = [Buf(f"tmpb{r}") for r in range(4)]
        P.dma(POOL, wqb[:], w_qb.rearrange("(k p) c -> p k c", p=128), writes=[b_wqb])
        P.dma(POOL, wkvb[:], w_kvb.rearrange("(k p) c -> p k c", p=128), writes=[b_wkvb])
        for h in range(8):
            for tb in range(HT // 512):
                cs = slice(tb * 512, (tb + 1) * 512)
                pa, pb = bank()
                mm_acc(P, pa, pb, [(wqb[:, c, 256 * h:256 * h + 128], cqn[:, c, cs]) for c in range(4)],
                       [b_wqb, b_cqn[tb]])
                P.op(ACT, "copy", dict(out=qn[:, h, cs], in_=pa), reads=[pb], writes=[b_qn])
                pa, pb = bank()
                mm_acc(P, pa[:64, :], pb, [(wqb[:, c, 256 * h + 128:256 * h + 192], cqn[:, c, cs]) for c in range(4)],
                       [b_wqb, b_cqn[tb]])
                pu, pbu = bank()
                mm_acc(P, pu[:64, :], pbu, [(wqb[:, c, 256 * h + 192:256 * h + 256], cqn[:, c, cs]) for c in range(4)],
                       [b_wqb, b_cqn[tb]])
                rope_block(pa[:64, :], pb, pu[:64, :], pbu, cosm, sinm, 64, qpe[:, h, cs], b_qpe, tb)
                pa, pb = bank()
                mm_acc(P, pa, pb, [(wkvb[:, c, 128 * h:128 * h + 128], ckvn[:, c, cs]) for c in range(2)],
                       [b_wkvb, b_ckvn[tb]])
                P.op(ACT, "copy", dict(out=KT[:, h, cs], in_=pa), reads=[pb], writes=[b_KT])
        for i in range(HT // 128):
            tb = i // 4
            for hf in range(2):
                pa, pb = bank()
                mm_acc(P, pa, pb,
                       [(ckvn[:, c, i * 128:(i + 1) * 128], wkvb[:, c, 1024 + hf * 512:1024 + (hf + 1) * 512])
                        for c in range(2)], [b_wkvb, b_ckvn[tb]])
                P.op(DVE, "tensor_copy", dict(out=Vt[:, i, hf * 512:(hf + 1) * 512], in_=pa),
                     reads=[pb], writes=[b_Vt])
        P.dma(SP, qn_out.rearrange("(h p) t -> p h t", p=128)[:, :, t0:t0 + HT], qn[:], reads=[b_qn], writes=[b_out])
        P.dma(SP, qpe_out.rearrange("(h p) t -> p h t", p=64)[:, :, t0:t0 + HT], qpe[:], reads=[b_qpe], writes=[b_out])
        for h in range(8):
            P.dma(SP, kv_out[(2 * h) * 128:(2 * h + 1) * 128, t0:t0 + HT], KT[:, h, :], reads=[b_KT], writes=[b_out])
            dst = kv_out[(2 * h + 1) * 128:(2 * h + 2) * 128, :].rearrange("p (i d) -> p i d", d=128)
            P.dma(SP, dst[:, half * (HT // 128):(half + 1) * (HT // 128), :], Vt[:, :, h * 128:(h + 1) * 128],
                  reads=[b_Vt], writes=[b_out])
        P.barrier()

    P.final_wait(SP, [b_out])
    with nc.Block() as block:
        P.emit(block)
    return nc


def build_B(moe, F, stage="all", caps=None):
    nc = bass.Bass("TRN2", target_bir_lowering=False)

    def din(name, shape, dt=F32):
        return nc.dram_tensor(name, list(shape), dt, kind="ExternalInput").ap()
    ln_d = din("ln_bc", [128, 4, D])
    if stage != "b":
        x = din("x", [T, D])
        kv_all = din("kv_all", [8 * KVROWS, T], BF16)
        qn_d = din("qn", [1024, T], BF16)
        qpe_d = din("qpe", [512, T], BF16)
        qs_d = din("qs", [1024, T], BF16)
        ks_d = din("ks_ext", [256, T + 256], BF16)
        vs_d = din("vs_ext", [T + 256, 256], BF16)
        masks_d = din("masks", [128, 3, 384])
        sink_d = din("sink_bc", [128, 8])
        gmla_d = din("g_mla", [128, 8])
        gswa_d = din("g_swa_bc", [128, 1024])
        wout_d = din("w_out", [D, D])
    if stage == "a":
        wr_d = din("w_router", [D, 8])
        lmat_d = din("lmat", [128, 128], BF16)
    if stage == "b":
        iota_d = din("iota_row", [128, MOE_CAPMAX])
        wg_d = din("wg", [8, D, F])
        wu_d = din("wu", [8, D, F])
        wd_d = din("wd", [8, F, D])
        x1s = din("x1s_in", [T, D])
        xb_d = din("xb_in", [T, D], BF16)
        gates_i = din("gates_in", [128, 128])
        rank_i = din("rank_in", [128, 128])
    if stage == "all":
        wg_d = din("wg", [1, D, F])
        wu_d = din("wu", [1, D, F])
        wd_d = din("wd", [1, F, D])
    if stage == "a":
        x1s = nc.dram_tensor("x1s", [T, D], F32, kind="ExternalOutput").ap()
        xb_d = nc.dram_tensor("xb_d", [T, D], BF16, kind="ExternalOutput").ap()
        gates_o = nc.dram_tensor("gates_o", [128, 128], F32, kind="ExternalOutput").ap()
        rank_o = nc.dram_tensor("rank_o", [128, 128], F32, kind="ExternalOutput").ap()
        cnt_o = nc.dram_tensor("cnt_o", [128, 8], F32, kind="ExternalOutput").ap()
    else:
        y_out = nc.dram_tensor("y", [T, D], F32, kind="ExternalOutput").ap()
    if stage == "all":
        x1s = nc.dram_tensor("x1s", [T, D], F32).ap()
        xb_d = nc.dram_tensor("xb_d", [T, D], BF16).ap()
    x1T_d = nc.dram_tensor("x1T_d", [128, 16, T], BF16).ap()
    yacc_d = nc.dram_tensor("yacc_d", [T, D], F32).ap()
    b_xbd = Buf("xb_d")

    P = Prog(nc)
    sb = SBAlloc(nc)
    ps = nc.alloc_psum_tensor("ps", [128, 8, 512], F32)
    psb = [Buf(f"ps{i}") for i in range(8)]
    pctr = [0]
    prange = [0, 8]

    def bank():
        lo, hi = prange
        i = lo + pctr[0] % (hi - lo)
        pctr[0] += 1
        return ps[:, i, :], psb[i]

    b_out = Buf("out")
    b_x1s = Buf("x1s")
    b_x1T = Buf("x1T_d")

    ident = sb.alloc("ident", [128, 128], BF16)
    ones = sb.alloc("ones", [128, 128], BF16)
    sink = sb.alloc("sink", [128, 8], F32)
    gmla = sb.alloc("gmla", [128, 8], F32)
    b_c = Buf("consts")
    P.op(POOL, "memset", dict(ap=ident[:], constant=0.0), writes=[b_c])
    P.op(POOL, "affine_select", dict(out=ident[:], in_=ident[:], pattern=[[-1, 128]],
                                     compare_op=ALU.not_equal, fill=1.0, base=0,
                                     channel_multiplier=1), reads=[b_c], writes=[b_c])
    P.op(POOL, "memset", dict(ap=ones[:], constant=1.0), reads=[b_c], writes=[b_c])
    b_p = Buf("params")
    if stage != "b":
        P.dma(SP, sink[:], sink_d[:, :], writes=[b_p])
        P.dma(SP, gmla[:], gmla_d[:, :], writes=[b_p])
    gates = sb.alloc("gates", [128, 16, 8], F32)
    b_gates = Buf("gates")
    selm = sb.alloc("selm", [128, 128], F32)
    b_selm = Buf("selm")
    if stage == "a":
        wr32 = sb.alloc("wr32", [128, 16, 8], F32)
        wrh = sb.alloc("wrh", [128, 16, 8], BF16)
        wrl = sb.alloc("wrl", [128, 16, 8], BF16)
        b_wr = Buf("wr")
        P.dma(SP, wr32[:], wr_d.rearrange("(k p) e -> p k e", p=128), writes=[b_wr])
        P.op(ACT, "copy", dict(out=wrh[:], in_=wr32[:]), reads=[b_wr], writes=[b_wr])
        P.op(DVE, "tensor_tensor", dict(out=wrl[:], in0=wr32[:], in1=wrh[:], op=ALU.subtract), reads=[b_wr], writes=[b_wr])
    m_const = sb.mark()

    mT = sb.alloc("mT", [128, 16, T], BF16)
    b_mT = [Buf(f"mT{i}") for i in range(16)]
    m_mT = sb.mark()

    def small_pool(prefix, n, width=1):
        ts = [sb.alloc(f"{prefix}{i}", [128, width], F32) for i in range(n)]
        bs = [Buf(f"{prefix}{i}") for i in range(n)]
        return ts, bs

    def layer_norm(src, b_src, gi, dst, b_dst, slot):
        s = [ls[8 * slot + q] for q in range(8)]
        bs = [b_ls[8 * slot + q] for q in range(8)]
        P.op(ACT, "activation", dict(out=junk[:], in_=src, func=AF.Identity, accum_out=s[0][:]),
             reads=[b_src], writes=[b_junk, bs[0]])
        P.op(ACT, "activation", dict(out=junk[:], in_=src, func=AF.Square, accum_out=s[1][:]),
             reads=[b_src], writes=[b_junk, bs[1]])
        P.op(DVE, "tensor_scalar", dict(out=s[2][:], in0=s[0][:], scalar1=1.0 / D, scalar2=None, op0=ALU.mult),
             reads=[bs[0]], writes=[bs[2]])
        P.op(DVE, "tensor_tensor", dict(out=s[3][:], in0=s[2][:], in1=s[2][:], op=ALU.mult),
             reads=[bs[2]], writes=[bs[3]])
        P.op(DVE, "scalar_tensor_tensor", dict(out=s[4][:], in0=s[1][:], scalar=1.0 / D, in1=s[3][:],
                                               op0=ALU.mult, op1=ALU.subtract),
             reads=[bs[1], bs[3]], writes=[bs[4]])
        P.op(DVE, "tensor_scalar", dict(out=s[4][:], in0=s[4][:], scalar1=LN_EPS, scalar2=None, op0=ALU.add),
             reads=[bs[4]], writes=[bs[4]])
        P.op(ACT, "activation", dict(out=s[5][:], in_=s[4][:], func=AF.Sqrt), reads=[bs[4]], writes=[bs[5]])
        P.op(DVE, "reciprocal", dict(out=s[6][:], in_=s[5][:]), reads=[bs[5]], writes=[bs[6]])
        P.op(DVE, "scalar_tensor_tensor", dict(out=s[7][:], in0=s[2][:], scalar=-1.0, in1=s[6][:],
                                               op0=ALU.mult, op1=ALU.mult),
             reads=[bs[2], bs[6]], writes=[bs[7]])
        P.op(ACT, "activation", dict(out=dst, in_=src, func=AF.Identity, scale=s[6][:], bias=s[7][:]),
             reads=[b_src, bs[6], bs[7]], writes=[b_dst])
        P.op(DVE, "tensor_tensor", dict(out=dst, in0=dst, in1=lnp[:, 0, :], op=ALU.mult),
             reads=[b_dst, b_lnp], writes=[b_dst])
        P.op(POOL, "tensor_tensor", dict(out=dst, in0=dst, in1=lnp[:, 1, :], op=ALU.add),
             reads=[b_dst, b_lnp], writes=[b_dst])

    if stage != "b":
        qsT = sb.alloc("qsT", [128, 8, T], BF16)
        ksT = sb.alloc("ksT", [128, 2, T + 256], BF16)
        vsx = sb.alloc("vsx", [128, 18, 256], BF16)
        msk = sb.alloc("msk", [128, 3, 384], F32)
        gswa = sb.alloc("gswa", [128, 1024], F32)
        b_swa_in = Buf("swa_in")
        P.dma(SP, qsT[:], qs_d.rearrange("(h p) t -> p h t", p=128), writes=[b_swa_in])
        P.dma(SP, ksT[:], ks_d.rearrange("(h p) t -> p h t", p=128), writes=[b_swa_in])
        P.dma(SP, vsx[:], vs_d.rearrange("(i p) c -> p i c", p=128), writes=[b_swa_in])
        P.dma(SP, msk[:], masks_d[:, :, :], writes=[b_swa_in])
        P.dma(SP, gswa[:], gswa_d[:, :], writes=[b_swa_in])
        NR = 3
        Sm = [sb.alloc(f"Sm{r}", [128, 384], F32) for r in range(NR)]
        b_Sm = [Buf(f"Sm{r}") for r in range(NR)]
        Pb = [sb.alloc(f"Pb{r}", [128, 384], BF16) for r in range(NR)]
        b_Pb = [Buf(f"Pb{r}") for r in range(NR)]
        PTs = [sb.alloc(f"PTs{r}", [128, 3, 128], BF16) for r in range(NR)]
        b_PTs = [Buf(f"PTs{r}") for r in range(NR)]
        st, b_st = small_pool("st", 8 * NR)
        otile = [sb.alloc(f"otile{r}", [128, 1024], F32) for r in range(2)]
        b_otile = [Buf(f"otile{r}") for r in range(2)]
        obf = [sb.alloc(f"obf{r}", [128, 1024], BF16) for r in range(2)]
        b_obf = [Buf(f"obf{r}") for r in range(2)]
        junk = sb.alloc("junk", [128, 2048], BF16)
        b_junk = Buf("junk")
        st2, b_st2 = small_pool("st2", 4)
        scale_s = 128 ** -0.5
        it = 0
        for i in range(16):
            mi = 0 if i == 0 else (2 if i == 15 else 1)
            ot, b_ot = otile[i % 2], b_otile[i % 2]
            for h in range(8):
                kvh = h // 4
                r = it % NR
                it += 1
                s = [st[8 * r + q] for q in range(8)]
                bs = [b_st[8 * r + q] for q in range(8)]
                pa, pb = bank()
                P.op(PE, "matmul", dict(out=pa[:, :384], lhsT=qsT[:, h, i * 128:(i + 1) * 128],
                                        rhs=ksT[:, kvh, i * 128:i * 128 + 384], start=True, stop=True),
                     reads=[b_swa_in], writes=[pb])
                P.op(DVE, "scalar_tensor_tensor", dict(out=Sm[r][:], in0=pa[:, :384], scalar=scale_s, in1=msk[:, mi, :],
                                                       op0=ALU.mult, op1=ALU.add), reads=[pb, b_swa_in], writes=[b_Sm[r]])
                P.op(DVE, "tensor_reduce", dict(out=s[0][:], in_=Sm[r][:], axis=AX.X, op=ALU.max),
                     reads=[b_Sm[r]], writes=[bs[0]])
                P.op(DVE, "tensor_tensor", dict(out=s[1][:], in0=s[0][:], in1=sink[:, h:h + 1], op=ALU.max),
                     reads=[bs[0], b_p], writes=[bs[1]])
                P.op(DVE, "tensor_scalar", dict(out=s[2][:], in0=s[1][:], scalar1=-1.0, scalar2=None, op0=ALU.mult),
                     reads=[bs[1]], writes=[bs[2]])
                P.op(ACT, "activation", dict(out=Pb[r][:], in_=Sm[r][:], func=AF.Exp, bias=s[2][:], accum_out=s[3][:]),
                     reads=[b_Sm[r], bs[2]], writes=[b_Pb[r], bs[3]])
                P.op(ACT, "activation", dict(out=s[4][:], in_=sink[:, h:h + 1], func=AF.Exp, bias=s[2][:]),
                     reads=[bs[2], b_p], writes=[bs[4]])
                P.op(DVE, "tensor_tensor", dict(out=s[5][:], in0=s[3][:], in1=s[4][:], op=ALU.add),
                     reads=[bs[3], bs[4]], writes=[bs[5]])
                P.op(DVE, "reciprocal", dict(out=s[6][:], in_=s[5][:]), reads=[bs[5]], writes=[bs[6]])
                pa2, pb2 = bank()
                pT = pa2.bitcast(BF16)[:, :384].rearrange("p (a b) -> p a b", b=128)
                for j in range(3):
                    P.op(PE, "transpose", dict(out=pT[:, j, :], in_=Pb[r][:, j * 128:(j + 1) * 128], identity=ident[:]),
                         reads=[b_Pb[r], b_c], writes=[pb2], sig=(j == 2))
                P.op(ACT, "copy", dict(out=PTs[r][:], in_=pT), reads=[pb2], writes=[b_PTs[r]])
                pa3, pb3 = bank()
                mm_acc(P, pa3[:, :128], pb3,
                       [(PTs[r][:, j, :], vsx[:, i + j, kvh * 128:(kvh + 1) * 128]) for j in range(3)],
                       [b_PTs[r], b_swa_in])
                P.op(DVE, "tensor_scalar", dict(out=ot[:, h * 128:(h + 1) * 128], in0=pa3[:, :128], scalar1=s[6][:],
                                                scalar2=None, op0=ALU.mult), reads=[pb3, bs[6]], writes=[b_ot])
            q4 = [st2[q] for q in range(4)]
            bq = [b_st2[q] for q in range(4)]
            P.op(ACT, "activation", dict(out=junk[:, :1024], in_=ot[:], func=AF.Square, accum_out=q4[0][:]),
                 reads=[b_ot], writes=[b_junk, bq[0]])
            P.op(DVE, "tensor_scalar", dict(out=q4[1][:], in0=q4[0][:], scalar1=1.0 / 1024, scalar2=RMS_EPS,
                                            op0=ALU.mult, op1=ALU.add), reads=[bq[0]], writes=[bq[1]])
            P.op(ACT, "activation", dict(out=q4[2][:], in_=q4[1][:], func=AF.Sqrt), reads=[bq[1]], writes=[bq[2]])
            P.op(DVE, "reciprocal", dict(out=q4[3][:], in_=q4[2][:]), reads=[bq[2]], writes=[bq[3]])
            ob, b_ob = obf[i % 2], b_obf[i % 2]
            P.op(DVE, "scalar_tensor_tensor", dict(out=ob[:], in0=ot[:], scalar=q4[3][:], in1=gswa[:],
                                                   op0=ALU.mult, op1=ALU.mult), reads=[b_ot, bq[3], b_swa_in], writes=[b_ob])
            pa4, pb4 = bank()
            pT = pa4.bitcast(BF16).rearrange("p (a b) -> p a b", b=128)
            for c in range(8):
                P.op(PE, "transpose", dict(out=pT[:, c, :], in_=ob[:, c * 128:(c + 1) * 128], identity=ident[:]),
                     reads=[b_ob, b_c], writes=[pb4], sig=(c == 7))
            P.op(ACT, "copy", dict(out=mT[:, 8:16, i * 128:(i + 1) * 128], in_=pT), reads=[pb4], writes=[b_mT[i]])

        P.barrier()
        sb.reset(m_mT)
        QP = 1024
        qn = sb.alloc("qn", [128, 8, QP], BF16)
        qpe = sb.alloc("qpe", [64, 8, QP], BF16)
        b_q = Buf("q")
        NK = 3
        Kc = [sb.alloc(f"Kc{r}", [128, T], BF16) for r in range(NK)]
        Vc = [sb.alloc(f"Vc{r}", [128, 16, 128], BF16) for r in range(NK)]
        Pc = [sb.alloc(f"Pc{r}", [64, T], BF16) for r in range(NK)]
        b_kv = [Buf(f"kvc{r}") for r in range(NK)]
        NP = 4
        PT = [sb.alloc(f"PT{r}", [128, 512], BF16) for r in range(NP)]
        b_PT = [Buf(f"PT{r}") for r in range(NP)]
        oT = sb.alloc("oT", [128, 8, QP], F32)
        b_oT = Buf("oT")
        rs = [sb.alloc(f"rs{r}", [128, 512], F32) for r in range(2)]
        b_rs = [Buf(f"rs{r}") for r in range(2)]
        sq = sb.alloc("sqm", [128, 8, 512], BF16)
        b_sq = Buf("sqm")
        rstd = sb.alloc("rstdm", [128, 512], F32)
        b_rstd = Buf("rstdm")
        scale_m = 192 ** -0.5
        prange[0], prange[1] = 4, 8
        pctr[0] = 0
        cctr = 0
        pctr_pt = 0
        for qp in range(2):
            q0 = qp * QP
            P.dma(SP, qn[:], qn_d.rearrange("(h p) t -> p h t", p=128)[:, :, q0:q0 + QP], writes=[b_q])
            P.dma(SP, qpe[:], qpe_d.rearrange("(h p) t -> p h t", p=64)[:, :, q0:q0 + QP], writes=[b_q])
            for h in range(8):
                for rk in range(8):
                    r = cctr % NK
                    cctr += 1
                    base = rk * KVROWS
                    P.dma(SP, Kc[r][:], kv_all[base + 2 * h * 128:base + (2 * h + 1) * 128, :], writes=[b_kv[r]])
                    P.dma(SP, Vc[r][:], kv_all[base + (2 * h + 1) * 128:base + (2 * h + 2) * 128, :]
                          .rearrange("p (i d) -> p i d", d=128), writes=[b_kv[r]])
                    P.dma(SP, Pc[r][:], kv_all[base + 2048:base + 2048 + 64, :], writes=[b_kv[r]])
                    for kt in range(16):
                        first = (rk == 0 and kt == 0)
                        last = (rk == 7 and kt == 15)
                        for qb in range(2):
                            qsl = slice(qb * 512, (qb + 1) * 512)
                            pa, pb = bank()
                            P.op(PE, "matmul", dict(out=pa, lhsT=Kc[r][:, kt * 128:(kt + 1) * 128], rhs=qn[:, h, qsl],
                                                    start=True, stop=False), reads=[b_kv[r], b_q], writes=[pb], sig=False)
                            P.op(PE, "matmul", dict(out=pa, lhsT=Pc[r][:, kt * 128:(kt + 1) * 128], rhs=qpe[:, h, qsl],
                                                    start=False, stop=True), reads=[b_kv[r], b_q], writes=[pb])
                            pr = pctr_pt % NP
                            pctr_pt += 1
                            P.op(ACT, "activation", dict(out=PT[pr][:], in_=pa, func=AF.Exp, scale=scale_m),
                                 reads=[pb], writes=[b_PT[pr]])
                            P.op(PE, "matmul", dict(out=ps[:, qb, :], lhsT=Vc[r][:, kt, :], rhs=PT[pr][:],
                                                    start=first, stop=last), reads=[b_PT[pr], b_kv[r]],
                                 writes=[psb[qb]], sig=False)
                            P.op(PE, "matmul", dict(out=ps[:, 2 + qb, :], lhsT=ones[:], rhs=PT[pr][:],
                                                    start=first, stop=last), reads=[b_PT[pr], b_kv[r], b_c],
                                 writes=[psb[qb], psb[2 + qb]])
                for qb in range(2):
                    qsl = slice(qb * 512, (qb + 1) * 512)
                    P.op(DVE, "reciprocal", dict(out=rs[qb][:], in_=ps[:, 2 + qb, :]), reads=[psb[2 + qb]], writes=[b_rs[qb]])
                    P.op(DVE, "tensor_tensor", dict(out=oT[:, h, qsl], in0=ps[:, qb, :], in1=rs[qb][:], op=ALU.mult),
                         reads=[psb[qb], psb[2 + qb], b_rs[qb]], writes=[b_oT])
            for qb in range(2):
                qsl = slice(qb * 512, (qb + 1) * 512)
                for h in range(8):
                    P.op(ACT, "activation", dict(out=sq[:, h, :], in_=oT[:, h, qsl], func=AF.Square),
                         reads=[b_oT], writes=[b_sq])
                sa, sbf = bank()
                mm_acc(P, sa, sbf, [(ones[:], sq[:, h, :]) for h in range(8)], [b_sq, b_c])
                P.op(DVE, "tensor_scalar", dict(out=rstd[:], in0=sa, scalar1=1.0 / 1024, scalar2=RMS_EPS,
                                                op0=ALU.mult, op1=ALU.add), reads=[sbf], writes=[b_rstd])
                P.op(ACT, "activation", dict(out=rstd[:], in_=rstd[:], func=AF.Sqrt), reads=[b_rstd], writes=[b_rstd])
                P.op(DVE, "reciprocal", dict(out=rstd[:], in_=rstd[:]), reads=[b_rstd], writes=[b_rstd])
                tiles = [b_mT[(q0 + qb * 512) // 128 + q] for q in range(4)]
                for h in range(8):
                    P.op(DVE, "scalar_tensor_tensor", dict(
                        out=mT[:, h, q0 + qb * 512:q0 + (qb + 1) * 512], in0=oT[:, h, qsl], scalar=gmla[:, h:h + 1],
                        in1=rstd[:], op0=ALU.mult, op1=ALU.mult), reads=[b_oT, b_rstd, b_p], writes=tiles)
        prange[0], prange[1] = 0, 8

        P.barrier()
        sb.reset(m_mT)
        wout = sb.alloc("wout", [128, 16, D], BF16)
        b_wout = Buf("wout")
        wv = wout_d.rearrange("(k p) c -> p k c", p=128)
        for cb in range(4):
            P.dma(POOL, wout[:, :, cb * 512:(cb + 1) * 512], wv[:, :, cb * 512:(cb + 1) * 512], writes=[b_wout])
        lnp = sb.alloc("lnpF", [128, 2, D], F32)
        b_lnp = Buf("lnpF")
        P.dma(SP, lnp[:], ln_d[:, 0:2, :], writes=[b_lnp])
        xt = [sb.alloc("xt0", [128, D], F32)] * 2
        b_xt = [Buf("xt0")] * 2
        yp = [sb.alloc(f"yp{r}", [128, D], F32) for r in range(2)]
        b_yp = [Buf(f"yp{r}") for r in range(2)]
        xo = yp
        b_xo = b_yp
        xb = [sb.alloc(f"xb{r}", [128, D], BF16) for r in range(2)]
        b_xb = [Buf(f"xb{r}") for r in range(2)]
        xTs = [sb.alloc(f"xTs{r}", [128, 16, 128], BF16) for r in range(2)]
        b_xTs = [Buf(f"xTs{r}") for r in range(2)]
        if moe:
            xlb = [sb.alloc("xlb0", [128, D], BF16)] * 2
            b_xlb = [Buf("xlb0")] * 2
            xlTs = [sb.alloc("xlTs0", [128, 16, 128], BF16)] * 2
            b_xlTs = [Buf("xlTs0")] * 2
            gs, b_gs = small_pool("gs", 12, 8)
        junk = sb.alloc("junkF", [128, D], BF16)
        b_junk = Buf("junkF")
        ls, b_ls = small_pool("ls", 16)

        for i in range(16):
            r = i % 2
            P.dma(SP, xt[r][:], x[i * 128:(i + 1) * 128, :], writes=[b_xt[r]])
            for cb in range(4):
                pa, pb = bank()
                mm_acc(P, pa, pb, [(mT[:, k, i * 128:(i + 1) * 128], wout[:, k, cb * 512:(cb + 1) * 512])
                                   for k in range(16)], [b_mT[i], b_wout])
                P.op(DVE, "scalar_tensor_tensor", dict(out=yp[r][:, cb * 512:(cb + 1) * 512],
                                                       in0=xt[r][:, cb * 512:(cb + 1) * 512], scalar=ALPHA, in1=pa,
                                                       op0=ALU.mult, op1=ALU.add),
                     reads=[pb, b_xt[r]], writes=[b_yp[r]])
            layer_norm(yp[r][:], b_yp[r], 0, xo[r][:], b_xo[r], r)
            P.dma(SP, x1s[i * 128:(i + 1) * 128, :], xo[r][:], reads=[b_xo[r]], writes=[b_x1s])
            P.op(ACT, "copy", dict(out=xb[r][:], in_=xo[r][:]), reads=[b_xo[r]], writes=[b_xb[r]])
            j = (pctr[0] // 2) % 4
            pctr[0] += 2
            pT = ps[:, 2 * j:2 * j + 2, :].bitcast(BF16).rearrange("p a (b c) -> p (a b) c", c=128)
            pbufs = [psb[2 * j], psb[2 * j + 1]]
            for k in range(16):
                P.op(PE, "transpose", dict(out=pT[:, k, :], in_=xb[r][:, k * 128:(k + 1) * 128], identity=ident[:]),
                     reads=[b_xb[r], b_c], writes=pbufs, sig=(k == 15))
            P.op(DVE, "tensor_copy", dict(out=xTs[r][:], in_=pT), reads=pbufs, writes=[b_xTs[r]])
            P.dma(SP, x1T_d[:, :, i * 128:(i + 1) * 128], xTs[r][:], reads=[b_xTs[r]], writes=[b_x1T])
            if moe:
                P.op(DVE, "tensor_tensor", dict(out=xlb[r][:], in0=xo[r][:], in1=xb[r][:], op=ALU.subtract),
                     reads=[b_xo[r], b_xb[r]], writes=[b_xlb[r]])
                j = (pctr[0] // 2) % 4
                pctr[0] += 2
                pT2 = ps[:, 2 * j:2 * j + 2, :].bitcast(BF16).rearrange("p a (b c) -> p (a b) c", c=128)
                pbufs2 = [psb[2 * j], psb[2 * j + 1]]
                for k in range(16):
                    P.op(PE, "transpose", dict(out=pT2[:, k, :], in_=xlb[r][:, k * 128:(k + 1) * 128], identity=ident[:]),
                         reads=[b_xlb[r], b_c], writes=pbufs2, sig=(k == 15))
                P.op(ACT, "copy", dict(out=xlTs[r][:], in_=pT2), reads=pbufs2, writes=[b_xlTs[r]])
                pl, pbl = bank()
                pairs = []
                for k in range(16):
                    pairs += [(xTs[r][:, k, :], wrh[:, k, :]), (xTs[r][:, k, :], wrl[:, k, :]), (xlTs[r][:, k, :], wrh[:, k, :])]
                mm_acc(P, pl[:, :8], pbl, pairs, [b_xTs[r], b_xlTs[r], b_wr])
                g = [gs[q] for q in range(12)]
                bg = [b_gs[q] for q in range(12)]
                P.op(DVE, "tensor_copy", dict(out=g[0][:], in_=pl[:, :8]), reads=[pbl], writes=[bg[0]])
                P.op(DVE, "tensor_reduce", dict(out=g[1][:, 0:1], in_=g[0][:], axis=AX.X, op=ALU.max), reads=[bg[0]], writes=[bg[1]])
                P.op(DVE, "tensor_scalar", dict(out=g[2][:], in0=g[0][:], scalar1=g[1][:, 0:1], scalar2=None, op0=ALU.is_equal),
                     reads=[bg[0], bg[1]], writes=[bg[2]])
                P.op(DVE, "scalar_tensor_tensor", dict(out=g[3][:], in0=g[2][:], scalar=NEG, in1=g[0][:], op0=ALU.mult, op1=ALU.add),
                     reads=[bg[2], bg[0]], writes=[bg[3]])
                P.op(DVE, "tensor_reduce", dict(out=g[4][:, 0:1], in_=g[3][:], axis=AX.X, op=ALU.max), reads=[bg[3]], writes=[bg[4]])
                P.op(DVE, "tensor_scalar", dict(out=g[5][:], in0=g[3][:], scalar1=g[4][:, 0:1], scalar2=None, op0=ALU.is_equal),
                     reads=[bg[3], bg[4]], writes=[bg[5]])
                P.op(DVE, "tensor_tensor", dict(out=g[6][:, 0:1], in0=g[4][:, 0:1], in1=g[1][:, 0:1], op=ALU.subtract),
                     reads=[bg[4], bg[1]], writes=[bg[6]])
                P.op(ACT, "activation", dict(out=g[7][:, 0:1], in_=g[6][:, 0:1], func=AF.Exp), reads=[bg[6]], writes=[bg[7]])
                P.op(DVE, "tensor_scalar", dict(out=g[8][:, 0:1], in0=g[7][:, 0:1], scalar1=1.0, scalar2=None, op0=ALU.add),
                     reads=[bg[7]], writes=[bg[8]])
                P.op(DVE, "reciprocal", dict(out=g[9][:, 0:1], in_=g[8][:, 0:1]), reads=[bg[8]], writes=[bg[9]])
                P.op(DVE, "tensor_tensor", dict(out=g[10][:, 0:1], in0=g[7][:, 0:1], in1=g[9][:, 0:1], op=ALU.mult),
                     reads=[bg[7], bg[9]], writes=[bg[10]])
                P.op(DVE, "tensor_scalar", dict(out=g[11][:], in0=g[5][:], scalar1=g[10][:, 0:1], scalar2=None, op0=ALU.mult),
                     reads=[bg[5], bg[10]], writes=[bg[11]])
                P.op(DVE, "scalar_tensor_tensor", dict(out=gates[:, i, :], in0=g[2][:], scalar=g[9][:, 0:1], in1=g[11][:],
                                                       op0=ALU.mult, op1=ALU.add),
                     reads=[bg[2], bg[9], bg[11]], writes=[b_gates])
                P.op(DVE, "tensor_tensor", dict(out=selm[:, i * 8:(i + 1) * 8], in0=g[2][:], in1=g[5][:], op=ALU.add),
                     reads=[bg[2], bg[5]], writes=[b_selm])
                P.dma(SP, xb_d[i * 128:(i + 1) * 128, :], xb[r][:], reads=[b_xb[r]], writes=[b_xbd])

    if stage == "a":
        P.barrier()
        sb.reset(m_const)
        Lm = sb.alloc("Lm", [128, 128], BF16)
        b_cm = Buf("moe_consts")
        P.dma(SP, Lm[:], lmat_d[:, :], writes=[b_cm])
        selb = sb.alloc("selb", [128, 128], BF16)
        b_selb = Buf("selb")
        rk = [sb.alloc(f"rk{q}", [128, 128], F32) for q in range(4)]
        b_rk = [Buf(f"rk{q}") for q in range(4)]
        cnt = sb.alloc("cnt", [128, 8], F32)
        b_cnt = Buf("cnt")
        P.op(ACT, "copy", dict(out=selb[:], in_=selm[:]), reads=[b_selm], writes=[b_selb])
        pw, pbw = bank()
        P.op(PE, "matmul", dict(out=pw[:, :128], lhsT=Lm[:], rhs=selb[:], start=True, stop=True),
             reads=[b_selb, b_cm], writes=[pbw])
        pt_, pbt = bank()
        P.op(PE, "matmul", dict(out=pt_[:, :128], lhsT=ones[:], rhs=selb[:], start=True, stop=True),
             reads=[b_selb, b_c], writes=[pbt])
        P.op(DVE, "tensor_copy", dict(out=rk[1][:], in_=pt_[:, :128]), reads=[pbt], writes=[b_rk[1]])
        P.op(DVE, "memset", dict(ap=rk[2][:, 0:8], constant=0.0), writes=[b_rk[2]])
        for i in range(1, 16):
            P.op(DVE, "tensor_tensor", dict(out=rk[2][:, i * 8:(i + 1) * 8], in0=rk[2][:, (i - 1) * 8:i * 8],
                                            in1=rk[1][:, (i - 1) * 8:i * 8], op=ALU.add),
                 reads=[b_rk[1], b_rk[2]], writes=[b_rk[2]])
        P.op(DVE, "tensor_tensor", dict(out=cnt[:], in0=rk[2][:, 120:128], in1=rk[1][:, 120:128], op=ALU.add),
             reads=[b_rk[1], b_rk[2]], writes=[b_cnt])
        P.op(DVE, "tensor_tensor", dict(out=rk[0][:], in0=pw[:, :128], in1=rk[2][:], op=ALU.add),
             reads=[pbw, b_rk[2]], writes=[b_rk[0]])
        P.op(DVE, "scalar_tensor_tensor", dict(out=rk[3][:], in0=rk[0][:], scalar=1.0, in1=selm[:],
                                               op0=ALU.add, op1=ALU.mult), reads=[b_rk[0], b_selm], writes=[b_rk[3]])
        P.op(DVE, "tensor_scalar", dict(out=rk[3][:], in0=rk[3][:], scalar1=-1.0, scalar2=None, op0=ALU.add),
             reads=[b_rk[3]], writes=[b_rk[3]])
        P.dma(SP, gates_o[:, :], gates[:].rearrange("p a b -> p (a b)"), reads=[b_gates], writes=[b_out])
        P.dma(SP, rank_o[:, :], rk[3][:], reads=[b_rk[3]], writes=[b_out])
        P.dma(SP, cnt_o[:, :], cnt[:], reads=[b_cnt], writes=[b_out])
        P.final_wait(SP, [b_out, b_x1s, b_xbd])
        with nc.Block() as block:
            P.emit(block)
        return nc

    P.barrier()
    sb.reset(m_const)
    if stage == "all":
        HT = 1024
        x1T = sb.alloc("x1T", [128, 16, HT], BF16)
        b_x1Th = Buf("x1Th")
        yacc = sb.alloc("yacc", [128, HT // 128, D], F32)
        b_yacc = [Buf(f"yacc{t}") for t in range(HT // 128)]
        GF = 256
        NW = 2
        wgb = [sb.alloc(f"wgb{r}", [128, 16, GF], BF16) for r in range(NW)]
        wub = [sb.alloc(f"wub{r}", [128, 16, GF], BF16) for r in range(NW)]
        wdb = [sb.alloc(f"wdb{r}", [128, GF // 128, D], BF16) for r in range(NW)]
        b_w = [Buf(f"wffn{r}") for r in range(NW)]
        sg = [sb.alloc(f"sg{r}", [128, 512], F32) for r in range(2)]
        b_sg = [Buf(f"sg{r}") for r in range(2)]
        hT = [sb.alloc(f"hT{r}", [128, GF // 128, 512], BF16) for r in range(2)]
        b_hT = [Buf(f"hT{r}") for r in range(2)]
        xt = [sb.alloc(f"xtG{r}", [128, D], F32) for r in range(2)]
        b_xt = [Buf(f"xtG{r}") for r in range(2)]
        lnp = sb.alloc("lnpG", [128, 2, D], F32)
        b_lnp = Buf("lnpG")
        P.dma(SP, lnp[:], ln_d[:, 2:4, :], writes=[b_lnp])
        junk = sb.alloc("junkG", [128, D], BF16)
        b_junk = Buf("junkG")
        ls, b_ls = small_pool("lsG", 16)
        n_exp = 8 if moe else 1
        ngrp = F // GF
        wctr = 0
        sgc = 0
        hc = 0
        for half in range(2):
            t0 = half * HT
            P.dma(SP, x1T[:], x1T_d[:, :, t0:t0 + HT], reads=[b_x1T], writes=[b_x1Th])
            for e in range(n_exp):
                for g in range(ngrp):
                    r = wctr % NW
                    wctr += 1
                    f0 = g * GF
                    P.dma(POOL, wgb[r][:], wg_d[e].rearrange("(k p) f -> p k f", p=128)[:, :, f0:f0 + GF], writes=[b_w[r]])
                    P.dma(POOL, wub[r][:], wu_d[e].rearrange("(k p) f -> p k f", p=128)[:, :, f0:f0 + GF], writes=[b_w[r]])
                    P.dma(POOL, wdb[r][:], wd_d[e, f0:f0 + GF, :].rearrange("(c p) d -> p c d", p=128), writes=[b_w[r]])
                    for tb in range(HT // 512):
                        tsl = slice(tb * 512, (tb + 1) * 512)
                        hr = hc % 2
                        hc += 1
                        for c in range(GF // 128):
                            pg, pbg = bank()
                            mm_acc(P, pg, pbg, [(wgb[r][:, k, c * 128:(c + 1) * 128], x1T[:, k, tsl]) for k in range(16)],
                                   [b_w[r], b_x1Th])
                            pu, pbu = bank()
                            mm_acc(P, pu, pbu, [(wub[r][:, k, c * 128:(c + 1) * 128], x1T[:, k, tsl]) for k in range(16)],
                                   [b_w[r], b_x1Th])
                            sr = sgc % 2
                            sgc += 1
                            P.op(ACT, "activation", dict(out=sg[sr][:], in_=pg, func=AF.Silu), reads=[pbg], writes=[b_sg[sr]])
                            P.op(DVE, "tensor_tensor", dict(out=hT[hr][:, c, :], in0=pu, in1=sg[sr][:], op=ALU.mult),
                                 reads=[pbu, b_sg[sr]], writes=[b_hT[hr]])
                        for t in range(4):
                            tt = tb * 4 + t
                            for cb in range(4):
                                pa, pb = bank()
                                mm_acc(P, pa, pb, [(hT[hr][:, c, t * 128:(t + 1) * 128], wdb[r][:, c, cb * 512:(cb + 1) * 512])
                                                   for c in range(GF // 128)], [b_hT[hr], b_w[r]])
                                dst = yacc[:, tt, cb * 512:(cb + 1) * 512]
                                gsc = gates[:, half * (HT // 128) + tt, e:e + 1]
                                if e == 0 and g == 0:
                                    if moe:
                                        P.op(DVE, "tensor_scalar", dict(out=dst, in0=pa, scalar1=gsc, scalar2=None, op0=ALU.mult),
                                             reads=[pb, b_gates], writes=[b_yacc[tt]])
                                    else:
                                        P.op(DVE, "tensor_copy", dict(out=dst, in_=pa), reads=[pb], writes=[b_yacc[tt]])
                                elif moe:
                                    P.op(DVE, "scalar_tensor_tensor", dict(out=dst, in0=pa, scalar=gsc, in1=dst,
                                                                           op0=ALU.mult, op1=ALU.add),
                                         reads=[pb, b_yacc[tt], b_gates], writes=[b_yacc[tt]])
                                else:
                                    P.op(DVE, "tensor_tensor", dict(out=dst, in0=pa, in1=dst, op=ALU.add),
                                         reads=[pb, b_yacc[tt]], writes=[b_yacc[tt]])
            for tt in range(HT // 128):
                r = tt % 2
                row0 = t0 + tt * 128
                P.dma(SP, xt[r][:], x1s[row0:row0 + 128, :], reads=[b_x1s], writes=[b_xt[r]])
                P.op(DVE, "scalar_tensor_tensor", dict(out=yacc[:, tt, :], in0=xt[r][:], scalar=ALPHA, in1=yacc[:, tt, :],
                                                       op0=ALU.mult, op1=ALU.add),
                     reads=[b_xt[r], b_yacc[tt]], writes=[b_yacc[tt]])
                layer_norm(yacc[:, tt, :], b_yacc[tt], 2, xt[r][:], b_xt[r], r)
                P.dma(SP, y_out[row0:row0 + 128, :], xt[r][:], reads=[b_xt[r]], writes=[b_out])

    if stage == "b":
        CAPMAX = MOE_CAPMAX
        GF = 256
        NW = 2
        ngrp = F // GF
        ls, b_ls = small_pool("lsG", 16)
        iot = sb.alloc("iot", [128, CAPMAX], F32)
        iot2 = sb.alloc("iot2", [128, CAPMAX], F32)
        b_cm = Buf("moe_consts")
        b_iot2 = Buf("iot2")
        P.dma(SP, iot[:], iota_d[:, :], writes=[b_cm])
        rankm = sb.alloc("rankm", [128, 128], F32)
        b_rankm = Buf("rankm")
        P.dma(SP, rankm[:], rank_i[:, :], writes=[b_rankm])
        P.dma(SP, gates[:].rearrange("p a b -> p (a b)"), gates_i[:, :], writes=[b_gates])
        NSL = 3
        Sel = [sb.alloc(f"Sel{r}", [128, CAPMAX], BF16) for r in range(NSL)]
        b_Sel = [Buf(f"Sel{r}") for r in range(NSL)]
        xeT = sb.alloc("xeT", [128, 16 * CAPMAX], BF16)
        b_xeT = Buf("xeT")
        yeacc = sb.alloc("yeacc", [128, CAPMAX // 128, D], F32)
        b_ye = [Buf(f"ye{q}") for q in range(CAPMAX // 128)]
        m_w = sb.mark()
        lnp = sb.alloc("lnpG", [128, 2, D], F32)
        junk = sb.alloc("junkG", [128, D], BF16)
        b_lnp = Buf("lnpG")
        b_junk = Buf("junkG")
        sb.reset(m_w)
        wgb = [sb.alloc(f"wgb{r}", [128, 16, GF], BF16) for r in range(NW)]
        wub = [sb.alloc(f"wub{r}", [128, 16, GF], BF16) for r in range(NW)]
        wdb = [sb.alloc(f"wdb{r}", [128, GF // 128, D], BF16) for r in range(NW)]
        b_w = [Buf(f"wffn{r}") for r in range(NW)]
        NX = 2
        xtok = [sb.alloc(f"xtok{r}", [128, D], BF16) for r in range(NX)]
        b_xtok = [Buf(f"xtok{r}") for r in range(NX)]
        sg = [sb.alloc(f"sg{r}", [128, 512], F32) for r in range(2)]
        b_sg = [Buf(f"sg{r}") for r in range(2)]
        hT = [sb.alloc(f"hT{r}", [128, GF // 128, CAPMAX], BF16) for r in range(2)]
        b_hT = [Buf(f"hT{r}") for r in range(2)]
        selT = [sb.alloc(f"selT{r}", [128, CAPMAX // 128, 128], BF16) for r in range(2)]
        b_selT = [Buf(f"selT{r}") for r in range(2)]
        ybuf = [sb.alloc(f"ybuf{r}", [128, D], F32) for r in range(2)]
        b_ybuf = [Buf(f"ybuf{r}") for r in range(2)]
        b_yd = [Buf(f"yacc_d{i}") for i in range(16)]
        ls_x = sb.alloc("lsx0", [128, D], F32)
        b_lsx = Buf("lsx0")
        segs = []
        for e in range(8):
            c0 = 0
            while c0 < caps[e]:
                segs.append((e, c0, min(CAPMAX, caps[e] - c0)))
                c0 += CAPMAX
        assert segs
        xctr = 0
        wctr = 0
        sgc = 0
        hc = 0
        yc = 0
        selctr = [0]
        for si, (e, base, cp) in enumerate(segs):
            first_seg = (si == 0)
            last_seg = (si == len(segs) - 1)
            NS = cp // 128
            cblk = [(0, min(cp, 512))] + ([(512, cp)] if cp > 512 else [])
            xe = xeT[:, 0:16 * cp].rearrange("p (k c) -> p k c", c=cp)
            ye_bf = xeT[:, 0:NS * D].rearrange("p (s d) -> p s d", d=D)
            P.op(DVE, "tensor_scalar", dict(out=iot2[:, :cp], in0=iot[:, :cp], scalar1=float(base), scalar2=None, op0=ALU.add),
                 reads=[b_cm], writes=[b_iot2])

            def build_sel(i):
                selctr[0] += 1
                r_ = selctr[0] % NSL
                P.op(DVE, "tensor_scalar", dict(out=Sel[r_][:, :cp], in0=iot2[:, :cp],
                                                scalar1=rankm[:, i * 8 + e:i * 8 + e + 1], scalar2=None, op0=ALU.is_equal),
                     reads=[b_iot2, b_rankm], writes=[b_Sel[r_]])
                return Sel[r_], b_Sel[r_]
            nb = len(cblk)
            kgsz = 8 // nb
            for kg in range(16 // kgsz):
                for i in range(16):
                    xr = xctr % NX
                    xctr += 1
                    P.dma(SP, xtok[xr][:], xb_d[i * 128:(i + 1) * 128, :], reads=[b_xbd], writes=[b_xtok[xr]])
                    S_, bS_ = build_sel(i)
                    for kk in range(kgsz):
                        k = kgsz * kg + kk
                        for bi_, (a0, a1) in enumerate(cblk):
                            bi = nb * kk + bi_
                            lastmm = (kk == kgsz - 1 and bi_ == nb - 1)
                            P.op(PE, "matmul", dict(out=ps[:, bi, :a1 - a0], lhsT=xtok[xr][:, k * 128:(k + 1) * 128],
                                                    rhs=S_[:, a0:a1], start=(i == 0), stop=(i == 15)),
                                 reads=[b_xtok[xr], bS_], writes=[psb[bi]], sig=(i == 15 or lastmm))
                for kk in range(kgsz):
                    k = kgsz * kg + kk
                    for bi_, (a0, a1) in enumerate(cblk):
                        bi = nb * kk + bi_
                        if bi % 2 == 0:
                            P.op(ACT, "copy", dict(out=xe[:, k, a0:a1], in_=ps[:, bi, :a1 - a0]),
                                 reads=[psb[bi]], writes=[b_xeT])
                        else:
                            P.op(DVE, "tensor_copy", dict(out=xe[:, k, a0:a1], in_=ps[:, bi, :a1 - a0]),
                                 reads=[psb[bi]], writes=[b_xeT])
            for g in range(ngrp):
                r = wctr % NW
                wctr += 1
                f0 = g * GF
                P.dma(POOL, wgb[r][:], wg_d[e].rearrange("(k p) f -> p k f", p=128)[:, :, f0:f0 + GF], writes=[b_w[r]])
                P.dma(POOL, wub[r][:], wu_d[e].rearrange("(k p) f -> p k f", p=128)[:, :, f0:f0 + GF], writes=[b_w[r]])
                P.dma(POOL, wdb[r][:], wd_d[e, f0:f0 + GF, :].rearrange("(c p) d -> p c d", p=128), writes=[b_w[r]])
                hr = hc % 2
                hc += 1
                for c in range(GF // 128):
                    for (a0, a1) in cblk:
                        w_ = a1 - a0
                        pg, pbg = bank()
                        mm_acc(P, pg[:, :w_], pbg, [(wgb[r][:, k, c * 128:(c + 1) * 128], xe[:, k, a0:a1]) for k in range(16)],
                               [b_w[r], b_xeT])
                        pu, pbu = bank()
                        mm_acc(P, pu[:, :w_], pbu, [(wub[r][:, k, c * 128:(c + 1) * 128], xe[:, k, a0:a1]) for k in range(16)],
                               [b_w[r], b_xeT])
                        sr = sgc % 2
                        sgc += 1
                        P.op(ACT, "activation", dict(out=sg[sr][:, :w_], in_=pg[:, :w_], func=AF.Silu), reads=[pbg], writes=[b_sg[sr]])
                        P.op(DVE, "tensor_tensor", dict(out=hT[hr][:, c, a0:a1], in0=pu[:, :w_], in1=sg[sr][:, :w_], op=ALU.mult),
                             reads=[pbu, b_sg[sr]], writes=[b_hT[hr]])
                for s_ in range(NS):
                    for cb in range(4):
                        pa, pb = bank()
                        mm_acc(P, pa, pb, [(hT[hr][:, c, s_ * 128:(s_ + 1) * 128], wdb[r][:, c, cb * 512:(cb + 1) * 512])
                                           for c in range(GF // 128)], [b_hT[hr], b_w[r]])
                        dst = yeacc[:, s_, cb * 512:(cb + 1) * 512]
                        if g == 0:
                            P.op(DVE, "tensor_copy", dict(out=dst, in_=pa), reads=[pb], writes=[b_ye[s_]])
                        else:
                            P.op(DVE, "tensor_tensor", dict(out=dst, in0=pa, in1=dst, op=ALU.add),
                                 reads=[pb, b_ye[s_]], writes=[b_ye[s_]])
            for s_ in range(NS):
                P.op(ACT, "copy", dict(out=ye_bf[:, s_, :], in_=yeacc[:, s_, :]), reads=[b_ye[s_]], writes=[b_xeT])
            if last_seg:
                P.dma(SP, lnp[:], ln_d[:, 2:4, :], writes=[b_lnp, b_w[0], b_w[1], b_junk])
            for i in range(16):
                sr = i % 2
                S_, bS_ = build_sel(i)
                pq, pbq = bank()
                pT = pq.bitcast(BF16)[:, :NS * 128].rearrange("p (a b) -> p a b", b=128)
                for c in range(NS):
                    P.op(PE, "transpose", dict(out=pT[:, c, :], in_=S_[:, c * 128:(c + 1) * 128], identity=ident[:]),
                         reads=[bS_, b_c], writes=[pbq], sig=(c == NS - 1))
                P.op(ACT, "copy", dict(out=selT[sr][:, :NS, :], in_=pT), reads=[pbq], writes=[b_selT[sr]])
                yr = yc % 2
                yc += 1
                if not first_seg:
                    P.dma(SP, ybuf[yr][:], yacc_d[i * 128:(i + 1) * 128, :], reads=[b_yd[i]], writes=[b_ybuf[yr]])
                for cb in range(4):
                    pa, pb = bank()
                    mm_acc(P, pa, pb, [(selT[sr][:, c, :], ye_bf[:, c, cb * 512:(cb + 1) * 512]) for c in range(NS)],
                           [b_selT[sr], b_xeT])
                    dst = ybuf[yr][:, cb * 512:(cb + 1) * 512]
                    gsc = gates[:, i, e:e + 1]
                    if first_seg:
                        P.op(DVE, "tensor_scalar", dict(out=dst, in0=pa, scalar1=gsc, scalar2=None, op0=ALU.mult),
                             reads=[pb, b_gates], writes=[b_ybuf[yr]])
                    else:
                        P.op(DVE, "scalar_tensor_tensor", dict(out=dst, in0=pa, scalar=gsc, in1=dst,
                                                               op0=ALU.mult, op1=ALU.add),
                             reads=[pb, b_gates, b_ybuf[yr]], writes=[b_ybuf[yr]])
                if not last_seg:
                    P.dma(SP, yacc_d[i * 128:(i + 1) * 128, :], ybuf[yr][:], reads=[b_ybuf[yr]], writes=[b_yd[i]])
                else:
                    P.dma(SP, ls_x[:], x1s[i * 128:(i + 1) * 128, :], reads=[b_x1s], writes=[b_lsx])
                    P.op(DVE, "scalar_tensor_tensor", dict(out=ybuf[yr][:], in0=ls_x[:], scalar=ALPHA, in1=ybuf[yr][:],
                                                           op0=ALU.mult, op1=ALU.add),
                         reads=[b_lsx, b_ybuf[yr]], writes=[b_ybuf[yr]])
                    layer_norm(ybuf[yr][:], b_ybuf[yr], 2, ybuf[yr][:], b_ybuf[yr], i % 2)
                    P.dma(SP, y_out[i * 128:(i + 1) * 128, :], ybuf[yr][:], reads=[b_ybuf[yr]], writes=[b_out])

    P.final_wait(SP, [b_out])
    with nc.Block() as block:
        P.emit(block)
    return nc


def kernel(**inputs):
    inp = {k: np.asarray(v) for k, v in inputs.items()}
    x = inp["x"][0]
    xs = [np.ascontiguousarray(x[c * T:(c + 1) * T]) for c in range(NCORE)]
    cores = list(range(NCORE))
    for l in range(2):
        ncA = build_A()
        resA = run_bass_kernel_spmd(ncA, inputs_A(xs, l, inp), core_ids=cores).results
        moe = (l % 2 == 1)
        if not moe:
            ncB = build_B(False, 5632)
            resB = run_bass_kernel_spmd(ncB, inputs_B(xs, l, inp, resA, False), core_ids=cores).results
        else:
            ncBa = build_B(True, 7168, "a")
            resBa = run_bass_kernel_spmd(ncBa, inputs_B(xs, l, inp, resA, True), core_ids=cores).results
            cnt = np.stack([np.asarray(resBa[c]["cnt_o"])[0] for c in range(NCORE)])
            caps = [int(-(-int(round(float(cnt[:, e].max()))) // 128) * 128) for e in range(8)]
            ncBb = build_B(True, 7168, "b", caps)
            resB = run_bass_kernel_spmd(ncBb, inputs_Bb(l, inp, resBa), core_ids=cores).results
        xs = [np.asarray(resB[c]["y"], dtype=np.float32) for c in range(NCORE)]
    return np.concatenate(xs, axis=0)[None].astype(np.float32)
```

```python
import numpy as np
import concourse.bass as bass
import concourse.mybir as mybir
from concourse.bass_utils import run_bass_kernel_spmd

F32 = mybir.dt.float32
BF16 = mybir.dt.bfloat16
AF = mybir.ActivationFunctionType
ALU = mybir.AluOpType
AX = mybir.AxisListType

PE, ACT, DVE, POOL, SP = "tensor", "scalar", "vector", "gpsimd", "sync"
ENGS = (PE, ACT, DVE, POOL, SP)
SEM_ROLL = 30000

T = 2048
D = 2048
HT = 1024
NC_IN = 3712
KVROWS = 16 * 128 + 64
ALPHA = 4 ** 0.25
LN_EPS = 1e-5
RMS_EPS = 1e-6
NEG = -1e30
NCORE = 8
S = 16384
DEBUG_COUNTS = False
MOE_CAPMAX = 896


class Buf:
    __slots__ = ("name", "w", "r", "dsem")

    def __init__(self, name=""):
        self.name = name
        self.w = None
        self.r = {}
        self.dsem = None


class Prog:
    def __init__(self, nc):
        self.nc = nc
        self.streams = {e: [] for e in ENGS}
        self.esems = {e: None for e in ENGS}
        self.waited = {}
        self.all_sems = []

    def _new_sem(self, name):
        h = self.nc.alloc_semaphore(name)
        rec = [h, 0]
        self.all_sems.append(rec)
        return rec

    def _eng_sem(self, eng):
        rec = self.esems[eng]
        if rec is None or rec[1] >= SEM_ROLL:
            rec = self._new_sem(f"e_{eng}_{len(self.all_sems)}")
            self.esems[eng] = rec
        return rec

    def _collect(self, eng, reads, writes, is_dma):
        waits = {}

        def need(tok, same_ok):
            if tok is None:
                return
            rec, val, src, dma = tok
            if dma:
                val = rec[1]
            elif src == eng and not is_dma:
                if eng == PE or same_ok:
                    return
            k = id(rec)
            if waits.get(k, (None, -1))[1] < val:
                waits[k] = (rec, val)

        for b in reads:
            need(b.w, False)
        for b in writes:
            need(b.w, True)
            for t in b.r.values():
                need(t, True)
        out = []
        for k, (rec, val) in waits.items():
            key = (eng, k)
            if self.waited.get(key, -1) >= val:
                continue
            self.waited[key] = val
            out.append((rec[0], val))
        return out

    def op(self, eng, name, args, reads=(), writes=(), sig=True, own_sem_inc=None):
        def fn(e, name=name, args=args):
            return getattr(e, name)(**args)
        waits = self._collect(eng, reads, writes, False)
        tok = None
        if own_sem_inc is not None:
            rec = self._new_sem(f"own_{len(self.all_sems)}")
            rec[1] += own_sem_inc
            tok = (rec, rec[1], eng, True)
            self.streams[eng].append((waits, fn, (rec[0], own_sem_inc)))
        elif sig:
            rec = self._eng_sem(eng)
            rec[1] += 1
            tok = (rec, rec[1], eng, False)
            self.streams[eng].append((waits, fn, (rec[0], 1)))
        else:
            self.streams[eng].append((waits, fn, None))
        if tok is not None:
            for b in writes:
                b.w = tok
                b.r = {}
            for b in reads:
                b.r[id(rec)] = tok
        return tok

    def dma(self, eng, out, in_, reads=(), writes=(), sembuf=None, **kw):
        waits = self._collect(eng, reads, writes, True)
        sb = sembuf if sembuf is not None else (writes[0] if writes else reads[0])
        if sb.dsem is None:
            sb.dsem = self._new_sem(f"d_{sb.name}_{len(self.all_sems)}")
        rec = sb.dsem
        rec[1] += 16
        tok = (rec, rec[1], eng, True)

        def fn(e, out=out, in_=in_, kw=kw):
            return e.dma_start(out=out, in_=in_, **kw)
        self.streams[eng].append((waits, fn, (rec[0], 16)))
        for b in writes:
            b.w = tok
            b.r = {}
        for b in reads:
            b.r[id(rec)] = tok
        return tok

    def barrier(self):
        for eng in ENGS:
            waits = []
            for rec in self.all_sems:
                if rec[1] > 0 and self.waited.get((eng, id(rec)), -1) < rec[1]:
                    self.waited[(eng, id(rec))] = rec[1]
                    waits.append((rec[0], rec[1]))
            if waits:
                self.streams[eng].append((waits, None, None))

    def final_wait(self, eng, bufs):
        waits = self._collect(eng, bufs, (), True)
        self.streams[eng].append((waits, None, None))

    def emit(self, block):
        nc = self.nc

        def make(eng):
            stream = self.streams[eng]

            def body(e):
                for waits, fn, inc in stream:
                    for h, v in waits:
                        e.wait_ge(h, v)
                    if fn is not None:
                        ins = fn(e)
                        if inc is not None:
                            ins.then_inc(inc[0], inc[1])
            return body
        block.tensor(make(PE))
        block.scalar(make(ACT))
        block.vector(make(DVE))
        block.gpsimd(make(POOL))
        block.sync(make(SP))


SB_BASE = 16512
SB_END = 229376
_DT_BYTES = {F32: 4, BF16: 2}


class SBAlloc:
    def __init__(self, nc):
        self.nc = nc
        self.off = SB_BASE
        self.n = 0

    def alloc(self, name, shape, dtype):
        size = 1
        for d in shape[1:]:
            size *= d
        size *= _DT_BYTES[dtype]
        size = (size + 31) // 32 * 32
        assert self.off + size <= SB_END, f"SBUF overflow at {name}: {self.off + size}"
        t = self.nc.alloc_sbuf_tensor_at(f"{name}_{self.n}", list(shape), dtype, offset=self.off)
        self.n += 1
        self.off += size
        return t

    def mark(self):
        return self.off

    def reset(self, m):
        self.off = m


def mm_acc(P, out_ap, out_buf, pairs, reads):
    n = len(pairs)
    for i, (l, r) in enumerate(pairs):
        P.op(PE, "matmul", dict(out=out_ap, lhsT=l, rhs=r, start=(i == 0), stop=(i == n - 1)),
             reads=reads, writes=[out_buf], sig=(i == n - 1))


import ml_dtypes

BF = ml_dtypes.bfloat16
ROPE_THETA = 10000.0


def rope_np(dim):
    inv = np.power(np.float32(ROPE_THETA), -(np.arange(0, dim, 2, dtype=np.float32) / np.float32(dim))).astype(np.float32)
    ang = np.arange(S, dtype=np.float32)[:, None] * inv[None, :]
    return np.cos(ang).astype(np.float32), np.sin(ang).astype(np.float32)


def rope_tables_fm(dim):
    c, s = rope_np(dim)
    cf = np.concatenate([c, c], axis=1).T
    sf = np.concatenate([-s, s], axis=1).T
    return np.ascontiguousarray(cf), np.ascontiguousarray(sf)


def perm_half(w, dim):
    n = w.shape[1] // dim
    idx = np.concatenate([(np.arange(dim) + dim // 2) % dim + b * dim for b in range(n)])
    return w[:, idx]


def prep_w_in(w):
    c_q = w[:, 0:512]
    c_kv = w[:, 512:768]
    k_pe = w[:, 768:832]
    q_s = w[:, 832:1856]
    k_s = w[:, 1856:2112]
    v_s = w[:, 2112:2368]
    cols = [c_q, c_kv, k_pe, perm_half(k_pe, 64)]
    qsp = perm_half(q_s, 128)
    for h in range(8):
        cols += [q_s[:, h * 128:(h + 1) * 128], qsp[:, h * 128:(h + 1) * 128]]
    ksp = perm_half(k_s, 128)
    for h in range(2):
        cols += [k_s[:, h * 128:(h + 1) * 128], ksp[:, h * 128:(h + 1) * 128]]
    cols.append(v_s)
    out = np.ascontiguousarray(np.concatenate(cols, axis=1))
    assert out.shape[1] == 3712
    return out


def prep_w_qb(w):
    cols = []
    for h in range(8):
        blk = w[:, h * 192:(h + 1) * 192]
        pe = blk[:, 128:192]
        cols += [blk[:, :128], pe, perm_half(pe, 64)]
    return np.ascontiguousarray(np.concatenate(cols, axis=1))


def prep_w_kvb(w):
    ks = [w[:, h * 256:h * 256 + 128] for h in range(8)]
    vs = [w[:, h * 256 + 128:(h + 1) * 256] for h in range(8)]
    return np.ascontiguousarray(np.concatenate(ks + vs, axis=1))


def fm_vec(g, nchunk):
    return np.ascontiguousarray(g.reshape(nchunk, 128).T)


_TABS = {}


def tables():
    if not _TABS:
        _TABS["m"] = rope_tables_fm(64)
        _TABS["s"] = rope_tables_fm(128)
    return _TABS


def inputs_A(xs, l, inp):
    tb = tables()
    w_in = prep_w_in(inp["w_in"][l])
    w_qb = prep_w_qb(inp["w_qb"][l])
    w_kvb = prep_w_kvb(inp["w_kvb"][l])
    g_cq = fm_vec(inp["g_cq"][l], 4)
    g_ckv = fm_vec(inp["g_ckv"][l], 2)
    maps = []
    for c in range(NCORE):
        sl = slice(c * T, (c + 1) * T)
        maps.append({
            "x": np.ascontiguousarray(xs[c]), "w_in": w_in, "w_qb": w_qb, "w_kvb": w_kvb,
            "g_cq": g_cq, "g_ckv": g_ckv,
            "cos_m": np.ascontiguousarray(tb["m"][0][:, sl]), "sin_m": np.ascontiguousarray(tb["m"][1][:, sl]),
            "cos_s": np.ascontiguousarray(tb["s"][0][:, sl]), "sin_s": np.ascontiguousarray(tb["s"][1][:, sl]),
        })
    return maps


def swa_masks(core):
    qi = np.arange(128)[:, None]
    ki = np.arange(128)[None, :]
    prev = np.where(qi <= ki, 0.0, NEG).astype(np.float32)
    mid = np.zeros((128, 128), np.float32)
    nxt = np.where(ki <= qi, 0.0, NEG).astype(np.float32)
    full = np.concatenate([prev, mid, nxt], axis=1)
    allneg = np.full((128, 128), NEG, np.float32)
    first = full.copy()
    last = full.copy()
    if core == 0:
        first[:, :128] = allneg
    if core == NCORE - 1:
        last[:, 256:] = allneg
    return np.ascontiguousarray(np.stack([first, full, last], axis=1))


def inputs_Bb(l, inp, resBa):
    j = l // 2
    ln = np.stack([inp["ln1_g"][l], inp["ln1_b"][l], inp["ln2_g"][l], inp["ln2_b"][l]], 0)
    ln_bc = np.ascontiguousarray(np.broadcast_to(ln[None], (128, 4, 2048))).astype(np.float32)
    iota_row = np.ascontiguousarray(np.broadcast_to(np.arange(MOE_CAPMAX, dtype=np.float32)[None, :], (128, MOE_CAPMAX)))
    maps = []
    for c in range(NCORE):
        maps.append({"ln_bc": ln_bc, "iota_row": iota_row,
                     "wg": inp["moe_wg"][j], "wu": inp["moe_wu"][j], "wd": inp["moe_wd"][j],
                     "x1s_in": resBa[c]["x1s"], "xb_in": resBa[c]["xb_d"],
                     "gates_in": resBa[c]["gates_o"], "rank_in": resBa[c]["rank_o"]})
    return maps


def inputs_B(xs, l, inp, resA, moe):
    kv_all = np.ascontiguousarray(np.concatenate([resA[c]["kv_out"] for c in range(NCORE)], axis=0))
    sink_bc = np.ascontiguousarray(np.broadcast_to(inp["sink"][l][None, :], (128, 8))).astype(np.float32)
    g_mla = fm_vec(inp["g_out_mla"][l], 8)
    g_swa = np.ascontiguousarray(np.broadcast_to(inp["g_out_swa"][l][None, :], (128, 1024))).astype(np.float32)
    ln = np.stack([inp["ln1_g"][l], inp["ln1_b"][l], inp["ln2_g"][l], inp["ln2_b"][l]], 0)
    ln_bc = np.ascontiguousarray(np.broadcast_to(ln[None], (128, 4, 2048))).astype(np.float32)
    j = l // 2
    maps = []
    for c in range(NCORE):
        ks = resA[c]["ks_out"]
        vs = resA[c]["vs_out"]
        kprev = resA[c - 1]["ks_out"][:, -128:] if c > 0 else np.zeros((256, 128), ks.dtype)
        knext = resA[c + 1]["ks_out"][:, :128] if c < NCORE - 1 else np.zeros((256, 128), ks.dtype)
        vprev = resA[c - 1]["vs_out"][-128:] if c > 0 else np.zeros((128, 256), vs.dtype)
        vnext = resA[c + 1]["vs_out"][:128] if c < NCORE - 1 else np.zeros((128, 256), vs.dtype)
        m = {
            "x": np.ascontiguousarray(xs[c]), "kv_all": kv_all,
            "qn": resA[c]["qn_out"], "qpe": resA[c]["qpe_out"], "qs": resA[c]["qs_out"],
            "ks_ext": np.ascontiguousarray(np.concatenate([kprev, ks, knext], axis=1)),
            "vs_ext": np.ascontiguousarray(np.concatenate([vprev, vs, vnext], axis=0)),
            "masks": swa_masks(c), "sink_bc": sink_bc, "g_mla": g_mla, "g_swa_bc": g_swa,
            "w_out": inp["w_out"][l], "ln_bc": ln_bc,
        }
        if moe:
            lmat = (np.arange(128)[:, None] < np.arange(128)[None, :]).astype(np.float32).astype(BF)
            m.update({"w_router": inp["router_w"][j], "lmat": lmat})
        else:
            m.update({"wg": inp["dense_wg"][j], "wu": inp["dense_wu"][j], "wd": inp["dense_wd"][j]})
        maps.append(m)
    return maps


def build_A():
    nc = bass.Bass("TRN2", target_bir_lowering=False)
    x = nc.dram_tensor("x", [T, D], F32, kind="ExternalInput").ap()
    w_in = nc.dram_tensor("w_in", [D, NC_IN], F32, kind="ExternalInput").ap()
    w_qb = nc.dram_tensor("w_qb", [512, 2048], F32, kind="ExternalInput").ap()
    w_kvb = nc.dram_tensor("w_kvb", [256, 2048], F32, kind="ExternalInput").ap()
    g_cq = nc.dram_tensor("g_cq", [128, 4], F32, kind="ExternalInput").ap()
    g_ckv = nc.dram_tensor("g_ckv", [128, 2], F32, kind="ExternalInput").ap()
    cos_m = nc.dram_tensor("cos_m", [64, T], F32, kind="ExternalInput").ap()
    sin_m = nc.dram_tensor("sin_m", [64, T], F32, kind="ExternalInput").ap()
    cos_s = nc.dram_tensor("cos_s", [128, T], F32, kind="ExternalInput").ap()
    sin_s = nc.dram_tensor("sin_s", [128, T], F32, kind="ExternalInput").ap()
    kv_out = nc.dram_tensor("kv_out", [16 * 128 + 64, T], BF16, kind="ExternalOutput").ap()
    qn_out = nc.dram_tensor("qn_out", [8 * 128, T], BF16, kind="ExternalOutput").ap()
    qpe_out = nc.dram_tensor("qpe_out", [8 * 64, T], BF16, kind="ExternalOutput").ap()
    qs_out = nc.dram_tensor("qs_out", [8 * 128, T], BF16, kind="ExternalOutput").ap()
    ks_out = nc.dram_tensor("ks_out", [2 * 128, T], BF16, kind="ExternalOutput").ap()
    vs_out = nc.dram_tensor("vs_out", [T, 256], BF16, kind="ExternalOutput").ap()

    P = Prog(nc)
    sb = SBAlloc(nc)
    ps = nc.alloc_psum_tensor("ps", [128, 8, 512], F32)
    psb = [Buf(f"ps{i}") for i in range(8)]
    pctr = [0]

    def bank():
        i = pctr[0] % 8
        pctr[0] += 1
        return ps[:, i, :], psb[i]

    b_out = Buf("out")

    ident = sb.alloc("ident", [128, 128], BF16)
    ones = sb.alloc("ones", [128, 128], BF16)
    gq = sb.alloc("gq", [128, 4], F32)
    gkv = sb.alloc("gkv", [128, 2], F32)
    b_c = Buf("consts")
    P.op(POOL, "memset", dict(ap=ident[:], constant=0.0), writes=[b_c])
    P.op(POOL, "affine_select", dict(out=ident[:], in_=ident[:], pattern=[[-1, 128]],
                                     compare_op=ALU.not_equal, fill=1.0, base=0,
                                     channel_multiplier=1), reads=[b_c], writes=[b_c])
    P.op(POOL, "memset", dict(ap=ones[:], constant=1.0), reads=[b_c], writes=[b_c])
    b_g = Buf("g")
    P.dma(SP, gq[:], g_cq[:, :], writes=[b_g])
    P.dma(SP, gkv[:], g_ckv[:, :], writes=[b_g])

    cosm = sb.alloc("cosm", [64, HT], F32)
    sinm = sb.alloc("sinm", [64, HT], F32)
    coss = sb.alloc("coss", [128, HT], F32)
    sins = sb.alloc("sins", [128, HT], F32)
    cqn = sb.alloc("cqn", [128, 4, HT], BF16)
    ckvn = sb.alloc("ckvn", [128, 2, HT], BF16)
    kpeT = sb.alloc("kpeT", [64, HT], BF16)
    b_tab = Buf("tab")
    b_cqn = [Buf(f"cqn{i}") for i in range(2)]
    b_ckvn = [Buf(f"ckvn{i}") for i in range(2)]
    b_kpe = Buf("kpe")
    m_phase = sb.mark()

    w_in_v = w_in.rearrange("(k p) c -> p k c", p=128)
    groups = [(0, 512), (512, 896)] + [(896 + 512 * i, 896 + 512 * (i + 1)) for i in range(4)] + \
             [(2944, 3456), (3456, 3712)]

    for half in range(2):
        t0 = half * HT
        sb.reset(m_phase)
        xT = sb.alloc("xT", [128, 16, HT], BF16)
        b_xT = [Buf(f"xT{i}") for i in range(HT // 128)]
        xt = [sb.alloc(f"xt{r}", [128, D], BF16) for r in range(2)]
        b_xt = [Buf(f"xt{r}") for r in range(2)]
        wring = [sb.alloc(f"wr{r}", [128, 16, 512], BF16) for r in range(2)]
        b_wr = [Buf(f"wr{r}") for r in range(2)]
        sq = sb.alloc("sq", [128, 4, 512], BF16)
        b_sq = Buf("sq")
        rstd = sb.alloc("rstd", [128, 512], F32)
        b_rstd = Buf("rstd")
        tmp = [sb.alloc(f"tmp{r}", [128, 512], F32) for r in range(4)]
        b_tmp = [Buf(f"tmp{r}") for r in range(4)]
        stg = [sb.alloc(f"stg{r}", [128, HT], BF16) for r in range(4)]
        b_stg = [Buf(f"stg{r}") for r in range(4)]
        vstg = sb.alloc("vstg", [128, HT // 128, 256], BF16)
        b_vstg = Buf("vstg")

        P.dma(SP, cosm[:], cos_m[:, t0:t0 + HT], writes=[b_tab])
        P.dma(SP, sinm[:], sin_m[:, t0:t0 + HT], writes=[b_tab])
        P.dma(SP, coss[:], cos_s[:, t0:t0 + HT], writes=[b_tab])
        P.dma(SP, sins[:], sin_s[:, t0:t0 + HT], writes=[b_tab])

        for i in range(HT // 128):
            r = i % 2
            P.dma(POOL, xt[r][:], x[t0 + i * 128:t0 + (i + 1) * 128, :], writes=[b_xt[r]])
            j = (pctr[0] // 2) % 4
            pctr[0] += 2
            pT = ps[:, 2 * j:2 * j + 2, :].bitcast(BF16).rearrange("p a (b c) -> p (a b) c", c=128)
            pbufs = [psb[2 * j], psb[2 * j + 1]]
            for k in range(16):
                P.op(PE, "transpose", dict(out=pT[:, k, :], in_=xt[r][:, k * 128:(k + 1) * 128], identity=ident[:]),
                     reads=[b_xt[r], b_c], writes=pbufs, sig=(k == 15))
            eng = ACT if i % 2 == 0 else DVE
            if eng == ACT:
                P.op(ACT, "copy", dict(out=xT[:, :, i * 128:(i + 1) * 128], in_=pT[:, :, :]),
                     reads=pbufs, writes=[b_xT[i]])
            else:
                P.op(DVE, "tensor_copy", dict(out=xT[:, :, i * 128:(i + 1) * 128], in_=pT[:, :, :]),
                     reads=pbufs, writes=[b_xT[i]])

        def rms_block(chunks, nfeat, gvec, dst, dstbuf, tb):
            n = len(chunks)
            for c, (pa, pb) in enumerate(chunks):
                P.op(ACT, "activation", dict(out=sq[:, c, :], in_=pa, func=AF.Square),
                     reads=[pb], writes=[b_sq])
            sa, sbf = bank()
            mm_acc(P, sa, sbf, [(ones[:], sq[:, c, :]) for c in range(n)], [b_sq, b_c])
            P.op(DVE, "tensor_scalar", dict(out=rstd[:], in0=sa, scalar1=1.0 / nfeat, scalar2=RMS_EPS,
                                            op0=ALU.mult, op1=ALU.add), reads=[sbf], writes=[b_rstd])
            P.op(ACT, "activation", dict(out=rstd[:], in_=rstd[:], func=AF.Sqrt), reads=[b_rstd], writes=[b_rstd])
            P.op(DVE, "reciprocal", dict(out=rstd[:], in_=rstd[:]), reads=[b_rstd], writes=[b_rstd])
            for c, (pa, pb) in enumerate(chunks):
                P.op(DVE, "scalar_tensor_tensor", dict(
                    out=dst[:, c, tb * 512:(tb + 1) * 512], in0=pa, scalar=gvec[:, c:c + 1], in1=rstd[:],
                    op0=ALU.mult, op1=ALU.mult), reads=[pb, b_rstd, b_g], writes=[dstbuf])

        tctr = [0]

        def rope_block(pa, pb_a, pu, pb_u, cosT, sinT, nrow, dst_ap, dst_buf, tb):
            i0 = tctr[0] % 4
            i1 = (tctr[0] + 1) % 4
            tctr[0] += 2
            cs = slice(tb * 512, (tb + 1) * 512)
            P.op(DVE, "tensor_tensor", dict(out=tmp[i0][:nrow, :], in0=pa, in1=cosT[:nrow, cs], op=ALU.mult),
                 reads=[pb_a, b_tab], writes=[b_tmp[i0]])
            P.op(DVE, "tensor_tensor", dict(out=tmp[i1][:nrow, :], in0=pu, in1=sinT[:nrow, cs], op=ALU.mult),
                 reads=[pb_u, b_tab], writes=[b_tmp[i1]])
            P.op(POOL, "tensor_tensor", dict(out=dst_ap, in0=tmp[i0][:nrow, :], in1=tmp[i1][:nrow, :], op=ALU.add),
                 reads=[b_tmp[i0], b_tmp[i1]], writes=[dst_buf])

        sctr = [0]
        for gi, (c0, c1) in enumerate(groups):
            r = gi % 2
            ncol = c1 - c0
            P.dma(POOL, wring[r][:, :, :ncol], w_in_v[:, :, c0:c1], writes=[b_wr[r]])
            W = wring[r]
            if gi == 7:
                for i in range(HT // 128):
                    pa, pb = bank()
                    mm_acc(P, pa[:, :256], pb,
                           [(xT[:, k, i * 128:(i + 1) * 128], W[:, k, 0:256]) for k in range(16)],
                           [b_xT[i], b_wr[r]])
                    P.op(ACT, "copy", dict(out=vstg[:, i, :], in_=pa[:, :256]),
                         reads=[pb], writes=[b_vstg])
                P.dma(SP, vs_out[t0:t0 + HT, :].rearrange("(i p) c -> p i c", p=128), vstg[:],
                      reads=[b_vstg], writes=[b_out])
                continue
            if gi >= 2:
                sidx = [sctr[0] % 4, (sctr[0] + 1) % 4]
                sctr[0] += 2
            for tb in range(HT // 512):
                xb = [b_xT[4 * tb + q] for q in range(4)]
                cs = slice(tb * 512, (tb + 1) * 512)

                def chunk(col0, ncols_):
                    pa, pb = bank()
                    mm_acc(P, pa[:ncols_, :], pb,
                           [(W[:, k, col0:col0 + ncols_], xT[:, k, cs]) for k in range(16)],
                           xb + [b_wr[r]])
                    return pa, pb
                if gi == 0:
                    chunks = [chunk(c * 128, 128) for c in range(4)]
                    rms_block(chunks, 512, gq, cqn, b_cqn[tb], tb)
                elif gi == 1:
                    chunks = [chunk(c * 128, 128) for c in range(2)]
                    rms_block(chunks, 256, gkv, ckvn, b_ckvn[tb], tb)
                    pa, pb = chunk(256, 64)
                    pu, pbu = chunk(320, 64)
                    rope_block(pa[:64, :], pb, pu[:64, :], pbu, cosm, sinm, 64, kpeT[:, cs], b_kpe, tb)
                else:
                    for hh in range(2):
                        pa, pb = chunk(hh * 256, 128)
                        pu, pbu = chunk(hh * 256 + 128, 128)
                        rope_block(pa, pb, pu, pbu, coss, sins, 128, stg[sidx[hh]][:, cs], b_stg[sidx[hh]], tb)
            if gi >= 2:
                for hh in range(2):
                    if gi <= 5:
                        h = (gi - 2) * 2 + hh
                        dst = qs_out[h * 128:(h + 1) * 128, t0:t0 + HT]
                    else:
                        dst = ks_out[hh * 128:(hh + 1) * 128, t0:t0 + HT]
                    P.dma(SP, dst, stg[sidx[hh]][:], reads=[b_stg[sidx[hh]]], writes=[b_out])

        P.dma(SP, kv_out[16 * 128:16 * 128 + 64, t0:t0 + HT], kpeT[:], reads=[b_kpe], writes=[b_out])

        P.barrier()
        sb.reset(m_phase)
        wqb = sb.alloc("wqb", [128, 4, 2048], BF16)
        wkvb = sb.alloc("wkvb", [128, 2, 2048], BF16)
        b_wqb, b_wkvb = Buf("wqb"), Buf("wkvb")
        qn = sb.alloc("qn", [128, 8, HT], BF16)
        qpe = sb.alloc("qpe", [64, 8, HT], BF16)
        KT = sb.alloc("KT", [128, 8, HT], BF16)
        Vt = sb.alloc("Vt", [128, HT // 128, 1024], BF16)
        b_qn, b_qpe, b_KT, b_Vt = Buf("qn"), Buf("qpe"), Buf("KT"), Buf("Vt")
        tmp = [sb.alloc(f"tmpb{r}", [128, 512], F32) for r in range(4)]
        b_tmp = [Buf(f"tmpb{r}") for r in range(4)]
        P.dma(POOL, wqb[:], w_qb.rearrange("(k p) c -> p k c", p=128), writes=[b_wqb])
        P.dma(POOL, wkvb[:], w_kvb.rearrange("(k p) c -> p k c", p=128), writes=[b_wkvb])
        for h in range(8):
            for tb in range(HT // 512):
                cs = slice(tb * 512, (tb + 1) * 512)
                pa, pb = bank()
                mm_acc(P, pa, pb, [(wqb[:, c, 256 * h:256 * h + 128], cqn[:, c, cs]) for c in range(4)],
                       [b_wqb, b_cqn[tb]])
                P.op(ACT, "copy", dict(out=qn[:, h, cs], in_=pa), reads=[pb], writes=[b_qn])
                pa, pb = bank()
                mm_acc(P, pa[:64, :], pb, [(wqb[:, c, 256 * h + 128:256 * h + 192], cqn[:, c, cs]) for c in range(4)],
                       [b_wqb, b_cqn[tb]])
                pu, pbu = bank()
                mm_acc(P, pu[:64, :], pbu, [(wqb[:, c, 256 * h + 192:256 * h + 256], cqn[:, c, cs]) for c in range(4)],
                       [b_wqb, b_cqn[tb]])
                rope_block(pa[:64, :], pb, pu[:64, :], pbu, cosm, sinm, 64, qpe[:, h, cs], b_qpe, tb)
                pa, pb = bank()
                mm_acc(P, pa, pb, [(wkvb[:, c, 128 * h:128 * h + 128], ckvn[:, c, cs]) for c in range(2)],
                       [b_wkvb, b_ckvn[tb]])
                P.op(ACT, "copy", dict(out=KT[:, h, cs], in_=pa), reads=[pb], writes=[b_KT])
        for i in range(HT // 128):
            tb = i // 4
            for hf in range(2):
                pa, pb = bank()
                mm_acc(P, pa, pb,
                       [(ckvn[:, c, i * 128:(i + 1) * 128], wkvb[:, c, 1024 + hf * 512:1024 + (hf + 1) * 512])
                        for c in range(2)], [b_wkvb, b_ckvn[tb]])
                P.op(DVE, "tensor_copy", dict(out=Vt[:, i, hf * 512:(hf + 1) * 512], in_=pa),
                     reads=[pb], writes=[b_Vt])
        P.dma(SP, qn_out.rearrange("(h p) t -> p h t", p=128)[:, :, t0:t0 + HT], qn[:], reads=[b_qn], writes=[b_out])
        P.dma(SP, qpe_out.rearrange("(h p) t -> p h t", p=64)[:, :, t0:t0 + HT], qpe[:], reads=[b_qpe], writes=[b_out])
        for h in range(8):
            P.dma(SP, kv_out[(2 * h) * 128:(2 * h + 1) * 128, t0:t0 + HT], KT[:, h, :], reads=[b_KT], writes=[b_out])
            dst = kv_out[(2 * h + 1) * 128:(2 * h + 2) * 128, :].rearrange("p (i d) -> p i d", d=128)
            P.dma(SP, dst[:, half * (HT // 128):(half + 1) * (HT // 128), :], Vt[:, :, h * 128:(h + 1) * 128],
                  reads=[b_Vt], writes=[b_out])
        P.barrier()

    P.final_wait(SP, [b_out])
    with nc.Block() as block:
        P.emit(block)
    return nc


def build_B(moe, F, stage="all", caps=None):
    nc = bass.Bass("TRN2", target_bir_lowering=False)

    def din(name, shape, dt=F32):
        return nc.dram_tensor(name, list(shape), dt, kind="ExternalInput").ap()
    ln_d = din("ln_bc", [128, 4, D])
    if stage != "b":
        x = din("x", [T, D])
        kv_all = din("kv_all", [8 * KVROWS, T], BF16)
        qn_d = din("qn", [1024, T], BF16)
        qpe_d = din("qpe", [512, T], BF16)
        qs_d = din("qs", [1024, T], BF16)
        ks_d = din("ks_ext", [256, T + 256], BF16)
        vs_d = din("vs_ext", [T + 256, 256], BF16)
        masks_d = din("masks", [128, 3, 384])
        sink_d = din("sink_bc", [128, 8])
        gmla_d = din("g_mla", [128, 8])
        gswa_d = din("g_swa_bc", [128, 1024])
        wout_d = din("w_out", [D, D])
    if stage == "a":
        wr_d = din("w_router", [D, 8])
        lmat_d = din("lmat", [128, 128], BF16)
    if stage == "b":
        iota_d = din("iota_row", [128, MOE_CAPMAX])
        wg_d = din("wg", [8, D, F])
        wu_d = din("wu", [8, D, F])
        wd_d = din("wd", [8, F, D])
        x1s = din("x1s_in", [T, D])
        xb_d = din("xb_in", [T, D], BF16)
        gates_i = din("gates_in", [128, 128])
        rank_i = din("rank_in", [128, 128])
    if stage == "all":
        wg_d = din("wg", [1, D, F])
        wu_d = din("wu", [1, D, F])
        wd_d = din("wd", [1, F, D])
    if stage == "a":
        x1s = nc.dram_tensor("x1s", [T, D], F32, kind="ExternalOutput").ap()
        xb_d = nc.dram_tensor("xb_d", [T, D], BF16, kind="ExternalOutput").ap()
        gates_o = nc.dram_tensor("gates_o", [128, 128], F32, kind="ExternalOutput").ap()
        rank_o = nc.dram_tensor("rank_o", [128, 128], F32, kind="ExternalOutput").ap()
        cnt_o = nc.dram_tensor("cnt_o", [128, 8], F32, kind="ExternalOutput").ap()
    else:
        y_out = nc.dram_tensor("y", [T, D], F32, kind="ExternalOutput").ap()
    if stage == "all":
        x1s = nc.dram_tensor("x1s", [T, D], F32).ap()
        xb_d = nc.dram_tensor("xb_d", [T, D], BF16).ap()
    x1T_d = nc.dram_tensor("x1T_d", [128, 16, T], BF16).ap()
    yacc_d = nc.dram_tensor("yacc_d", [T, D], F32).ap()
    b_xbd = Buf("xb_d")

    P = Prog(nc)
    sb = SBAlloc(nc)
    ps = nc.alloc_psum_tensor("ps", [128, 8, 512], F32)
    psb = [Buf(f"ps{i}") for i in range(8)]
    pctr = [0]
    prange = [0, 8]

    def bank():
        lo, hi = prange
        i = lo + pctr[0] % (hi - lo)
        pctr[0] += 1
        return ps[:, i, :], psb[i]

    b_out = Buf("out")
    b_x1s = Buf("x1s")
    b_x1T = Buf("x1T_d")

    ident = sb.alloc("ident", [128, 128], BF16)
    ones = sb.alloc("ones", [128, 128], BF16)
    sink = sb.alloc("sink", [128, 8], F32)
    gmla = sb.alloc("gmla", [128, 8], F32)
    b_c = Buf("consts")
    P.op(POOL, "memset", dict(ap=ident[:], constant=0.0), writes=[b_c])
    P.op(POOL, "affine_select", dict(out=ident[:], in_=ident[:], pattern=[[-1, 128]],
                                     compare_op=ALU.not_equal, fill=1.0, base=0,
                                     channel_multiplier=1), reads=[b_c], writes=[b_c])
    P.op(POOL, "memset", dict(ap=ones[:], constant=1.0), reads=[b_c], writes=[b_c])
    b_p = Buf("params")
    if stage != "b":
        P.dma(SP, sink[:], sink_d[:, :], writes=[b_p])
        P.dma(SP, gmla[:], gmla_d[:, :], writes=[b_p])
    gates = sb.alloc("gates", [128, 16, 8], F32)
    b_gates = Buf("gates")
    selm = sb.alloc("selm", [128, 128], F32)
    b_selm = Buf("selm")
    if stage == "a":
        wr32 = sb.alloc("wr32", [128, 16, 8], F32)
        wrh = sb.alloc("wrh", [128, 16, 8], BF16)
        wrl = sb.alloc("wrl", [128, 16, 8], BF16)
        b_wr = Buf("wr")
        P.dma(SP, wr32[:], wr_d.rearrange("(k p) e -> p k e", p=128), writes=[b_wr])
        P.op(ACT, "copy", dict(out=wrh[:], in_=wr32[:]), reads=[b_wr], writes=[b_wr])
        P.op(DVE, "tensor_tensor", dict(out=wrl[:], in0=wr32[:], in1=wrh[:], op=ALU.subtract), reads=[b_wr], writes=[b_wr])
    m_const = sb.mark()

    mT = sb.alloc("mT", [128, 16, T], BF16)
    b_mT = [Buf(f"mT{i}") for i in range(16)]
    m_mT = sb.mark()

    def small_pool(prefix, n, width=1):
        ts = [sb.alloc(f"{prefix}{i}", [128, width], F32) for i in range(n)]
        bs = [Buf(f"{prefix}{i}") for i in range(n)]
        return ts, bs

    def layer_norm(src, b_src, gi, dst, b_dst, slot):
        s = [ls[8 * slot + q] for q in range(8)]
        bs = [b_ls[8 * slot + q] for q in range(8)]
        P.op(ACT, "activation", dict(out=junk[:], in_=src, func=AF.Identity, accum_out=s[0][:]),
             reads=[b_src], writes=[b_junk, bs[0]])
        P.op(ACT, "activation", dict(out=junk[:], in_=src, func=AF.Square, accum_out=s[1][:]),
             reads=[b_src], writes=[b_junk, bs[1]])
        P.op(DVE, "tensor_scalar", dict(out=s[2][:], in0=s[0][:], scalar1=1.0 / D, scalar2=None, op0=ALU.mult),
             reads=[bs[0]], writes=[bs[2]])
        P.op(DVE, "tensor_tensor", dict(out=s[3][:], in0=s[2][:], in1=s[2][:], op=ALU.mult),
             reads=[bs[2]], writes=[bs[3]])
        P.op(DVE, "scalar_tensor_tensor", dict(out=s[4][:], in0=s[1][:], scalar=1.0 / D, in1=s[3][:],
                                               op0=ALU.mult, op1=ALU.subtract),
             reads=[bs[1], bs[3]], writes=[bs[4]])
        P.op(DVE, "tensor_scalar", dict(out=s[4][:], in0=s[4][:], scalar1=LN_EPS, scalar2=None, op0=ALU.add),
             reads=[bs[4]], writes=[bs[4]])
        P.op(ACT, "activation", dict(out=s[5][:], in_=s[4][:], func=AF.Sqrt), reads=[bs[4]], writes=[bs[5]])
        P.op(DVE, "reciprocal", dict(out=s[6][:], in_=s[5][:]), reads=[bs[5]], writes=[bs[6]])
        P.op(DVE, "scalar_tensor_tensor", dict(out=s[7][:], in0=s[2][:], scalar=-1.0, in1=s[6][:],
                                               op0=ALU.mult, op1=ALU.mult),
             reads=[bs[2], bs[6]], writes=[bs[7]])
        P.op(ACT, "activation", dict(out=dst, in_=src, func=AF.Identity, scale=s[6][:], bias=s[7][:]),
             reads=[b_src, bs[6], bs[7]], writes=[b_dst])
        P.op(DVE, "tensor_tensor", dict(out=dst, in0=dst, in1=lnp[:, 0, :], op=ALU.mult),
             reads=[b_dst, b_lnp], writes=[b_dst])
        P.op(POOL, "tensor_tensor", dict(out=dst, in0=dst, in1=lnp[:, 1, :], op=ALU.add),
             reads=[b_dst, b_lnp], writes=[b_dst])

    if stage != "b":
        qsT = sb.alloc("qsT", [128, 8, T], BF16)
        ksT = sb.alloc("ksT", [128, 2, T + 256], BF16)
        vsx = sb.alloc("vsx", [128, 18, 256], BF16)
        msk = sb.alloc("msk", [128, 3, 384], F32)
        gswa = sb.alloc("gswa", [128, 1024], F32)
        b_swa_in = Buf("swa_in")
        P.dma(SP, qsT[:], qs_d.rearrange("(h p) t -> p h t", p=128), writes=[b_swa_in])
        P.dma(SP, ksT[:], ks_d.rearrange("(h p) t -> p h t", p=128), writes=[b_swa_in])
        P.dma(SP, vsx[:], vs_d.rearrange("(i p) c -> p i c", p=128), writes=[b_swa_in])
        P.dma(SP, msk[:], masks_d[:, :, :], writes=[b_swa_in])
        P.dma(SP, gswa[:], gswa_d[:, :], writes=[b_swa_in])
        NR = 3
        Sm = [sb.alloc(f"Sm{r}", [128, 384], F32) for r in range(NR)]
        b_Sm = [Buf(f"Sm{r}") for r in range(NR)]
        Pb = [sb.alloc(f"Pb{r}", [128, 384], BF16) for r in range(NR)]
        b_Pb = [Buf(f"Pb{r}") for r in range(NR)]
        PTs = [sb.alloc(f"PTs{r}", [128, 3, 128], BF16) for r in range(NR)]
        b_PTs = [Buf(f"PTs{r}") for r in range(NR)]
        st, b_st = small_pool("st", 8 * NR)
        otile = [sb.alloc(f"otile{r}", [128, 1024], F32) for r in range(2)]
        b_otile = [Buf(f"otile{r}") for r in range(2)]
        obf = [sb.alloc(f"obf{r}", [128, 1024], BF16) for r in range(2)]
        b_obf = [Buf(f"obf{r}") for r in range(2)]
        junk = sb.alloc("junk", [128, 2048], BF16)
        b_junk = Buf("junk")
        st2, b_st2 = small_pool("st2", 4)
        scale_s = 128 ** -0.5
        it = 0
        for i in range(16):
            mi = 0 if i == 0 else (2 if i == 15 else 1)
            ot, b_ot = otile[i % 2], b_otile[i % 2]
            for h in range(8):
                kvh = h // 4
                r = it % NR
                it += 1
                s = [st[8 * r + q] for q in range(8)]
                bs = [b_st[8 * r + q] for q in range(8)]
                pa, pb = bank()
                P.op(PE, "matmul", dict(out=pa[:, :384], lhsT=qsT[:, h, i * 128:(i + 1) * 128],
                                        rhs=ksT[:, kvh, i * 128:i * 128 + 384], start=True, stop=True),
                     reads=[b_swa_in], writes=[pb])
                P.op(DVE, "scalar_tensor_tensor", dict(out=Sm[r][:], in0=pa[:, :384], scalar=scale_s, in1=msk[:, mi, :],
                                                       op0=ALU.mult, op1=ALU.add), reads=[pb, b_swa_in], writes=[b_Sm[r]])
                P.op(DVE, "tensor_reduce", dict(out=s[0][:], in_=Sm[r][:], axis=AX.X, op=ALU.max),
                     reads=[b_Sm[r]], writes=[bs[0]])
                P.op(DVE, "tensor_tensor", dict(out=s[1][:], in0=s[0][:], in1=sink[:, h:h + 1], op=ALU.max),
                     reads=[bs[0], b_p], writes=[bs[1]])
                P.op(DVE, "tensor_scalar", dict(out=s[2][:], in0=s[1][:], scalar1=-1.0, scalar2=None, op0=ALU.mult),
                     reads=[bs[1]], writes=[bs[2]])
                P.op(ACT, "activation", dict(out=Pb[r][:], in_=Sm[r][:], func=AF.Exp, bias=s[2][:], accum_out=s[3][:]),
                     reads=[b_Sm[r], bs[2]], writes=[b_Pb[r], bs[3]])
                P.op(ACT, "activation", dict(out=s[4][:], in_=sink[:, h:h + 1], func=AF.Exp, bias=s[2][:]),
                     reads=[bs[2], b_p], writes=[bs[4]])
                P.op(DVE, "tensor_tensor", dict(out=s[5][:], in0=s[3][:], in1=s[4][:], op=ALU.add),
                     reads=[bs[3], bs[4]], writes=[bs[5]])
                P.op(DVE, "reciprocal", dict(out=s[6][:], in_=s[5][:]), reads=[bs[5]], writes=[bs[6]])
                pa2, pb2 = bank()
                pT = pa2.bitcast(BF16)[:, :384].rearrange("p (a b) -> p a b", b=128)
                for j in range(3):
                    P.op(PE, "transpose", dict(out=pT[:, j, :], in_=Pb[r][:, j * 128:(j + 1) * 128], identity=ident[:]),
                         reads=[b_Pb[r], b_c], writes=[pb2], sig=(j == 2))
                P.op(ACT, "copy", dict(out=PTs[r][:], in_=pT), reads=[pb2], writes=[b_PTs[r]])
                pa3, pb3 = bank()
                mm_acc(P, pa3[:, :128], pb3,
                       [(PTs[r][:, j, :], vsx[:, i + j, kvh * 128:(kvh + 1) * 128]) for j in range(3)],
                       [b_PTs[r], b_swa_in])
                P.op(DVE, "tensor_scalar", dict(out=ot[:, h * 128:(h + 1) * 128], in0=pa3[:, :128], scalar1=s[6][:],
                                                scalar2=None, op0=ALU.mult), reads=[pb3, bs[6]], writes=[b_ot])
            q4 = [st2[q] for q in range(4)]
            bq = [b_st2[q] for q in range(4)]
            P.op(ACT, "activation", dict(out=junk[:, :1024], in_=ot[:], func=AF.Square, accum_out=q4[0][:]),
                 reads=[b_ot], writes=[b_junk, bq[0]])
            P.op(DVE, "tensor_scalar", dict(out=q4[1][:], in0=q4[0][:], scalar1=1.0 / 1024, scalar2=RMS_EPS,
                                            op0=ALU.mult, op1=ALU.add), reads=[bq[0]], writes=[bq[1]])
            P.op(ACT, "activation", dict(out=q4[2][:], in_=q4[1][:], func=AF.Sqrt), reads=[bq[1]], writes=[bq[2]])
            P.op(DVE, "reciprocal", dict(out=q4[3][:], in_=q4[2][:]), reads=[bq[2]], writes=[bq[3]])
            ob, b_ob = obf[i % 2], b_obf[i % 2]
            P.op(DVE, "scalar_tensor_tensor", dict(out=ob[:], in0=ot[:], scalar=q4[3][:], in1=gswa[:],
                                                   op0=ALU.mult, op1=ALU.mult), reads=[b_ot, bq[3], b_swa_in], writes=[b_ob])
            pa4, pb4 = bank()
            pT = pa4.bitcast(BF16).rearrange("p (a b) -> p a b", b=128)
            for c in range(8):
                P.op(PE, "transpose", dict(out=pT[:, c, :], in_=ob[:, c * 128:(c + 1) * 128], identity=ident[:]),
                     reads=[b_ob, b_c], writes=[pb4], sig=(c == 7))
            P.op(ACT, "copy", dict(out=mT[:, 8:16, i * 128:(i + 1) * 128], in_=pT), reads=[pb4], writes=[b_mT[i]])

        P.barrier()
        sb.reset(m_mT)
        QP = 1024
        qn = sb.alloc("qn", [128, 8, QP], BF16)
        qpe = sb.alloc("qpe", [64, 8, QP], BF16)
        b_q = Buf("q")
        NK = 3
        Kc = [sb.alloc(f"Kc{r}", [128, T], BF16) for r in range(NK)]
        Vc = [sb.alloc(f"Vc{r}", [128, 16, 128], BF16) for r in range(NK)]
        Pc = [sb.alloc(f"Pc{r}", [64, T], BF16) for r in range(NK)]
        b_kv = [Buf(f"kvc{r}") for r in range(NK)]
        NP = 4
        PT = [sb.alloc(f"PT{r}", [128, 512], BF16) for r in range(NP)]
        b_PT = [Buf(f"PT{r}") for r in range(NP)]
        oT = sb.alloc("oT", [128, 8, QP], F32)
        b_oT = Buf("oT")
        rs = [sb.alloc(f"rs{r}", [128, 512], F32) for r in range(2)]
        b_rs = [Buf(f"rs{r}") for r in range(2)]
        sq = sb.alloc("sqm", [128, 8, 512], BF16)
        b_sq = Buf("sqm")
        rstd = sb.alloc("rstdm", [128, 512], F32)
        b_rstd = Buf("rstdm")
        scale_m = 192 ** -0.5
        prange[0], prange[1] = 4, 8
        pctr[0] = 0
        cctr = 0
        pctr_pt = 0
        ones32 = sb.alloc("ones32", [128, 128], F32)
        b_o32 = Buf("ones32")
        P.op(POOL, "memset", dict(ap=ones32[:], constant=1.0), writes=[b_o32])
        accs = [sb.alloc(f"accs{q}", [128, 512], F32) for q in range(2)]
        b_acc = [Buf(f"accs{q}") for q in range(2)]
        iters = [(qp, h, rk, kt, qb) for qp in range(2) for h in range(8) for rk in range(8)
                 for kt in range(16) for qb in range(2)]
        NIT = len(iters)
        SKEW = 2
        st_ = {}
        chunk_slot = {}
        ctrs = {"c": 0, "pt": 0}

        def emit_S(j):
            qp, h, rk, kt, qb = iters[j]
            q0 = qp * QP
            if h == 0 and rk == 0 and kt == 0 and qb == 0:
                P.dma(SP, qn[:], qn_d.rearrange("(h p) t -> p h t", p=128)[:, :, q0:q0 + QP], writes=[b_q])
                P.dma(SP, qpe[:], qpe_d.rearrange("(h p) t -> p h t", p=64)[:, :, q0:q0 + QP], writes=[b_q])
            if kt == 0 and qb == 0:
                r = ctrs["c"] % NK
                ctrs["c"] += 1
                chunk_slot[(qp, h, rk)] = r
                base = rk * KVROWS
                P.dma(SP, Kc[r][:], kv_all[base + 2 * h * 128:base + (2 * h + 1) * 128, :], writes=[b_kv[r]])
                P.dma(SP, Vc[r][:], kv_all[base + (2 * h + 1) * 128:base + (2 * h + 2) * 128, :]
                      .rearrange("p (i d) -> p i d", d=128), writes=[b_kv[r]])
                P.dma(SP, Pc[r][:], kv_all[base + 2048:base + 2048 + 64, :], writes=[b_kv[r]])
            r = chunk_slot[(qp, h, rk)]
            qsl = slice(qb * 512, (qb + 1) * 512)
            pa, pb = bank()
            P.op(PE, "matmul", dict(out=pa, lhsT=Kc[r][:, kt * 128:(kt + 1) * 128], rhs=qn[:, h, qsl],
                                    start=True, stop=False), reads=[b_kv[r], b_q], writes=[pb], sig=False)
            P.op(PE, "matmul", dict(out=pa, lhsT=Pc[r][:, kt * 128:(kt + 1) * 128], rhs=qpe[:, h, qsl],
                                    start=False, stop=True), reads=[b_kv[r], b_q], writes=[pb])
            pr = ctrs["pt"] % NP
            ctrs["pt"] += 1
            P.op(ACT, "activation", dict(out=PT[pr][:], in_=pa, func=AF.Exp, scale=scale_m),
                 reads=[pb], writes=[b_PT[pr]])
            st_[j] = (r, pr)

        def emit_PV(j):
            qp, h, rk, kt, qb = iters[j]
            q0 = qp * QP
            r, pr = st_.pop(j)
            first = (rk == 0 and kt == 0)
            last = (rk == 7 and kt == 15)
            qsl = slice(qb * 512, (qb + 1) * 512)
            P.op(PE, "matmul", dict(out=ps[:, qb, :], lhsT=Vc[r][:, kt, :], rhs=PT[pr][:], start=first, stop=last),
                 reads=[b_PT[pr], b_kv[r]], writes=[psb[qb]])
            if first:
                P.op(DVE, "tensor_copy", dict(out=accs[qb][:], in_=PT[pr][:]), reads=[b_PT[pr]], writes=[b_acc[qb]])
            else:
                P.op(DVE, "tensor_tensor", dict(out=accs[qb][:], in0=accs[qb][:], in1=PT[pr][:], op=ALU.add),
                     reads=[b_PT[pr], b_acc[qb]], writes=[b_acc[qb]])
            if not last:
                return
            P.op(PE, "matmul", dict(out=ps[:, 2 + qb, :], lhsT=ones32[:], rhs=accs[qb][:], start=True, stop=True),
                 reads=[b_acc[qb], b_o32], writes=[psb[2 + qb]])
            P.op(DVE, "reciprocal", dict(out=rs[qb][:], in_=ps[:, 2 + qb, :]), reads=[psb[2 + qb]], writes=[b_rs[qb]])
            P.op(DVE, "tensor_tensor", dict(out=oT[:, h, qsl], in0=ps[:, qb, :], in1=rs[qb][:], op=ALU.mult),
                 reads=[psb[qb], b_rs[qb]], writes=[b_oT])
            if not (h == 7 and qb == 1):
                return
            for qb2 in range(2):
                qs2 = slice(qb2 * 512, (qb2 + 1) * 512)
                for h2 in range(8):
                    P.op(ACT, "activation", dict(out=sq[:, h2, :], in_=oT[:, h2, qs2], func=AF.Square),
                         reads=[b_oT], writes=[b_sq])
                sa, sbf = bank()
                mm_acc(P, sa, sbf, [(ones[:], sq[:, h2, :]) for h2 in range(8)], [b_sq, b_c])
                P.op(DVE, "tensor_scalar", dict(out=rstd[:], in0=sa, scalar1=1.0 / 1024, scalar2=RMS_EPS,
                                                op0=ALU.mult, op1=ALU.add), reads=[sbf], writes=[b_rstd])
                P.op(ACT, "activation", dict(out=rstd[:], in_=rstd[:], func=AF.Sqrt), reads=[b_rstd], writes=[b_rstd])
                P.op(DVE, "reciprocal", dict(out=rstd[:], in_=rstd[:]), reads=[b_rstd], writes=[b_rstd])
                tiles = [b_mT[(q0 + qb2 * 512) // 128 + q] for q in range(4)]
                for h2 in range(8):
                    P.op(DVE, "scalar_tensor_tensor", dict(
                        out=mT[:, h2, q0 + qb2 * 512:q0 + (qb2 + 1) * 512], in0=oT[:, h2, qs2], scalar=gmla[:, h2:h2 + 1],
                        in1=rstd[:], op0=ALU.mult, op1=ALU.mult), reads=[b_oT, b_rstd, b_p], writes=tiles)

        for j in range(NIT + SKEW):
            if j < NIT:
                emit_S(j)
            if j >= SKEW:
                emit_PV(j - SKEW)
        prange[0], prange[1] = 0, 8

        P.barrier()
        sb.reset(m_mT)
        wout = sb.alloc("wout", [128, 16, D], BF16)
        b_wout = Buf("wout")
        wv = wout_d.rearrange("(k p) c -> p k c", p=128)
        for cb in range(4):
            P.dma(POOL, wout[:, :, cb * 512:(cb + 1) * 512], wv[:, :, cb * 512:(cb + 1) * 512], writes=[b_wout])
        lnp = sb.alloc("lnpF", [128, 2, D], F32)
        b_lnp = Buf("lnpF")
        P.dma(SP, lnp[:], ln_d[:, 0:2, :], writes=[b_lnp])
        xt = [sb.alloc("xt0", [128, D], F32)] * 2
        b_xt = [Buf("xt0")] * 2
        yp = [sb.alloc(f"yp{r}", [128, D], F32) for r in range(2)]
        b_yp = [Buf(f"yp{r}") for r in range(2)]
        xo = yp
        b_xo = b_yp
        xb = [sb.alloc(f"xb{r}", [128, D], BF16) for r in range(2)]
        b_xb = [Buf(f"xb{r}") for r in range(2)]
        xTs = [sb.alloc(f"xTs{r}", [128, 16, 128], BF16) for r in range(2)]
        b_xTs = [Buf(f"xTs{r}") for r in range(2)]
        if moe:
            xlb = [sb.alloc("xlb0", [128, D], BF16)] * 2
            b_xlb = [Buf("xlb0")] * 2
            xlTs = [sb.alloc("xlTs0", [128, 16, 128], BF16)] * 2
            b_xlTs = [Buf("xlTs0")] * 2
            gs, b_gs = small_pool("gs", 12, 8)
        junk = sb.alloc("junkF", [128, D], BF16)
        b_junk = Buf("junkF")
        ls, b_ls = small_pool("ls", 16)

        for i in range(16):
            r = i % 2
            P.dma(SP, xt[r][:], x[i * 128:(i + 1) * 128, :], writes=[b_xt[r]])
            for cb in range(4):
                pa, pb = bank()
                mm_acc(P, pa, pb, [(mT[:, k, i * 128:(i + 1) * 128], wout[:, k, cb * 512:(cb + 1) * 512])
                                   for k in range(16)], [b_mT[i], b_wout])
                P.op(DVE, "scalar_tensor_tensor", dict(out=yp[r][:, cb * 512:(cb + 1) * 512],
                                                       in0=xt[r][:, cb * 512:(cb + 1) * 512], scalar=ALPHA, in1=pa,
                                                       op0=ALU.mult, op1=ALU.add),
                     reads=[pb, b_xt[r]], writes=[b_yp[r]])
            layer_norm(yp[r][:], b_yp[r], 0, xo[r][:], b_xo[r], r)
            P.dma(SP, x1s[i * 128:(i + 1) * 128, :], xo[r][:], reads=[b_xo[r]], writes=[b_x1s])
            P.op(ACT, "copy", dict(out=xb[r][:], in_=xo[r][:]), reads=[b_xo[r]], writes=[b_xb[r]])
            j = (pctr[0] // 2) % 4
            pctr[0] += 2
            pT = ps[:, 2 * j:2 * j + 2, :].bitcast(BF16).rearrange("p a (b c) -> p (a b) c", c=128)
            pbufs = [psb[2 * j], psb[2 * j + 1]]
            for k in range(16):
                P.op(PE, "transpose", dict(out=pT[:, k, :], in_=xb[r][:, k * 128:(k + 1) * 128], identity=ident[:]),
                     reads=[b_xb[r], b_c], writes=pbufs, sig=(k == 15))
            P.op(DVE, "tensor_copy", dict(out=xTs[r][:], in_=pT), reads=pbufs, writes=[b_xTs[r]])
            P.dma(SP, x1T_d[:, :, i * 128:(i + 1) * 128], xTs[r][:], reads=[b_xTs[r]], writes=[b_x1T])
            if moe:
                P.op(DVE, "tensor_tensor", dict(out=xlb[r][:], in0=xo[r][:], in1=xb[r][:], op=ALU.subtract),
                     reads=[b_xo[r], b_xb[r]], writes=[b_xlb[r]])
                j = (pctr[0] // 2) % 4
                pctr[0] += 2
                pT2 = ps[:, 2 * j:2 * j + 2, :].bitcast(BF16).rearrange("p a (b c) -> p (a b) c", c=128)
                pbufs2 = [psb[2 * j], psb[2 * j + 1]]
                for k in range(16):
                    P.op(PE, "transpose", dict(out=pT2[:, k, :], in_=xlb[r][:, k * 128:(k + 1) * 128], identity=ident[:]),
                         reads=[b_xlb[r], b_c], writes=pbufs2, sig=(k == 15))
                P.op(ACT, "copy", dict(out=xlTs[r][:], in_=pT2), reads=pbufs2, writes=[b_xlTs[r]])
                pl, pbl = bank()
                pairs = []
                for k in range(16):
                    pairs += [(xTs[r][:, k, :], wrh[:, k, :]), (xTs[r][:, k, :], wrl[:, k, :]), (xlTs[r][:, k, :], wrh[:, k, :])]
                mm_acc(P, pl[:, :8], pbl, pairs, [b_xTs[r], b_xlTs[r], b_wr])
                g = [gs[q] for q in range(12)]
                bg = [b_gs[q] for q in range(12)]
                P.op(DVE, "tensor_copy", dict(out=g[0][:], in_=pl[:, :8]), reads=[pbl], writes=[bg[0]])
                P.op(DVE, "tensor_reduce", dict(out=g[1][:, 0:1], in_=g[0][:], axis=AX.X, op=ALU.max), reads=[bg[0]], writes=[bg[1]])
                P.op(DVE, "tensor_scalar", dict(out=g[2][:], in0=g[0][:], scalar1=g[1][:, 0:1], scalar2=None, op0=ALU.is_equal),
                     reads=[bg[0], bg[1]], writes=[bg[2]])
                P.op(DVE, "scalar_tensor_tensor", dict(out=g[3][:], in0=g[2][:], scalar=NEG, in1=g[0][:], op0=ALU.mult, op1=ALU.add),
                     reads=[bg[2], bg[0]], writes=[bg[3]])
                P.op(DVE, "tensor_reduce", dict(out=g[4][:, 0:1], in_=g[3][:], axis=AX.X, op=ALU.max), reads=[bg[3]], writes=[bg[4]])
                P.op(DVE, "tensor_scalar", dict(out=g[5][:], in0=g[3][:], scalar1=g[4][:, 0:1], scalar2=None, op0=ALU.is_equal),
                     reads=[bg[3], bg[4]], writes=[bg[5]])
                P.op(DVE, "tensor_tensor", dict(out=g[6][:, 0:1], in0=g[4][:, 0:1], in1=g[1][:, 0:1], op=ALU.subtract),
                     reads=[bg[4], bg[1]], writes=[bg[6]])
                P.op(ACT, "activation", dict(out=g[7][:, 0:1], in_=g[6][:, 0:1], func=AF.Exp), reads=[bg[6]], writes=[bg[7]])
                P.op(DVE, "tensor_scalar", dict(out=g[8][:, 0:1], in0=g[7][:, 0:1], scalar1=1.0, scalar2=None, op0=ALU.add),
                     reads=[bg[7]], writes=[bg[8]])
                P.op(DVE, "reciprocal", dict(out=g[9][:, 0:1], in_=g[8][:, 0:1]), reads=[bg[8]], writes=[bg[9]])
                P.op(DVE, "tensor_tensor", dict(out=g[10][:, 0:1], in0=g[7][:, 0:1], in1=g[9][:, 0:1], op=ALU.mult),
                     reads=[bg[7], bg[9]], writes=[bg[10]])
                P.op(DVE, "tensor_scalar", dict(out=g[11][:], in0=g[5][:], scalar1=g[10][:, 0:1], scalar2=None, op0=ALU.mult),
                     reads=[bg[5], bg[10]], writes=[bg[11]])
                P.op(DVE, "scalar_tensor_tensor", dict(out=gates[:, i, :], in0=g[2][:], scalar=g[9][:, 0:1], in1=g[11][:],
                                                       op0=ALU.mult, op1=ALU.add),
                     reads=[bg[2], bg[9], bg[11]], writes=[b_gates])
                P.op(DVE, "tensor_tensor", dict(out=selm[:, i * 8:(i + 1) * 8], in0=g[2][:], in1=g[5][:], op=ALU.add),
                     reads=[bg[2], bg[5]], writes=[b_selm])
                P.dma(SP, xb_d[i * 128:(i + 1) * 128, :], xb[r][:], reads=[b_xb[r]], writes=[b_xbd])

    if stage == "a":
        P.barrier()
        sb.reset(m_const)
        Lm = sb.alloc("Lm", [128, 128], BF16)
        b_cm = Buf("moe_consts")
        P.dma(SP, Lm[:], lmat_d[:, :], writes=[b_cm])
        selb = sb.alloc("selb", [128, 128], BF16)
        b_selb = Buf("selb")
        rk = [sb.alloc(f"rk{q}", [128, 128], F32) for q in range(4)]
        b_rk = [Buf(f"rk{q}") for q in range(4)]
        cnt = sb.alloc("cnt", [128, 8], F32)
        b_cnt = Buf("cnt")
        P.op(ACT, "copy", dict(out=selb[:], in_=selm[:]), reads=[b_selm], writes=[b_selb])
        pw, pbw = bank()
        P.op(PE, "matmul", dict(out=pw[:, :128], lhsT=Lm[:], rhs=selb[:], start=True, stop=True),
             reads=[b_selb, b_cm], writes=[pbw])
        pt_, pbt = bank()
        P.op(PE, "matmul", dict(out=pt_[:, :128], lhsT=ones[:], rhs=selb[:], start=True, stop=True),
             reads=[b_selb, b_c], writes=[pbt])
        P.op(DVE, "tensor_copy", dict(out=rk[1][:], in_=pt_[:, :128]), reads=[pbt], writes=[b_rk[1]])
        P.op(DVE, "memset", dict(ap=rk[2][:, 0:8], constant=0.0), writes=[b_rk[2]])
        for i in range(1, 16):
            P.op(DVE, "tensor_tensor", dict(out=rk[2][:, i * 8:(i + 1) * 8], in0=rk[2][:, (i - 1) * 8:i * 8],
                                            in1=rk[1][:, (i - 1) * 8:i * 8], op=ALU.add),
                 reads=[b_rk[1], b_rk[2]], writes=[b_rk[2]])
        P.op(DVE, "tensor_tensor", dict(out=cnt[:], in0=rk[2][:, 120:128], in1=rk[1][:, 120:128], op=ALU.add),
             reads=[b_rk[1], b_rk[2]], writes=[b_cnt])
        P.op(DVE, "tensor_tensor", dict(out=rk[0][:], in0=pw[:, :128], in1=rk[2][:], op=ALU.add),
             reads=[pbw, b_rk[2]], writes=[b_rk[0]])
        P.op(DVE, "scalar_tensor_tensor", dict(out=rk[3][:], in0=rk[0][:], scalar=1.0, in1=selm[:],
                                               op0=ALU.add, op1=ALU.mult), reads=[b_rk[0], b_selm], writes=[b_rk[3]])
        P.op(DVE, "tensor_scalar", dict(out=rk[3][:], in0=rk[3][:], scalar1=-1.0, scalar2=None, op0=ALU.add),
             reads=[b_rk[3]], writes=[b_rk[3]])
        P.dma(SP, gates_o[:, :], gates[:].rearrange("p a b -> p (a b)"), reads=[b_gates], writes=[b_out])
        P.dma(SP, rank_o[:, :], rk[3][:], reads=[b_rk[3]], writes=[b_out])
        P.dma(SP, cnt_o[:, :], cnt[:], reads=[b_cnt], writes=[b_out])
        P.final_wait(SP, [b_out, b_x1s, b_xbd])
        with nc.Block() as block:
            P.emit(block)
        return nc

    P.barrier()
    sb.reset(m_const)
    if stage == "all":
        HT = 1024
        x1T = sb.alloc("x1T", [128, 16, HT], BF16)
        b_x1Th = Buf("x1Th")
        yacc = sb.alloc("yacc", [128, HT // 128, D], F32)
        b_yacc = [Buf(f"yacc{t}") for t in range(HT // 128)]
        GF = 256
        NW = 2
        wgb = [sb.alloc(f"wgb{r}", [128, 16, GF], BF16) for r in range(NW)]
        wub = [sb.alloc(f"wub{r}", [128, 16, GF], BF16) for r in range(NW)]
        wdb = [sb.alloc(f"wdb{r}", [128, GF // 128, D], BF16) for r in range(NW)]
        b_w = [Buf(f"wffn{r}") for r in range(NW)]
        sg = [sb.alloc(f"sg{r}", [128, 512], F32) for r in range(2)]
        b_sg = [Buf(f"sg{r}") for r in range(2)]
        hT = [sb.alloc(f"hT{r}", [128, GF // 128, 512], BF16) for r in range(2)]
        b_hT = [Buf(f"hT{r}") for r in range(2)]
        xt = [sb.alloc(f"xtG{r}", [128, D], F32) for r in range(2)]
        b_xt = [Buf(f"xtG{r}") for r in range(2)]
        lnp = sb.alloc("lnpG", [128, 2, D], F32)
        b_lnp = Buf("lnpG")
        P.dma(SP, lnp[:], ln_d[:, 2:4, :], writes=[b_lnp])
        junk = sb.alloc("junkG", [128, D], BF16)
        b_junk = Buf("junkG")
        ls, b_ls = small_pool("lsG", 16)
        n_exp = 8 if moe else 1
        ngrp = F // GF
        wctr = 0
        sgc = 0
        hc = 0
        for half in range(2):
            t0 = half * HT
            P.dma(SP, x1T[:], x1T_d[:, :, t0:t0 + HT], reads=[b_x1T], writes=[b_x1Th])
            for e in range(n_exp):
                for g in range(ngrp):
                    r = wctr % NW
                    wctr += 1
                    f0 = g * GF
                    P.dma(POOL, wgb[r][:], wg_d[e].rearrange("(k p) f -> p k f", p=128)[:, :, f0:f0 + GF], writes=[b_w[r]])
                    P.dma(POOL, wub[r][:], wu_d[e].rearrange("(k p) f -> p k f", p=128)[:, :, f0:f0 + GF], writes=[b_w[r]])
                    P.dma(POOL, wdb[r][:], wd_d[e, f0:f0 + GF, :].rearrange("(c p) d -> p c d", p=128), writes=[b_w[r]])
                    for tb in range(HT // 512):
                        tsl = slice(tb * 512, (tb + 1) * 512)
                        hr = hc % 2
                        hc += 1
                        for c in range(GF // 128):
                            pg, pbg = bank()
                            mm_acc(P, pg, pbg, [(wgb[r][:, k, c * 128:(c + 1) * 128], x1T[:, k, tsl]) for k in range(16)],
                                   [b_w[r], b_x1Th])
                            pu, pbu = bank()
                            mm_acc(P, pu, pbu, [(wub[r][:, k, c * 128:(c + 1) * 128], x1T[:, k, tsl]) for k in range(16)],
                                   [b_w[r], b_x1Th])
                            sr = sgc % 2
                            sgc += 1
                            P.op(ACT, "activation", dict(out=sg[sr][:], in_=pg, func=AF.Silu), reads=[pbg], writes=[b_sg[sr]])
                            P.op(DVE, "tensor_tensor", dict(out=hT[hr][:, c, :], in0=pu, in1=sg[sr][:], op=ALU.mult),
                                 reads=[pbu, b_sg[sr]], writes=[b_hT[hr]])
                        for t in range(4):
                            tt = tb * 4 + t
                            for cb in range(4):
                                pa, pb = bank()
                                mm_acc(P, pa, pb, [(hT[hr][:, c, t * 128:(t + 1) * 128], wdb[r][:, c, cb * 512:(cb + 1) * 512])
                                                   for c in range(GF // 128)], [b_hT[hr], b_w[r]])
                                dst = yacc[:, tt, cb * 512:(cb + 1) * 512]
                                gsc = gates[:, half * (HT // 128) + tt, e:e + 1]
                                if e == 0 and g == 0:
                                    if moe:
                                        P.op(DVE, "tensor_scalar", dict(out=dst, in0=pa, scalar1=gsc, scalar2=None, op0=ALU.mult),
                                             reads=[pb, b_gates], writes=[b_yacc[tt]])
                                    else:
                                        P.op(DVE, "tensor_copy", dict(out=dst, in_=pa), reads=[pb], writes=[b_yacc[tt]])
                                elif moe:
                                    P.op(DVE, "scalar_tensor_tensor", dict(out=dst, in0=pa, scalar=gsc, in1=dst,
                                                                           op0=ALU.mult, op1=ALU.add),
                                         reads=[pb, b_yacc[tt], b_gates], writes=[b_yacc[tt]])
                                else:
                                    P.op(DVE, "tensor_tensor", dict(out=dst, in0=pa, in1=dst, op=ALU.add),
                                         reads=[pb, b_yacc[tt]], writes=[b_yacc[tt]])
            for tt in range(HT // 128):
                r = tt % 2
                row0 = t0 + tt * 128
                P.dma(SP, xt[r][:], x1s[row0:row0 + 128, :], reads=[b_x1s], writes=[b_xt[r]])
                P.op(DVE, "scalar_tensor_tensor", dict(out=yacc[:, tt, :], in0=xt[r][:], scalar=ALPHA, in1=yacc[:, tt, :],
                                                       op0=ALU.mult, op1=ALU.add),
                     reads=[b_xt[r], b_yacc[tt]], writes=[b_yacc[tt]])
                layer_norm(yacc[:, tt, :], b_yacc[tt], 2, xt[r][:], b_xt[r], r)
                P.dma(SP, y_out[row0:row0 + 128, :], xt[r][:], reads=[b_xt[r]], writes=[b_out])

    if stage == "b":
        CAPMAX = MOE_CAPMAX
        GF = 256
        NW = 2
        ngrp = F // GF
        ls, b_ls = small_pool("lsG", 16)
        iot = sb.alloc("iot", [128, CAPMAX], F32)
        iot2 = sb.alloc("iot2", [128, CAPMAX], F32)
        b_cm = Buf("moe_consts")
        b_iot2 = Buf("iot2")
        P.dma(SP, iot[:], iota_d[:, :], writes=[b_cm])
        rankm = sb.alloc("rankm", [128, 128], F32)
        b_rankm = Buf("rankm")
        P.dma(SP, rankm[:], rank_i[:, :], writes=[b_rankm])
        P.dma(SP, gates[:].rearrange("p a b -> p (a b)"), gates_i[:, :], writes=[b_gates])
        NSL = 3
        Sel = [sb.alloc(f"Sel{r}", [128, CAPMAX], BF16) for r in range(NSL)]
        b_Sel = [Buf(f"Sel{r}") for r in range(NSL)]
        xeT = sb.alloc("xeT", [128, 16 * CAPMAX], BF16)
        b_xeT = Buf("xeT")
        yeacc = sb.alloc("yeacc", [128, CAPMAX // 128, D], F32)
        b_ye = [Buf(f"ye{q}") for q in range(CAPMAX // 128)]
        m_w = sb.mark()
        lnp = sb.alloc("lnpG", [128, 2, D], F32)
        junk = sb.alloc("junkG", [128, D], BF16)
        b_lnp = Buf("lnpG")
        b_junk = Buf("junkG")
        sb.reset(m_w)
        wgb = [sb.alloc(f"wgb{r}", [128, 16, GF], BF16) for r in range(NW)]
        wub = [sb.alloc(f"wub{r}", [128, 16, GF], BF16) for r in range(NW)]
        wdb = [sb.alloc(f"wdb{r}", [128, GF // 128, D], BF16) for r in range(NW)]
        b_w = [Buf(f"wffn{r}") for r in range(NW)]
        NX = 2
        xtok = [sb.alloc(f"xtok{r}", [128, D], BF16) for r in range(NX)]
        b_xtok = [Buf(f"xtok{r}") for r in range(NX)]
        sg = [sb.alloc(f"sg{r}", [128, 512], F32) for r in range(2)]
        b_sg = [Buf(f"sg{r}") for r in range(2)]
        hT = [sb.alloc(f"hT{r}", [128, GF // 128, CAPMAX], BF16) for r in range(2)]
        b_hT = [Buf(f"hT{r}") for r in range(2)]
        selT = [sb.alloc(f"selT{r}", [128, CAPMAX // 128, 128], BF16) for r in range(2)]
        b_selT = [Buf(f"selT{r}") for r in range(2)]
        ybuf = [sb.alloc(f"ybuf{r}", [128, D], F32) for r in range(2)]
        b_ybuf = [Buf(f"ybuf{r}") for r in range(2)]
        b_yd = [Buf(f"yacc_d{i}") for i in range(16)]
        ls_x = sb.alloc("lsx0", [128, D], F32)
        b_lsx = Buf("lsx0")
        segs = []
        for e in range(8):
            c0 = 0
            while c0 < caps[e]:
                segs.append((e, c0, min(CAPMAX, caps[e] - c0)))
                c0 += CAPMAX
        assert segs
        xctr = 0
        wctr = 0
        sgc = 0
        hc = 0
        yc = 0
        selctr = [0]
        for si, (e, base, cp) in enumerate(segs):
            first_seg = (si == 0)
            last_seg = (si == len(segs) - 1)
            NS = cp // 128
            cblk = [(0, min(cp, 512))] + ([(512, cp)] if cp > 512 else [])
            xe = xeT[:, 0:16 * cp].rearrange("p (k c) -> p k c", c=cp)
            ye_bf = xeT[:, 0:NS * D].rearrange("p (s d) -> p s d", d=D)
            P.op(DVE, "tensor_scalar", dict(out=iot2[:, :cp], in0=iot[:, :cp], scalar1=float(base), scalar2=None, op0=ALU.add),
                 reads=[b_cm], writes=[b_iot2])

            def build_sel(i):
                selctr[0] += 1
                r_ = selctr[0] % NSL
                P.op(DVE, "tensor_scalar", dict(out=Sel[r_][:, :cp], in0=iot2[:, :cp],
                                                scalar1=rankm[:, i * 8 + e:i * 8 + e + 1], scalar2=None, op0=ALU.is_equal),
                     reads=[b_iot2, b_rankm], writes=[b_Sel[r_]])
                return Sel[r_], b_Sel[r_]
            nb = len(cblk)
            kgsz = 8 // nb
            for kg in range(16 // kgsz):
                for i in range(16):
                    xr = xctr % NX
                    xctr += 1
                    P.dma(SP, xtok[xr][:], xb_d[i * 128:(i + 1) * 128, :], reads=[b_xbd], writes=[b_xtok[xr]])
                    S_, bS_ = build_sel(i)
                    for kk in range(kgsz):
                        k = kgsz * kg + kk
                        for bi_, (a0, a1) in enumerate(cblk):
                            bi = nb * kk + bi_
                            lastmm = (kk == kgsz - 1 and bi_ == nb - 1)
                            P.op(PE, "matmul", dict(out=ps[:, bi, :a1 - a0], lhsT=xtok[xr][:, k * 128:(k + 1) * 128],
                                                    rhs=S_[:, a0:a1], start=(i == 0), stop=(i == 15)),
                                 reads=[b_xtok[xr], bS_], writes=[psb[bi]], sig=(i == 15 or lastmm))
                for kk in range(kgsz):
                    k = kgsz * kg + kk
                    for bi_, (a0, a1) in enumerate(cblk):
                        bi = nb * kk + bi_
                        if bi % 2 == 0:
                            P.op(ACT, "copy", dict(out=xe[:, k, a0:a1], in_=ps[:, bi, :a1 - a0]),
                                 reads=[psb[bi]], writes=[b_xeT])
                        else:
                            P.op(DVE, "tensor_copy", dict(out=xe[:, k, a0:a1], in_=ps[:, bi, :a1 - a0]),
                                 reads=[psb[bi]], writes=[b_xeT])
            for g in range(ngrp):
                r = wctr % NW
                wctr += 1
                f0 = g * GF
                P.dma(POOL, wgb[r][:], wg_d[e].rearrange("(k p) f -> p k f", p=128)[:, :, f0:f0 + GF], writes=[b_w[r]])
                P.dma(POOL, wub[r][:], wu_d[e].rearrange("(k p) f -> p k f", p=128)[:, :, f0:f0 + GF], writes=[b_w[r]])
                P.dma(POOL, wdb[r][:], wd_d[e, f0:f0 + GF, :].rearrange("(c p) d -> p c d", p=128), writes=[b_w[r]])
                hr = hc % 2
                hc += 1
                for c in range(GF // 128):
                    for (a0, a1) in cblk:
                        w_ = a1 - a0
                        pg, pbg = bank()
                        mm_acc(P, pg[:, :w_], pbg, [(wgb[r][:, k, c * 128:(c + 1) * 128], xe[:, k, a0:a1]) for k in range(16)],
                               [b_w[r], b_xeT])
                        pu, pbu = bank()
                        mm_acc(P, pu[:, :w_], pbu, [(wub[r][:, k, c * 128:(c + 1) * 128], xe[:, k, a0:a1]) for k in range(16)],
                               [b_w[r], b_xeT])
                        sr = sgc % 2
                        sgc += 1
                        P.op(ACT, "activation", dict(out=sg[sr][:, :w_], in_=pg[:, :w_], func=AF.Silu), reads=[pbg], writes=[b_sg[sr]])
                        P.op(DVE, "tensor_tensor", dict(out=hT[hr][:, c, a0:a1], in0=pu[:, :w_], in1=sg[sr][:, :w_], op=ALU.mult),
                             reads=[pbu, b_sg[sr]], writes=[b_hT[hr]])
                for s_ in range(NS):
                    for cb in range(4):
                        pa, pb = bank()
                        mm_acc(P, pa, pb, [(hT[hr][:, c, s_ * 128:(s_ + 1) * 128], wdb[r][:, c, cb * 512:(cb + 1) * 512])
                                           for c in range(GF // 128)], [b_hT[hr], b_w[r]])
                        dst = yeacc[:, s_, cb * 512:(cb + 1) * 512]
                        if g == 0:
                            P.op(DVE, "tensor_copy", dict(out=dst, in_=pa), reads=[pb], writes=[b_ye[s_]])
                        else:
                            P.op(DVE, "tensor_tensor", dict(out=dst, in0=pa, in1=dst, op=ALU.add),
                                 reads=[pb, b_ye[s_]], writes=[b_ye[s_]])
            for s_ in range(NS):
                P.op(ACT, "copy", dict(out=ye_bf[:, s_, :], in_=yeacc[:, s_, :]), reads=[b_ye[s_]], writes=[b_xeT])
            if last_seg:
                P.dma(SP, lnp[:], ln_d[:, 2:4, :], writes=[b_lnp, b_w[0], b_w[1], b_junk])
            for i in range(16):
                sr = i % 2
                S_, bS_ = build_sel(i)
                pq, pbq = bank()
                pT = pq.bitcast(BF16)[:, :NS * 128].rearrange("p (a b) -> p a b", b=128)
                for c in range(NS):
                    P.op(PE, "transpose", dict(out=pT[:, c, :], in_=S_[:, c * 128:(c + 1) * 128], identity=ident[:]),
                         reads=[bS_, b_c], writes=[pbq], sig=(c == NS - 1))
                P.op(ACT, "copy", dict(out=selT[sr][:, :NS, :], in_=pT), reads=[pbq], writes=[b_selT[sr]])
                yr = yc % 2
                yc += 1
                if not first_seg:
                    P.dma(SP, ybuf[yr][:], yacc_d[i * 128:(i + 1) * 128, :], reads=[b_yd[i]], writes=[b_ybuf[yr]])
                for cb in range(4):
                    pa, pb = bank()
                    mm_acc(P, pa, pb, [(selT[sr][:, c, :], ye_bf[:, c, cb * 512:(cb + 1) * 512]) for c in range(NS)],
                           [b_selT[sr], b_xeT])
                    dst = ybuf[yr][:, cb * 512:(cb + 1) * 512]
                    gsc = gates[:, i, e:e + 1]
                    if first_seg:
                        P.op(DVE, "tensor_scalar", dict(out=dst, in0=pa, scalar1=gsc, scalar2=None, op0=ALU.mult),
                             reads=[pb, b_gates], writes=[b_ybuf[yr]])
                    else:
                        P.op(DVE, "scalar_tensor_tensor", dict(out=dst, in0=pa, scalar=gsc, in1=dst,
                                                               op0=ALU.mult, op1=ALU.add),
                             reads=[pb, b_gates, b_ybuf[yr]], writes=[b_ybuf[yr]])
                if not last_seg:
                    P.dma(SP, yacc_d[i * 128:(i + 1) * 128, :], ybuf[yr][:], reads=[b_ybuf[yr]], writes=[b_yd[i]])
                else:
                    P.dma(SP, ls_x[:], x1s[i * 128:(i + 1) * 128, :], reads=[b_x1s], writes=[b_lsx])
                    P.op(DVE, "scalar_tensor_tensor", dict(out=ybuf[yr][:], in0=ls_x[:], scalar=ALPHA, in1=ybuf[yr][:],
                                                           op0=ALU.mult, op1=ALU.add),
                         reads=[b_lsx, b_ybuf[yr]], writes=[b_ybuf[yr]])
                    layer_norm(ybuf[yr][:], b_ybuf[yr], 2, ybuf[yr][:], b_ybuf[yr], i % 2)
                    P.dma(SP, y_out[i * 128:(i + 1) * 128, :], ybuf[yr][:], reads=[b_ybuf[yr]], writes=[b_out])

    P.final_wait(SP, [b_out])
    with nc.Block() as block:
        P.emit(block)
    return nc


def kernel(**inputs):
    inp = {k: np.asarray(v) for k, v in inputs.items()}
    x = inp["x"][0]
    xs = [np.ascontiguousarray(x[c * T:(c + 1) * T]) for c in range(NCORE)]
    cores = list(range(NCORE))
    for l in range(2):
        ncA = build_A()
        resA = run_bass_kernel_spmd(ncA, inputs_A(xs, l, inp), core_ids=cores).results
        moe = (l % 2 == 1)
        if not moe:
            ncB = build_B(False, 5632)
            resB = run_bass_kernel_spmd(ncB, inputs_B(xs, l, inp, resA, False), core_ids=cores).results
        else:
            ncBa = build_B(True, 7168, "a")
            resBa = run_bass_kernel_spmd(ncBa, inputs_B(xs, l, inp, resA, True), core_ids=cores).results
            cnt = np.stack([np.asarray(resBa[c]["cnt_o"])[0] for c in range(NCORE)])
            caps = [int(-(-int(round(float(cnt[:, e].max()))) // 128) * 128) for e in range(8)]
            ncBb = build_B(True, 7168, "b", caps)
            resB = run_bass_kernel_spmd(ncBb, inputs_Bb(l, inp, resBa), core_ids=cores).results
        xs = [np.asarray(resB[c]["y"], dtype=np.float32) for c in range(NCORE)]
    return np.concatenate(xs, axis=0)[None].astype(np.float32)
```

```python
import numpy as np
import concourse.bass as bass
import concourse.mybir as mybir
from concourse.bass_utils import run_bass_kernel_spmd

F32 = mybir.dt.float32
BF16 = mybir.dt.bfloat16
AF = mybir.ActivationFunctionType
ALU = mybir.AluOpType
AX = mybir.AxisListType

PE, ACT, DVE, POOL, SP = "tensor", "scalar", "vector", "gpsimd", "sync"
ENGS = (PE, ACT, DVE, POOL, SP)
SEM_ROLL = 30000

T = 2048
D = 2048
HT = 1024
NC_IN = 3712
KVROWS = 16 * 128 + 64
ALPHA = 4 ** 0.25
LN_EPS = 1e-5
RMS_EPS = 1e-6
NEG = -1e30
NCORE = 8
S = 16384
DEBUG_COUNTS = False
MOE_CAPMAX = 896


class Buf:
    __slots__ = ("name", "w", "r", "dsem")

    def __init__(self, name=""):
        self.name = name
        self.w = None
        self.r = {}
        self.dsem = None


class Prog:
    def __init__(self, nc):
        self.nc = nc
        self.streams = {e: [] for e in ENGS}
        self.esems = {e: None for e in ENGS}
        self.waited = {}
        self.all_sems = []

    def _new_sem(self, name):
        h = self.nc.alloc_semaphore(name)
        rec = [h, 0]
        self.all_sems.append(rec)
        return rec

    def _eng_sem(self, eng):
        rec = self.esems[eng]
        if rec is None or rec[1] >= SEM_ROLL:
            rec = self._new_sem(f"e_{eng}_{len(self.all_sems)}")
            self.esems[eng] = rec
        return rec

    def _collect(self, eng, reads, writes, is_dma):
        waits = {}

        def need(tok, same_ok):
            if tok is None:
                return
            rec, val, src, dma = tok
            if dma:
                val = rec[1]
            elif src == eng and not is_dma:
                if eng == PE or same_ok:
                    return
            k = id(rec)
            if waits.get(k, (None, -1))[1] < val:
                waits[k] = (rec, val)

        for b in reads:
            need(b.w, False)
        for b in writes:
            need(b.w, True)
            for t in b.r.values():
                need(t, True)
        out = []
        for k, (rec, val) in waits.items():
            key = (eng, k)
            if self.waited.get(key, -1) >= val:
                continue
            self.waited[key] = val
            out.append((rec[0], val))
        return out

    def op(self, eng, name, args, reads=(), writes=(), sig=True, own_sem_inc=None):
        def fn(e, name=name, args=args):
            return getattr(e, name)(**args)
        waits = self._collect(eng, reads, writes, False)
        tok = None
        if own_sem_inc is not None:
            rec = self._new_sem(f"own_{len(self.all_sems)}")
            rec[1] += own_sem_inc
            tok = (rec, rec[1], eng, True)
            self.streams[eng].append((waits, fn, (rec[0], own_sem_inc)))
        elif sig:
            rec = self._eng_sem(eng)
            rec[1] += 1
            tok = (rec, rec[1], eng, False)
            self.streams[eng].append((waits, fn, (rec[0], 1)))
        else:
            self.streams[eng].append((waits, fn, None))
        if tok is not None:
            for b in writes:
                b.w = tok
                b.r = {}
            for b in reads:
                b.r[id(rec)] = tok
        return tok

    def dma(self, eng, out, in_, reads=(), writes=(), sembuf=None, **kw):
        waits = self._collect(eng, reads, writes, True)
        sb = sembuf if sembuf is not None else (writes[0] if writes else reads[0])
        if sb.dsem is None:
            sb.dsem = self._new_sem(f"d_{sb.name}_{len(self.all_sems)}")
        rec = sb.dsem
        rec[1] += 16
        tok = (rec, rec[1], eng, True)

        def fn(e, out=out, in_=in_, kw=kw):
            return e.dma_start(out=out, in_=in_, **kw)
        self.streams[eng].append((waits, fn, (rec[0], 16)))
        for b in writes:
            b.w = tok
            b.r = {}
        for b in reads:
            b.r[id(rec)] = tok
        return tok

    def barrier(self):
        for eng in ENGS:
            waits = []
            for rec in self.all_sems:
                if rec[1] > 0 and self.waited.get((eng, id(rec)), -1) < rec[1]:
                    self.waited[(eng, id(rec))] = rec[1]
                    waits.append((rec[0], rec[1]))
            if waits:
                self.streams[eng].append((waits, None, None))

    def final_wait(self, eng, bufs):
        waits = self._collect(eng, bufs, (), True)
        self.streams[eng].append((waits, None, None))

    def emit(self, block):
        nc = self.nc

        def make(eng):
            stream = self.streams[eng]

            def body(e):
                for waits, fn, inc in stream:
                    for h, v in waits:
                        e.wait_ge(h, v)
                    if fn is not None:
                        ins = fn(e)
                        if inc is not None:
                            ins.then_inc(inc[0], inc[1])
            return body
        block.tensor(make(PE))
        block.scalar(make(ACT))
        block.vector(make(DVE))
        block.gpsimd(make(POOL))
        block.sync(make(SP))


SB_BASE = 16512
SB_END = 229376
_DT_BYTES = {F32: 4, BF16: 2}


class SBAlloc:
    def __init__(self, nc):
        self.nc = nc
        self.off = SB_BASE
        self.n = 0

    def alloc(self, name, shape, dtype):
        size = 1
        for d in shape[1:]:
            size *= d
        size *= _DT_BYTES[dtype]
        size = (size + 31) // 32 * 32
        assert self.off + size <= SB_END, f"SBUF overflow at {name}: {self.off + size}"
        t = self.nc.alloc_sbuf_tensor_at(f"{name}_{self.n}", list(shape), dtype, offset=self.off)
        self.n += 1
        self.off += size
        return t

    def mark(self):
        return self.off

    def reset(self, m):
        self.off = m


def mm_acc(P, out_ap, out_buf, pairs, reads):
    n = len(pairs)
    for i, (l, r) in enumerate(pairs):
        P.op(PE, "matmul", dict(out=out_ap, lhsT=l, rhs=r, start=(i == 0), stop=(i == n - 1)),
             reads=reads, writes=[out_buf], sig=(i == n - 1))


import ml_dtypes

BF = ml_dtypes.bfloat16
ROPE_THETA = 10000.0


def rope_np(dim):
    inv = np.power(np.float32(ROPE_THETA), -(np.arange(0, dim, 2, dtype=np.float32) / np.float32(dim))).astype(np.float32)
    ang = np.arange(S, dtype=np.float32)[:, None] * inv[None, :]
    return np.cos(ang).astype(np.float32), np.sin(ang).astype(np.float32)


def rope_tables_fm(dim):
    c, s = rope_np(dim)
    cf = np.concatenate([c, c], axis=1).T
    sf = np.concatenate([-s, s], axis=1).T
    return np.ascontiguousarray(cf), np.ascontiguousarray(sf)


def perm_half(w, dim):
    n = w.shape[1] // dim
    idx = np.concatenate([(np.arange(dim) + dim // 2) % dim + b * dim for b in range(n)])
    return w[:, idx]


def prep_w_in(w):
    c_q = w[:, 0:512]
    c_kv = w[:, 512:768]
    k_pe = w[:, 768:832]
    q_s = w[:, 832:1856]
    k_s = w[:, 1856:2112]
    v_s = w[:, 2112:2368]
    cols = [c_q, c_kv, k_pe, perm_half(k_pe, 64)]
    qsp = perm_half(q_s, 128)
    for h in range(8):
        cols += [q_s[:, h * 128:(h + 1) * 128], qsp[:, h * 128:(h + 1) * 128]]
    ksp = perm_half(k_s, 128)
    for h in range(2):
        cols += [k_s[:, h * 128:(h + 1) * 128], ksp[:, h * 128:(h + 1) * 128]]
    cols.append(v_s)
    out = np.ascontiguousarray(np.concatenate(cols, axis=1))
    assert out.shape[1] == 3712
    return out


def prep_w_qb(w):
    cols = []
    for h in range(8):
        blk = w[:, h * 192:(h + 1) * 192]
        pe = blk[:, 128:192]
        cols += [blk[:, :128], pe, perm_half(pe, 64)]
    return np.ascontiguousarray(np.concatenate(cols, axis=1))


def prep_w_kvb(w):
    ks = [w[:, h * 256:h * 256 + 128] for h in range(8)]
    vs = [w[:, h * 256 + 128:(h + 1) * 256] for h in range(8)]
    return np.ascontiguousarray(np.concatenate(ks + vs, axis=1))


def fm_vec(g, nchunk):
    return np.ascontiguousarray(g.reshape(nchunk, 128).T)


_TABS = {}


def tables():
    if not _TABS:
        _TABS["m"] = rope_tables_fm(64)
        _TABS["s"] = rope_tables_fm(128)
    return _TABS


def inputs_A(xs, l, inp):
    tb = tables()
    w_in = prep_w_in(inp["w_in"][l])
    w_qb = prep_w_qb(inp["w_qb"][l])
    w_kvb = prep_w_kvb(inp["w_kvb"][l])
    g_cq = fm_vec(inp["g_cq"][l], 4)
    g_ckv = fm_vec(inp["g_ckv"][l], 2)
    maps = []
    for c in range(NCORE):
        sl = slice(c * T, (c + 1) * T)
        maps.append({
            "x": np.ascontiguousarray(xs[c]), "w_in": w_in, "w_qb": w_qb, "w_kvb": w_kvb,
            "g_cq": g_cq, "g_ckv": g_ckv,
            "cos_m": np.ascontiguousarray(tb["m"][0][:, sl]), "sin_m": np.ascontiguousarray(tb["m"][1][:, sl]),
            "cos_s": np.ascontiguousarray(tb["s"][0][:, sl]), "sin_s": np.ascontiguousarray(tb["s"][1][:, sl]),
        })
    return maps


def swa_masks(core):
    qi = np.arange(128)[:, None]
    ki = np.arange(128)[None, :]
    prev = np.where(qi <= ki, 0.0, NEG).astype(np.float32)
    mid = np.zeros((128, 128), np.float32)
    nxt = np.where(ki <= qi, 0.0, NEG).astype(np.float32)
    full = np.concatenate([prev, mid, nxt], axis=1)
    allneg = np.full((128, 128), NEG, np.float32)
    first = full.copy()
    last = full.copy()
    if core == 0:
        first[:, :128] = allneg
    if core == NCORE - 1:
        last[:, 256:] = allneg
    return np.ascontiguousarray(np.stack([first, full, last], axis=1))


def inputs_Bb(l, inp, resBa):
    j = l // 2
    ln = np.stack([inp["ln1_g"][l], inp["ln1_b"][l], inp["ln2_g"][l], inp["ln2_b"][l]], 0)
    ln_bc = np.ascontiguousarray(np.broadcast_to(ln[None], (128, 4, 2048))).astype(np.float32)
    iota_row = np.ascontiguousarray(np.broadcast_to(np.arange(MOE_CAPMAX, dtype=np.float32)[None, :], (128, MOE_CAPMAX)))
    maps = []
    for c in range(NCORE):
        maps.append({"ln_bc": ln_bc, "iota_row": iota_row,
                     "wg": inp["moe_wg"][j], "wu": inp["moe_wu"][j], "wd": inp["moe_wd"][j],
                     "x1s_in": resBa[c]["x1s"], "xb_in": resBa[c]["xb_d"],
                     "gates_in": resBa[c]["gates_o"], "rank_in": resBa[c]["rank_o"]})
    return maps


def inputs_B(xs, l, inp, resA, moe):
    kv_all = np.ascontiguousarray(np.concatenate([resA[c]["kv_out"] for c in range(NCORE)], axis=0))
    sink_bc = np.ascontiguousarray(np.broadcast_to(inp["sink"][l][None, :], (128, 8))).astype(np.float32)
    g_mla = fm_vec(inp["g_out_mla"][l], 8)
    g_swa = np.ascontiguousarray(np.broadcast_to(inp["g_out_swa"][l][None, :], (128, 1024))).astype(np.float32)
    ln = np.stack([inp["ln1_g"][l], inp["ln1_b"][l], inp["ln2_g"][l], inp["ln2_b"][l]], 0)
    ln_bc = np.ascontiguousarray(np.broadcast_to(ln[None], (128, 4, 2048))).astype(np.float32)
    j = l // 2
    maps = []
    for c in range(NCORE):
        ks = resA[c]["ks_out"]
        vs = resA[c]["vs_out"]
        kprev = resA[c - 1]["ks_out"][:, -128:] if c > 0 else np.zeros((256, 128), ks.dtype)
        knext = resA[c + 1]["ks_out"][:, :128] if c < NCORE - 1 else np.zeros((256, 128), ks.dtype)
        vprev = resA[c - 1]["vs_out"][-128:] if c > 0 else np.zeros((128, 256), vs.dtype)
        vnext = resA[c + 1]["vs_out"][:128] if c < NCORE - 1 else np.zeros((128, 256), vs.dtype)
        m = {
            "x": np.ascontiguousarray(xs[c]), "kv_all": kv_all,
            "qn": resA[c]["qn_out"], "qpe": resA[c]["qpe_out"], "qs": resA[c]["qs_out"],
            "ks_ext": np.ascontiguousarray(np.concatenate([kprev, ks, knext], axis=1)),
            "vs_ext": np.ascontiguousarray(np.concatenate([vprev, vs, vnext], axis=0)),
            "masks": swa_masks(c), "sink_bc": sink_bc, "g_mla": g_mla, "g_swa_bc": g_swa,
            "w_out": inp["w_out"][l], "ln_bc": ln_bc,
        }
        if moe:
            lmat = (np.arange(128)[:, None] < np.arange(128)[None, :]).astype(np.float32).astype(BF)
            m.update({"w_router": inp["router_w"][j], "lmat": lmat})
        else:
            m.update({"wg": inp["dense_wg"][j], "wu": inp["dense_wu"][j], "wd": inp["dense_wd"][j]})
        maps.append(m)
    return maps


def build_A():
    nc = bass.Bass("TRN2", target_bir_lowering=False)
    x = nc.dram_tensor("x", [T, D], F32, kind="ExternalInput").ap()
    w_in = nc.dram_tensor("w_in", [D, NC_IN], F32, kind="ExternalInput").ap()
    w_qb = nc.dram_tensor("w_qb", [512, 2048], F32, kind="ExternalInput").ap()
    w_kvb = nc.dram_tensor("w_kvb", [256, 2048], F32, kind="ExternalInput").ap()
    g_cq = nc.dram_tensor("g_cq", [128, 4], F32, kind="ExternalInput").ap()
    g_ckv = nc.dram_tensor("g_ckv", [128, 2], F32, kind="ExternalInput").ap()
    cos_m = nc.dram_tensor("cos_m", [64, T], F32, kind="ExternalInput").ap()
    sin_m = nc.dram_tensor("sin_m", [64, T], F32, kind="ExternalInput").ap()
    cos_s = nc.dram_tensor("cos_s", [128, T], F32, kind="ExternalInput").ap()
    sin_s = nc.dram_tensor("sin_s", [128, T], F32, kind="ExternalInput").ap()
    kv_out = nc.dram_tensor("kv_out", [16 * 128 + 64, T], BF16, kind="ExternalOutput").ap()
    qn_out = nc.dram_tensor("qn_out", [8 * 128, T], BF16, kind="ExternalOutput").ap()
    qpe_out = nc.dram_tensor("qpe_out", [8 * 64, T], BF16, kind="ExternalOutput").ap()
    qs_out = nc.dram_tensor("qs_out", [8 * 128, T], BF16, kind="ExternalOutput").ap()
    ks_out = nc.dram_tensor("ks_out", [2 * 128, T], BF16, kind="ExternalOutput").ap()
    vs_out = nc.dram_tensor("vs_out", [T, 256], BF16, kind="ExternalOutput").ap()

    P = Prog(nc)
    sb = SBAlloc(nc)
    ps = nc.alloc_psum_tensor("ps", [128, 8, 512], F32)
    psb = [Buf(f"ps{i}") for i in range(8)]
    pctr = [0]

    def bank():
        i = pctr[0] % 8
        pctr[0] += 1
        return ps[:, i, :], psb[i]

    b_out = Buf("out")

    ident = sb.alloc("ident", [128, 128], BF16)
    ones = sb.alloc("ones", [128, 128], BF16)
    gq = sb.alloc("gq", [128, 4], F32)
    gkv = sb.alloc("gkv", [128, 2], F32)
    b_c = Buf("consts")
    P.op(POOL, "memset", dict(ap=ident[:], constant=0.0), writes=[b_c])
    P.op(POOL, "affine_select", dict(out=ident[:], in_=ident[:], pattern=[[-1, 128]],
                                     compare_op=ALU.not_equal, fill=1.0, base=0,
                                     channel_multiplier=1), reads=[b_c], writes=[b_c])
    P.op(POOL, "memset", dict(ap=ones[:], constant=1.0), reads=[b_c], writes=[b_c])
    b_g = Buf("g")
    P.dma(SP, gq[:], g_cq[:, :], writes=[b_g])
    P.dma(SP, gkv[:], g_ckv[:, :], writes=[b_g])

    cosm = sb.alloc("cosm", [64, HT], F32)
    sinm = sb.alloc("sinm", [64, HT], F32)
    coss = sb.alloc("coss", [128, HT], F32)
    sins = sb.alloc("sins", [128, HT], F32)
    cqn = sb.alloc("cqn", [128, 4, HT], BF16)
    ckvn = sb.alloc("ckvn", [128, 2, HT], BF16)
    kpeT = sb.alloc("kpeT", [64, HT], BF16)
    b_tab = Buf("tab")
    b_cqn = [Buf(f"cqn{i}") for i in range(2)]
    b_ckvn = [Buf(f"ckvn{i}") for i in range(2)]
    b_kpe = Buf("kpe")
    m_phase = sb.mark()

    w_in_v = w_in.rearrange("(k p) c -> p k c", p=128)
    groups = [(0, 512), (512, 896)] + [(896 + 512 * i, 896 + 512 * (i + 1)) for i in range(4)] + \
             [(2944, 3456), (3456, 3712)]

    for half in range(2):
        t0 = half * HT
        sb.reset(m_phase)
        xT = sb.alloc("xT", [128, 16, HT], BF16)
        b_xT = [Buf(f"xT{i}") for i in range(HT // 128)]
        xt = [sb.alloc(f"xt{r}", [128, D], BF16) for r in range(2)]
        b_xt = [Buf(f"xt{r}") for r in range(2)]
        wring = [sb.alloc(f"wr{r}", [128, 16, 512], BF16) for r in range(2)]
        b_wr = [Buf(f"wr{r}") for r in range(2)]
        sq = sb.alloc("sq", [128, 4, 512], BF16)
        b_sq = Buf("sq")
        rstd = sb.alloc("rstd", [128, 512], F32)
        b_rstd = Buf("rstd")
        tmp = [sb.alloc(f"tmp{r}", [128, 512], F32) for r in range(4)]
        b_tmp = [Buf(f"tmp{r}") for r in range(4)]
        stg = [sb.alloc(f"stg{r}", [128, HT], BF16) for r in range(4)]
        b_stg = [Buf(f"stg{r}") for r in range(4)]
        vstg = sb.alloc("vstg", [128, HT // 128, 256], BF16)
        b_vstg = Buf("vstg")

        P.dma(SP, cosm[:], cos_m[:, t0:t0 + HT], writes=[b_tab])
        P.dma(SP, sinm[:], sin_m[:, t0:t0 + HT], writes=[b_tab])
        P.dma(SP, coss[:], cos_s[:, t0:t0 + HT], writes=[b_tab])
        P.dma(SP, sins[:], sin_s[:, t0:t0 + HT], writes=[b_tab])

        for i in range(HT // 128):
            r = i % 2
            P.dma(POOL, xt[r][:], x[t0 + i * 128:t0 + (i + 1) * 128, :], writes=[b_xt[r]])
            j = (pctr[0] // 2) % 4
            pctr[0] += 2
            pT = ps[:, 2 * j:2 * j + 2, :].bitcast(BF16).rearrange("p a (b c) -> p (a b) c", c=128)
            pbufs = [psb[2 * j], psb[2 * j + 1]]
            for k in range(16):
                P.op(PE, "transpose", dict(out=pT[:, k, :], in_=xt[r][:, k * 128:(k + 1) * 128], identity=ident[:]),
                     reads=[b_xt[r], b_c], writes=pbufs, sig=(k == 15))
            eng = ACT if i % 2 == 0 else DVE
            if eng == ACT:
                P.op(ACT, "copy", dict(out=xT[:, :, i * 128:(i + 1) * 128], in_=pT[:, :, :]),
                     reads=pbufs, writes=[b_xT[i]])
            else:
                P.op(DVE, "tensor_copy", dict(out=xT[:, :, i * 128:(i + 1) * 128], in_=pT[:, :, :]),
                     reads=pbufs, writes=[b_xT[i]])

        def rms_block(chunks, nfeat, gvec, dst, dstbuf, tb):
            n = len(chunks)
            for c, (pa, pb) in enumerate(chunks):
                P.op(ACT, "activation", dict(out=sq[:, c, :], in_=pa, func=AF.Square),
                     reads=[pb], writes=[b_sq])
            sa, sbf = bank()
            mm_acc(P, sa, sbf, [(ones[:], sq[:, c, :]) for c in range(n)], [b_sq, b_c])
            P.op(DVE, "tensor_scalar", dict(out=rstd[:], in0=sa, scalar1=1.0 / nfeat, scalar2=RMS_EPS,
                                            op0=ALU.mult, op1=ALU.add), reads=[sbf], writes=[b_rstd])
            P.op(ACT, "activation", dict(out=rstd[:], in_=rstd[:], func=AF.Sqrt), reads=[b_rstd], writes=[b_rstd])
            P.op(DVE, "reciprocal", dict(out=rstd[:], in_=rstd[:]), reads=[b_rstd], writes=[b_rstd])
            for c, (pa, pb) in enumerate(chunks):
                P.op(DVE, "scalar_tensor_tensor", dict(
                    out=dst[:, c, tb * 512:(tb + 1) * 512], in0=pa, scalar=gvec[:, c:c + 1], in1=rstd[:],
                    op0=ALU.mult, op1=ALU.mult), reads=[pb, b_rstd, b_g], writes=[dstbuf])

        tctr = [0]

        def rope_block(pa, pb_a, pu, pb_u, cosT, sinT, nrow, dst_ap, dst_buf, tb):
            i0 = tctr[0] % 4
            i1 = (tctr[0] + 1) % 4
            tctr[0] += 2
            cs = slice(tb * 512, (tb + 1) * 512)
            P.op(DVE, "tensor_tensor", dict(out=tmp[i0][:nrow, :], in0=pa, in1=cosT[:nrow, cs], op=ALU.mult),
                 reads=[pb_a, b_tab], writes=[b_tmp[i0]])
            P.op(DVE, "tensor_tensor", dict(out=tmp[i1][:nrow, :], in0=pu, in1=sinT[:nrow, cs], op=ALU.mult),
                 reads=[pb_u, b_tab], writes=[b_tmp[i1]])
            P.op(POOL, "tensor_tensor", dict(out=dst_ap, in0=tmp[i0][:nrow, :], in1=tmp[i1][:nrow, :], op=ALU.add),
                 reads=[b_tmp[i0], b_tmp[i1]], writes=[dst_buf])

        sctr = [0]
        for gi, (c0, c1) in enumerate(groups):
            r = gi % 2
            ncol = c1 - c0
            P.dma(POOL, wring[r][:, :, :ncol], w_in_v[:, :, c0:c1], writes=[b_wr[r]])
            W = wring[r]
            if gi == 7:
                for i in range(HT // 128):
                    pa, pb = bank()
                    mm_acc(P, pa[:, :256], pb,
                           [(xT[:, k, i * 128:(i + 1) * 128], W[:, k, 0:256]) for k in range(16)],
                           [b_xT[i], b_wr[r]])
                    P.op(ACT, "copy", dict(out=vstg[:, i, :], in_=pa[:, :256]),
                         reads=[pb], writes=[b_vstg])
                P.dma(SP, vs_out[t0:t0 + HT, :].rearrange("(i p) c -> p i c", p=128), vstg[:],
                      reads=[b_vstg], writes=[b_out])
                continue
            if gi >= 2:
                sidx = [sctr[0] % 4, (sctr[0] + 1) % 4]
                sctr[0] += 2
            for tb in range(HT // 512):
                xb = [b_xT[4 * tb + q] for q in range(4)]
                cs = slice(tb * 512, (tb + 1) * 512)

                def chunk(col0, ncols_):
                    pa, pb = bank()
                    mm_acc(P, pa[:ncols_, :], pb,
                           [(W[:, k, col0:col0 + ncols_], xT[:, k, cs]) for k in range(16)],
                           xb + [b_wr[r]])
                    return pa, pb
                if gi == 0:
                    chunks = [chunk(c * 128, 128) for c in range(4)]
                    rms_block(chunks, 512, gq, cqn, b_cqn[tb], tb)
                elif gi == 1:
                    chunks = [chunk(c * 128, 128) for c in range(2)]
                    rms_block(chunks, 256, gkv, ckvn, b_ckvn[tb], tb)
                    pa, pb = chunk(256, 64)
                    pu, pbu = chunk(320, 64)
                    rope_block(pa[:64, :], pb, pu[:64, :], pbu, cosm, sinm, 64, kpeT[:, cs], b_kpe, tb)
                else:
                    for hh in range(2):
                        pa, pb = chunk(hh * 256, 128)
                        pu, pbu = chunk(hh * 256 + 128, 128)
                        rope_block(pa, pb, pu, pbu, coss, sins, 128, stg[sidx[hh]][:, cs], b_stg[sidx[hh]], tb)
            if gi >= 2:
                for hh in range(2):
                    if gi <= 5:
                        h = (gi - 2) * 2 + hh
                        dst = qs_out[h * 128:(h + 1) * 128, t0:t0 + HT]
                    else:
                        dst = ks_out[hh * 128:(hh + 1) * 128, t0:t0 + HT]
                    P.dma(SP, dst, stg[sidx[hh]][:], reads=[b_stg[sidx[hh]]], writes=[b_out])

        P.dma(SP, kv_out[16 * 128:16 * 128 + 64, t0:t0 + HT], kpeT[:], reads=[b_kpe], writes=[b_out])

        P.barrier()
        sb.reset(m_phase)
        wqb = sb.alloc("wqb", [128, 4, 2048], BF16)
        wkvb = sb.alloc("wkvb", [128, 2, 2048], BF16)
        b_wqb, b_wkvb = Buf("wqb"), Buf("wkvb")
        qn = sb.alloc("qn", [128, 8, HT], BF16)
        qpe = sb.alloc("qpe", [64, 8, HT], BF16)
        KT = sb.alloc("KT", [128, 8, HT], BF16)
        Vt = sb.alloc("Vt", [128, HT // 128, 1024], BF16)
        b_qn, b_qpe, b_KT, b_Vt = Buf("qn"), Buf("qpe"), Buf("KT"), Buf("Vt")
        tmp = [sb.alloc(f"tmpb{r}", [128, 512], F32) for r in range(4)]
        b_tmp = [Buf(f"tmpb{r}") for r in range(4)]
        P.dma(POOL, wqb[:], w_qb.rearrange("(k p) c -> p k c", p=128), writes=[b_wqb])
        P.dma(POOL, wkvb[:], w_kvb.rearrange("(k p) c -> p k c", p=128), writes=[b_wkvb])
        for h in range(8):
            for tb in range(HT // 512):
                cs = slice(tb * 512, (tb + 1) * 512)
                pa, pb = bank()
                mm_acc(P, pa, pb, [(wqb[:, c, 256 * h:256 * h + 128], cqn[:, c, cs]) for c in range(4)],
                       [b_wqb, b_cqn[tb]])
                P.op(ACT, "copy", dict(out=qn[:, h, cs], in_=pa), reads=[pb], writes=[b_qn])
                pa, pb = bank()
                mm_acc(P, pa[:64, :], pb, [(wqb[:, c, 256 * h + 128:256 * h + 192], cqn[:, c, cs]) for c in range(4)],
                       [b_wqb, b_cqn[tb]])
                pu, pbu = bank()
                mm_acc(P, pu[:64, :], pbu, [(wqb[:, c, 256 * h + 192:256 * h + 256], cqn[:, c, cs]) for c in range(4)],
                       [b_wqb, b_cqn[tb]])
                rope_block(pa[:64, :], pb, pu[:64, :], pbu, cosm, sinm, 64, qpe[:, h, cs], b_qpe, tb)
                pa, pb = bank()
                mm_acc(P, pa, pb, [(wkvb[:, c, 128 * h:128 * h + 128], ckvn[:, c, cs]) for c in range(2)],
                       [b_wkvb, b_ckvn[tb]])
                P.op(ACT, "copy", dict(out=KT[:, h, cs], in_=pa), reads=[pb], writes=[b_KT])
        for i in range(HT // 128):
            tb = i // 4
            for hf in range(2):
                pa, pb = bank()
                mm_acc(P, pa, pb,
                       [(ckvn[:, c, i * 128:(i + 1) * 128], wkvb[:, c, 1024 + hf * 512:1024 + (hf + 1) * 512])
                        for c in range(2)], [b_wkvb, b_ckvn[tb]])
                P.op(DVE, "tensor_copy", dict(out=Vt[:, i, hf * 512:(hf + 1) * 512], in_=pa),
                     reads=[pb], writes=[b_Vt])
        P.dma(SP, qn_out.rearrange("(h p) t -> p h t", p=128)[:, :, t0:t0 + HT], qn[:], reads=[b_qn], writes=[b_out])
        P.dma(SP, qpe_out.rearrange("(h p) t -> p h t", p=64)[:, :, t0:t0 + HT], qpe[:], reads=[b_qpe], writes=[b_out])
        for h in range(8):
            P.dma(SP, kv_out[(2 * h) * 128:(2 * h + 1) * 128, t0:t0 + HT], KT[:, h, :], reads=[b_KT], writes=[b_out])
            dst = kv_out[(2 * h + 1) * 128:(2 * h + 2) * 128, :].rearrange("p (i d) -> p i d", d=128)
            P.dma(SP, dst[:, half * (HT // 128):(half + 1) * (HT // 128), :], Vt[:, :, h * 128:(h + 1) * 128],
                  reads=[b_Vt], writes=[b_out])
        P.barrier()

    P.final_wait(SP, [b_out])
    with nc.Block() as block:
        P.emit(block)
    return nc


def build_B(moe, F, stage="all", caps=None):
    nc = bass.Bass("TRN2", target_bir_lowering=False)

    def din(name, shape, dt=F32):
        return nc.dram_tensor(name, list(shape), dt, kind="ExternalInput").ap()
    ln_d = din("ln_bc", [128, 4, D])
    if stage != "b":
        x = din("x", [T, D])
        kv_all = din("kv_all", [8 * KVROWS, T], BF16)
        qn_d = din("qn", [1024, T], BF16)
        qpe_d = din("qpe", [512, T], BF16)
        qs_d = din("qs", [1024, T], BF16)
        ks_d = din("ks_ext", [256, T + 256], BF16)
        vs_d = din("vs_ext", [T + 256, 256], BF16)
        masks_d = din("masks", [128, 3, 384])
        sink_d = din("sink_bc", [128, 8])
        gmla_d = din("g_mla", [128, 8])
        gswa_d = din("g_swa_bc", [128, 1024])
        wout_d = din("w_out", [D, D])
    if stage == "a":
        wr_d = din("w_router", [D, 8])
        lmat_d = din("lmat", [128, 128], BF16)
    if stage == "b":
        iota_d = din("iota_row", [128, MOE_CAPMAX])
        wg_d = din("wg", [8, D, F])
        wu_d = din("wu", [8, D, F])
        wd_d = din("wd", [8, F, D])
        x1s = din("x1s_in", [T, D])
        xb_d = din("xb_in", [T, D], BF16)
        gates_i = din("gates_in", [128, 128])
        rank_i = din("rank_in", [128, 128])
    if stage == "all":
        wg_d = din("wg", [1, D, F])
        wu_d = din("wu", [1, D, F])
        wd_d = din("wd", [1, F, D])
    if stage == "a":
        x1s = nc.dram_tensor("x1s", [T, D], F32, kind="ExternalOutput").ap()
        xb_d = nc.dram_tensor("xb_d", [T, D], BF16, kind="ExternalOutput").ap()
        gates_o = nc.dram_tensor("gates_o", [128, 128], F32, kind="ExternalOutput").ap()
        rank_o = nc.dram_tensor("rank_o", [128, 128], F32, kind="ExternalOutput").ap()
        cnt_o = nc.dram_tensor("cnt_o", [128, 8], F32, kind="ExternalOutput").ap()
    else:
        y_out = nc.dram_tensor("y", [T, D], F32, kind="ExternalOutput").ap()
    if stage == "all":
        x1s = nc.dram_tensor("x1s", [T, D], F32).ap()
        xb_d = nc.dram_tensor("xb_d", [T, D], BF16).ap()
    x1T_d = nc.dram_tensor("x1T_d", [128, 16, T], BF16).ap()
    yacc_d = nc.dram_tensor("yacc_d", [T, D], F32).ap()
    b_xbd = Buf("xb_d")

    P = Prog(nc)
    sb = SBAlloc(nc)
    ps = nc.alloc_psum_tensor("ps", [128, 8, 512], F32)
    psb = [Buf(f"ps{i}") for i in range(8)]
    pctr = [0]
    prange = [0, 8]

    def bank():
        lo, hi = prange
        i = lo + pctr[0] % (hi - lo)
        pctr[0] += 1
        return ps[:, i, :], psb[i]

    b_out = Buf("out")
    b_x1s = Buf("x1s")
    b_x1T = Buf("x1T_d")

    ident = sb.alloc("ident", [128, 128], BF16)
    ones = sb.alloc("ones", [128, 128], BF16)
    sink = sb.alloc("sink", [128, 8], F32)
    gmla = sb.alloc("gmla", [128, 8], F32)
    b_c = Buf("consts")
    P.op(POOL, "memset", dict(ap=ident[:], constant=0.0), writes=[b_c])
    P.op(POOL, "affine_select", dict(out=ident[:], in_=ident[:], pattern=[[-1, 128]],
                                     compare_op=ALU.not_equal, fill=1.0, base=0,
                                     channel_multiplier=1), reads=[b_c], writes=[b_c])
    P.op(POOL, "memset", dict(ap=ones[:], constant=1.0), reads=[b_c], writes=[b_c])
    b_p = Buf("params")
    if stage != "b":
        P.dma(SP, sink[:], sink_d[:, :], writes=[b_p])
        P.dma(SP, gmla[:], gmla_d[:, :], writes=[b_p])
    gates = sb.alloc("gates", [128, 16, 8], F32)
    b_gates = Buf("gates")
    selm = sb.alloc("selm", [128, 128], F32)
    b_selm = Buf("selm")
    if stage == "a":
        wr32 = sb.alloc("wr32", [128, 16, 8], F32)
        wrh = sb.alloc("wrh", [128, 16, 8], BF16)
        wrl = sb.alloc("wrl", [128, 16, 8], BF16)
        b_wr = Buf("wr")
        P.dma(SP, wr32[:], wr_d.rearrange("(k p) e -> p k e", p=128), writes=[b_wr])
        P.op(ACT, "copy", dict(out=wrh[:], in_=wr32[:]), reads=[b_wr], writes=[b_wr])
        P.op(DVE, "tensor_tensor", dict(out=wrl[:], in0=wr32[:], in1=wrh[:], op=ALU.subtract), reads=[b_wr], writes=[b_wr])
    m_const = sb.mark()

    mT = sb.alloc("mT", [128, 16, T], BF16)
    b_mT = [Buf(f"mT{i}") for i in range(16)]
    m_mT = sb.mark()

    def small_pool(prefix, n, width=1):
        ts = [sb.alloc(f"{prefix}{i}", [128, width], F32) for i in range(n)]
        bs = [Buf(f"{prefix}{i}") for i in range(n)]
        return ts, bs

    def layer_norm(src, b_src, gi, dst, b_dst, slot):
        s = [ls[8 * slot + q] for q in range(8)]
        bs = [b_ls[8 * slot + q] for q in range(8)]
        P.op(ACT, "activation", dict(out=junk[:], in_=src, func=AF.Identity, accum_out=s[0][:]),
             reads=[b_src], writes=[b_junk, bs[0]])
        P.op(ACT, "activation", dict(out=junk[:], in_=src, func=AF.Square, accum_out=s[1][:]),
             reads=[b_src], writes=[b_junk, bs[1]])
        P.op(DVE, "tensor_scalar", dict(out=s[2][:], in0=s[0][:], scalar1=1.0 / D, scalar2=None, op0=ALU.mult),
             reads=[bs[0]], writes=[bs[2]])
        P.op(DVE, "tensor_tensor", dict(out=s[3][:], in0=s[2][:], in1=s[2][:], op=ALU.mult),
             reads=[bs[2]], writes=[bs[3]])
        P.op(DVE, "scalar_tensor_tensor", dict(out=s[4][:], in0=s[1][:], scalar=1.0 / D, in1=s[3][:],
                                               op0=ALU.mult, op1=ALU.subtract),
             reads=[bs[1], bs[3]], writes=[bs[4]])
        P.op(DVE, "tensor_scalar", dict(out=s[4][:], in0=s[4][:], scalar1=LN_EPS, scalar2=None, op0=ALU.add),
             reads=[bs[4]], writes=[bs[4]])
        P.op(ACT, "activation", dict(out=s[5][:], in_=s[4][:], func=AF.Sqrt), reads=[bs[4]], writes=[bs[5]])
        P.op(DVE, "reciprocal", dict(out=s[6][:], in_=s[5][:]), reads=[bs[5]], writes=[bs[6]])
        P.op(DVE, "scalar_tensor_tensor", dict(out=s[7][:], in0=s[2][:], scalar=-1.0, in1=s[6][:],
                                               op0=ALU.mult, op1=ALU.mult),
             reads=[bs[2], bs[6]], writes=[bs[7]])
        P.op(ACT, "activation", dict(out=dst, in_=src, func=AF.Identity, scale=s[6][:], bias=s[7][:]),
             reads=[b_src, bs[6], bs[7]], writes=[b_dst])
        P.op(DVE, "tensor_tensor", dict(out=dst, in0=dst, in1=lnp[:, 0, :], op=ALU.mult),
             reads=[b_dst, b_lnp], writes=[b_dst])
        P.op(POOL, "tensor_tensor", dict(out=dst, in0=dst, in1=lnp[:, 1, :], op=ALU.add),
             reads=[b_dst, b_lnp], writes=[b_dst])

    if stage != "b":
        qsT = sb.alloc("qsT", [128, 8, T], BF16)
        ksT = sb.alloc("ksT", [128, 2, T + 256], BF16)
        vsx = sb.alloc("vsx", [128, 18, 256], BF16)
        msk = sb.alloc("msk", [128, 3, 384], F32)
        gswa = sb.alloc("gswa", [128, 1024], F32)
        b_swa_in = Buf("swa_in")
        P.dma(SP, qsT[:], qs_d.rearrange("(h p) t -> p h t", p=128), writes=[b_swa_in])
        P.dma(SP, ksT[:], ks_d.rearrange("(h p) t -> p h t", p=128), writes=[b_swa_in])
        P.dma(SP, vsx[:], vs_d.rearrange("(i p) c -> p i c", p=128), writes=[b_swa_in])
        P.dma(SP, msk[:], masks_d[:, :, :], writes=[b_swa_in])
        P.dma(SP, gswa[:], gswa_d[:, :], writes=[b_swa_in])
        NR = 3
        Sm = [sb.alloc(f"Sm{r}", [128, 384], F32) for r in range(NR)]
        b_Sm = [Buf(f"Sm{r}") for r in range(NR)]
        Pb = [sb.alloc(f"Pb{r}", [128, 384], BF16) for r in range(NR)]
        b_Pb = [Buf(f"Pb{r}") for r in range(NR)]
        PTs = [sb.alloc(f"PTs{r}", [128, 3, 128], BF16) for r in range(NR)]
        b_PTs = [Buf(f"PTs{r}") for r in range(NR)]
        st, b_st = small_pool("st", 8 * NR)
        otile = [sb.alloc(f"otile{r}", [128, 1024], F32) for r in range(2)]
        b_otile = [Buf(f"otile{r}") for r in range(2)]
        obf = [sb.alloc(f"obf{r}", [128, 1024], BF16) for r in range(2)]
        b_obf = [Buf(f"obf{r}") for r in range(2)]
        junk = sb.alloc("junk", [128, 2048], BF16)
        b_junk = Buf("junk")
        st2, b_st2 = small_pool("st2", 4)
        scale_s = 128 ** -0.5
        it = 0
        for i in range(16):
            mi = 0 if i == 0 else (2 if i == 15 else 1)
            ot, b_ot = otile[i % 2], b_otile[i % 2]
            for h in range(8):
                kvh = h // 4
                r = it % NR
                it += 1
                s = [st[8 * r + q] for q in range(8)]
                bs = [b_st[8 * r + q] for q in range(8)]
                pa, pb = bank()
                P.op(PE, "matmul", dict(out=pa[:, :384], lhsT=qsT[:, h, i * 128:(i + 1) * 128],
                                        rhs=ksT[:, kvh, i * 128:i * 128 + 384], start=True, stop=True),
                     reads=[b_swa_in], writes=[pb])
                P.op(DVE, "scalar_tensor_tensor", dict(out=Sm[r][:], in0=pa[:, :384], scalar=scale_s, in1=msk[:, mi, :],
                                                       op0=ALU.mult, op1=ALU.add), reads=[pb, b_swa_in], writes=[b_Sm[r]])
                P.op(DVE, "tensor_reduce", dict(out=s[0][:], in_=Sm[r][:], axis=AX.X, op=ALU.max),
                     reads=[b_Sm[r]], writes=[bs[0]])
                P.op(DVE, "tensor_tensor", dict(out=s[1][:], in0=s[0][:], in1=sink[:, h:h + 1], op=ALU.max),
                     reads=[bs[0], b_p], writes=[bs[1]])
                P.op(DVE, "tensor_scalar", dict(out=s[2][:], in0=s[1][:], scalar1=-1.0, scalar2=None, op0=ALU.mult),
                     reads=[bs[1]], writes=[bs[2]])
                P.op(ACT, "activation", dict(out=Pb[r][:], in_=Sm[r][:], func=AF.Exp, bias=s[2][:], accum_out=s[3][:]),
                     reads=[b_Sm[r], bs[2]], writes=[b_Pb[r], bs[3]])
                P.op(ACT, "activation", dict(out=s[4][:], in_=sink[:, h:h + 1], func=AF.Exp, bias=s[2][:]),
                     reads=[bs[2], b_p], writes=[bs[4]])
                P.op(DVE, "tensor_tensor", dict(out=s[5][:], in0=s[3][:], in1=s[4][:], op=ALU.add),
                     reads=[bs[3], bs[4]], writes=[bs[5]])
                P.op(DVE, "reciprocal", dict(out=s[6][:], in_=s[5][:]), reads=[bs[5]], writes=[bs[6]])
                pa2, pb2 = bank()
                pT = pa2.bitcast(BF16)[:, :384].rearrange("p (a b) -> p a b", b=128)
                for j in range(3):
                    P.op(PE, "transpose", dict(out=pT[:, j, :], in_=Pb[r][:, j * 128:(j + 1) * 128], identity=ident[:]),
                         reads=[b_Pb[r], b_c], writes=[pb2], sig=(j == 2))
                P.op(ACT, "copy", dict(out=PTs[r][:], in_=pT), reads=[pb2], writes=[b_PTs[r]])
                pa3, pb3 = bank()
                mm_acc(P, pa3[:, :128], pb3,
                       [(PTs[r][:, j, :], vsx[:, i + j, kvh * 128:(kvh + 1) * 128]) for j in range(3)],
                       [b_PTs[r], b_swa_in])
                P.op(DVE, "tensor_scalar", dict(out=ot[:, h * 128:(h + 1) * 128], in0=pa3[:, :128], scalar1=s[6][:],
                                                scalar2=None, op0=ALU.mult), reads=[pb3, bs[6]], writes=[b_ot])
            q4 = [st2[q] for q in range(4)]
            bq = [b_st2[q] for q in range(4)]
            P.op(ACT, "activation", dict(out=junk[:, :1024], in_=ot[:], func=AF.Square, accum_out=q4[0][:]),
                 reads=[b_ot], writes=[b_junk, bq[0]])
            P.op(DVE, "tensor_scalar", dict(out=q4[1][:], in0=q4[0][:], scalar1=1.0 / 1024, scalar2=RMS_EPS,
                                            op0=ALU.mult, op1=ALU.add), reads=[bq[0]], writes=[bq[1]])
            P.op(ACT, "activation", dict(out=q4[2][:], in_=q4[1][:], func=AF.Sqrt), reads=[bq[1]], writes=[bq[2]])
            P.op(DVE, "reciprocal", dict(out=q4[3][:], in_=q4[2][:]), reads=[bq[2]], writes=[bq[3]])
            ob, b_ob = obf[i % 2], b_obf[i % 2]
            P.op(DVE, "scalar_tensor_tensor", dict(out=ob[:], in0=ot[:], scalar=q4[3][:], in1=gswa[:],
                                                   op0=ALU.mult, op1=ALU.mult), reads=[b_ot, bq[3], b_swa_in], writes=[b_ob])
            pa4, pb4 = bank()
            pT = pa4.bitcast(BF16).rearrange("p (a b) -> p a b", b=128)
            for c in range(8):
                P.op(PE, "transpose", dict(out=pT[:, c, :], in_=ob[:, c * 128:(c + 1) * 128], identity=ident[:]),
                     reads=[b_ob, b_c], writes=[pb4], sig=(c == 7))
            P.op(ACT, "copy", dict(out=mT[:, 8:16, i * 128:(i + 1) * 128], in_=pT), reads=[pb4], writes=[b_mT[i]])

        P.barrier()
        sb.reset(m_mT)
        QP = 1024
        qn = sb.alloc("qn", [128, 8, QP], BF16)
        qpe = sb.alloc("qpe", [64, 8, QP], BF16)
        b_q = Buf("q")
        NK = 3
        Kc = [sb.alloc(f"Kc{r}", [128, T], BF16) for r in range(NK)]
        Vc = [sb.alloc(f"Vc{r}", [128, 16, 128], BF16) for r in range(NK)]
        Pc = [sb.alloc(f"Pc{r}", [64, T], BF16) for r in range(NK)]
        b_kv = [Buf(f"kvc{r}") for r in range(NK)]
        NP = 4
        PT = [sb.alloc(f"PT{r}", [128, 512], BF16) for r in range(NP)]
        b_PT = [Buf(f"PT{r}") for r in range(NP)]
        oT = sb.alloc("oT", [128, 8, QP], F32)
        b_oT = Buf("oT")
        rs = [sb.alloc(f"rs{r}", [128, 512], F32) for r in range(2)]
        b_rs = [Buf(f"rs{r}") for r in range(2)]
        sq = sb.alloc("sqm", [128, 8, 512], BF16)
        b_sq = Buf("sqm")
        rstd = sb.alloc("rstdm", [128, 512], F32)
        b_rstd = Buf("rstdm")
        scale_m = 192 ** -0.5
        prange[0], prange[1] = 4, 8
        pctr[0] = 0
        cctr = 0
        pctr_pt = 0
        ones32 = sb.alloc("ones32", [128, 128], F32)
        b_o32 = Buf("ones32")
        P.op(POOL, "memset", dict(ap=ones32[:], constant=1.0), writes=[b_o32])
        accs = [sb.alloc(f"accs{q}", [128, 512], F32) for q in range(2)]
        b_acc = [Buf(f"accs{q}") for q in range(2)]
        iters = [(qp, h, rk, kt, qb) for qp in range(2) for h in range(8) for rk in range(8)
                 for kt in range(16) for qb in range(2)]
        NIT = len(iters)
        SKEW = 2
        st_ = {}
        chunk_slot = {}
        ctrs = {"c": 0, "pt": 0}

        def emit_S(j):
            qp, h, rk, kt, qb = iters[j]
            q0 = qp * QP
            if h == 0 and rk == 0 and kt == 0 and qb == 0:
                P.dma(SP, qn[:], qn_d.rearrange("(h p) t -> p h t", p=128)[:, :, q0:q0 + QP], writes=[b_q])
                P.dma(SP, qpe[:], qpe_d.rearrange("(h p) t -> p h t", p=64)[:, :, q0:q0 + QP], writes=[b_q])
            if kt == 0 and qb == 0:
                r = ctrs["c"] % NK
                ctrs["c"] += 1
                chunk_slot[(qp, h, rk)] = r
                base = rk * KVROWS
                P.dma(SP, Kc[r][:], kv_all[base + 2 * h * 128:base + (2 * h + 1) * 128, :], writes=[b_kv[r]])
                P.dma(SP, Vc[r][:], kv_all[base + (2 * h + 1) * 128:base + (2 * h + 2) * 128, :]
                      .rearrange("p (i d) -> p i d", d=128), writes=[b_kv[r]])
                P.dma(SP, Pc[r][:], kv_all[base + 2048:base + 2048 + 64, :], writes=[b_kv[r]])
            r = chunk_slot[(qp, h, rk)]
            qsl = slice(qb * 512, (qb + 1) * 512)
            pa, pb = bank()
            P.op(PE, "matmul", dict(out=pa, lhsT=Kc[r][:, kt * 128:(kt + 1) * 128], rhs=qn[:, h, qsl],
                                    start=True, stop=False), reads=[b_kv[r], b_q], writes=[pb], sig=False)
            P.op(PE, "matmul", dict(out=pa, lhsT=Pc[r][:, kt * 128:(kt + 1) * 128], rhs=qpe[:, h, qsl],
                                    start=False, stop=True), reads=[b_kv[r], b_q], writes=[pb])
            pr = ctrs["pt"] % NP
            ctrs["pt"] += 1
            P.op(ACT, "activation", dict(out=PT[pr][:], in_=pa, func=AF.Exp, scale=scale_m),
                 reads=[pb], writes=[b_PT[pr]])
            st_[j] = (r, pr)

        def emit_PV(j):
            qp, h, rk, kt, qb = iters[j]
            q0 = qp * QP
            r, pr = st_.pop(j)
            first = (rk == 0 and kt == 0)
            last = (rk == 7 and kt == 15)
            qsl = slice(qb * 512, (qb + 1) * 512)
            if kt % 2 == 0:
                P.op(PE, "matmul", dict(out=ps[:, 2 + qb, :], lhsT=ones[:], rhs=PT[pr][:], start=first, stop=False),
                     reads=[b_PT[pr], b_c], writes=[psb[2 + qb]], sig=False)
            P.op(PE, "matmul", dict(out=ps[:, qb, :], lhsT=Vc[r][:, kt, :], rhs=PT[pr][:], start=first, stop=last),
                 reads=[b_PT[pr], b_kv[r]], writes=[psb[qb]])
            if kt % 2 == 1:
                if rk == 0 and kt == 1:
                    P.op(DVE, "tensor_copy", dict(out=accs[qb][:], in_=PT[pr][:]), reads=[b_PT[pr]], writes=[b_acc[qb]])
                else:
                    P.op(DVE, "tensor_tensor", dict(out=accs[qb][:], in0=accs[qb][:], in1=PT[pr][:], op=ALU.add),
                         reads=[b_PT[pr], b_acc[qb]], writes=[b_acc[qb]])
            if not last:
                return
            P.op(PE, "matmul", dict(out=ps[:, 2 + qb, :], lhsT=ones32[:], rhs=accs[qb][:], start=False, stop=True),
                 reads=[b_acc[qb], b_o32], writes=[psb[2 + qb]])
            P.op(DVE, "reciprocal", dict(out=rs[qb][:], in_=ps[:, 2 + qb, :]), reads=[psb[2 + qb]], writes=[b_rs[qb]])
            P.op(DVE, "tensor_tensor", dict(out=oT[:, h, qsl], in0=ps[:, qb, :], in1=rs[qb][:], op=ALU.mult),
                 reads=[psb[qb], b_rs[qb]], writes=[b_oT])
            if not (h == 7 and qb == 1):
                return
            for qb2 in range(2):
                qs2 = slice(qb2 * 512, (qb2 + 1) * 512)
                for h2 in range(8):
                    P.op(ACT, "activation", dict(out=sq[:, h2, :], in_=oT[:, h2, qs2], func=AF.Square),
                         reads=[b_oT], writes=[b_sq])
                sa, sbf = bank()
                mm_acc(P, sa, sbf, [(ones[:], sq[:, h2, :]) for h2 in range(8)], [b_sq, b_c])
                P.op(DVE, "tensor_scalar", dict(out=rstd[:], in0=sa, scalar1=1.0 / 1024, scalar2=RMS_EPS,
                                                op0=ALU.mult, op1=ALU.add), reads=[sbf], writes=[b_rstd])
                P.op(ACT, "activation", dict(out=rstd[:], in_=rstd[:], func=AF.Sqrt), reads=[b_rstd], writes=[b_rstd])
                P.op(DVE, "reciprocal", dict(out=rstd[:], in_=rstd[:]), reads=[b_rstd], writes=[b_rstd])
                tiles = [b_mT[(q0 + qb2 * 512) // 128 + q] for q in range(4)]
                for h2 in range(8):
                    P.op(DVE, "scalar_tensor_tensor", dict(
                        out=mT[:, h2, q0 + qb2 * 512:q0 + (qb2 + 1) * 512], in0=oT[:, h2, qs2], scalar=gmla[:, h2:h2 + 1],
                        in1=rstd[:], op0=ALU.mult, op1=ALU.mult), reads=[b_oT, b_rstd, b_p], writes=tiles)

        for j in range(NIT + SKEW):
            if j < NIT:
                emit_S(j)
            if j >= SKEW:
                emit_PV(j - SKEW)
        prange[0], prange[1] = 0, 8

        P.barrier()
        sb.reset(m_mT)
        wout = sb.alloc("wout", [128, 16, D], BF16)
        b_wout = Buf("wout")
        wv = wout_d.rearrange("(k p) c -> p k c", p=128)
        for cb in range(4):
            P.dma(POOL, wout[:, :, cb * 512:(cb + 1) * 512], wv[:, :, cb * 512:(cb + 1) * 512], writes=[b_wout])
        lnp = sb.alloc("lnpF", [128, 2, D], F32)
        b_lnp = Buf("lnpF")
        P.dma(SP, lnp[:], ln_d[:, 0:2, :], writes=[b_lnp])
        xt = [sb.alloc("xt0", [128, D], F32)] * 2
        b_xt = [Buf("xt0")] * 2
        yp = [sb.alloc(f"yp{r}", [128, D], F32) for r in range(2)]
        b_yp = [Buf(f"yp{r}") for r in range(2)]
        xo = yp
        b_xo = b_yp
        xb = [sb.alloc(f"xb{r}", [128, D], BF16) for r in range(2)]
        b_xb = [Buf(f"xb{r}") for r in range(2)]
        xTs = [sb.alloc(f"xTs{r}", [128, 16, 128], BF16) for r in range(2)]
        b_xTs = [Buf(f"xTs{r}") for r in range(2)]
        if moe:
            xlb = [sb.alloc("xlb0", [128, D], BF16)] * 2
            b_xlb = [Buf("xlb0")] * 2
            xlTs = [sb.alloc("xlTs0", [128, 16, 128], BF16)] * 2
            b_xlTs = [Buf("xlTs0")] * 2
            gs, b_gs = small_pool("gs", 12, 8)
        junk = sb.alloc("junkF", [128, D], BF16)
        b_junk = Buf("junkF")
        ls, b_ls = small_pool("ls", 16)

        for i in range(16):
            r = i % 2
            P.dma(SP, xt[r][:], x[i * 128:(i + 1) * 128, :], writes=[b_xt[r]])
            for cb in range(4):
                pa, pb = bank()
                mm_acc(P, pa, pb, [(mT[:, k, i * 128:(i + 1) * 128], wout[:, k, cb * 512:(cb + 1) * 512])
                                   for k in range(16)], [b_mT[i], b_wout])
                P.op(DVE, "scalar_tensor_tensor", dict(out=yp[r][:, cb * 512:(cb + 1) * 512],
                                                       in0=xt[r][:, cb * 512:(cb + 1) * 512], scalar=ALPHA, in1=pa,
                                                       op0=ALU.mult, op1=ALU.add),
                     reads=[pb, b_xt[r]], writes=[b_yp[r]])
            layer_norm(yp[r][:], b_yp[r], 0, xo[r][:], b_xo[r], r)
            P.dma(SP, x1s[i * 128:(i + 1) * 128, :], xo[r][:], reads=[b_xo[r]], writes=[b_x1s])
            P.op(ACT, "copy", dict(out=xb[r][:], in_=xo[r][:]), reads=[b_xo[r]], writes=[b_xb[r]])
            j = (pctr[0] // 2) % 4
            pctr[0] += 2
            pT = ps[:, 2 * j:2 * j + 2, :].bitcast(BF16).rearrange("p a (b c) -> p (a b) c", c=128)
            pbufs = [psb[2 * j], psb[2 * j + 1]]
            for k in range(16):
                P.op(PE, "transpose", dict(out=pT[:, k, :], in_=xb[r][:, k * 128:(k + 1) * 128], identity=ident[:]),
                     reads=[b_xb[r], b_c], writes=pbufs, sig=(k == 15))
            P.op(DVE, "tensor_copy", dict(out=xTs[r][:], in_=pT), reads=pbufs, writes=[b_xTs[r]])
            P.dma(SP, x1T_d[:, :, i * 128:(i + 1) * 128], xTs[r][:], reads=[b_xTs[r]], writes=[b_x1T])
            if moe:
                P.op(DVE, "tensor_tensor", dict(out=xlb[r][:], in0=xo[r][:], in1=xb[r][:], op=ALU.subtract),
                     reads=[b_xo[r], b_xb[r]], writes=[b_xlb[r]])
                j = (pctr[0] // 2) % 4
                pctr[0] += 2
                pT2 = ps[:, 2 * j:2 * j + 2, :].bitcast(BF16).rearrange("p a (b c) -> p (a b) c", c=128)
                pbufs2 = [psb[2 * j], psb[2 * j + 1]]
                for k in range(16):
                    P.op(PE, "transpose", dict(out=pT2[:, k, :], in_=xlb[r][:, k * 128:(k + 1) * 128], identity=ident[:]),
                         reads=[b_xlb[r], b_c], writes=pbufs2, sig=(k == 15))
                P.op(ACT, "copy", dict(out=xlTs[r][:], in_=pT2), reads=pbufs2, writes=[b_xlTs[r]])
                pl, pbl = bank()
                pairs = []
                for k in range(16):
                    pairs += [(xTs[r][:, k, :], wrh[:, k, :]), (xTs[r][:, k, :], wrl[:, k, :]), (xlTs[r][:, k, :], wrh[:, k, :])]
                mm_acc(P, pl[:, :8], pbl, pairs, [b_xTs[r], b_xlTs[r], b_wr])
                g = [gs[q] for q in range(12)]
                bg = [b_gs[q] for q in range(12)]
                P.op(DVE, "tensor_copy", dict(out=g[0][:], in_=pl[:, :8]), reads=[pbl], writes=[bg[0]])
                P.op(DVE, "tensor_reduce", dict(out=g[1][:, 0:1], in_=g[0][:], axis=AX.X, op=ALU.max), reads=[bg[0]], writes=[bg[1]])
                P.op(DVE, "tensor_scalar", dict(out=g[2][:], in0=g[0][:], scalar1=g[1][:, 0:1], scalar2=None, op0=ALU.is_equal),
                     reads=[bg[0], bg[1]], writes=[bg[2]])
                P.op(DVE, "scalar_tensor_tensor", dict(out=g[3][:], in0=g[2][:], scalar=NEG, in1=g[0][:], op0=ALU.mult, op1=ALU.add),
                     reads=[bg[2], bg[0]], writes=[bg[3]])
                P.op(DVE, "tensor_reduce", dict(out=g[4][:, 0:1], in_=g[3][:], axis=AX.X, op=ALU.max), reads=[bg[3]], writes=[bg[4]])
                P.op(DVE, "tensor_scalar", dict(out=g[5][:], in0=g[3][:], scalar1=g[4][:, 0:1], scalar2=None, op0=ALU.is_equal),
                     reads=[bg[3], bg[4]], writes=[bg[5]])
                P.op(DVE, "tensor_tensor", dict(out=g[6][:, 0:1], in0=g[4][:, 0:1], in1=g[1][:, 0:1], op=ALU.subtract),
                     reads=[bg[4], bg[1]], writes=[bg[6]])
                P.op(ACT, "activation", dict(out=g[7][:, 0:1], in_=g[6][:, 0:1], func=AF.Exp), reads=[bg[6]], writes=[bg[7]])
                P.op(DVE, "tensor_scalar", dict(out=g[8][:, 0:1], in0=g[7][:, 0:1], scalar1=1.0, scalar2=None, op0=ALU.add),
                     reads=[bg[7]], writes=[bg[8]])
                P.op(DVE, "reciprocal", dict(out=g[9][:, 0:1], in_=g[8][:, 0:1]), reads=[bg[8]], writes=[bg[9]])
                P.op(DVE, "tensor_tensor", dict(out=g[10][:, 0:1], in0=g[7][:, 0:1], in1=g[9][:, 0:1], op=ALU.mult),
                     reads=[bg[7], bg[9]], writes=[bg[10]])
                P.op(DVE, "tensor_scalar", dict(out=g[11][:], in0=g[5][:], scalar1=g[10][:, 0:1], scalar2=None, op0=ALU.mult),
                     reads=[bg[5], bg[10]], writes=[bg[11]])
                P.op(DVE, "scalar_tensor_tensor", dict(out=gates[:, i, :], in0=g[2][:], scalar=g[9][:, 0:1], in1=g[11][:],
                                                       op0=ALU.mult, op1=ALU.add),
                     reads=[bg[2], bg[9], bg[11]], writes=[b_gates])
                P.op(DVE, "tensor_tensor", dict(out=selm[:, i * 8:(i + 1) * 8], in0=g[2][:], in1=g[5][:], op=ALU.add),
                     reads=[bg[2], bg[5]], writes=[b_selm])
                P.dma(SP, xb_d[i * 128:(i + 1) * 128, :], xb[r][:], reads=[b_xb[r]], writes=[b_xbd])

    if stage == "a":
        P.barrier()
        sb.reset(m_const)
        Lm = sb.alloc("Lm", [128, 128], BF16)
        b_cm = Buf("moe_consts")
        P.dma(SP, Lm[:], lmat_d[:, :], writes=[b_cm])
        selb = sb.alloc("selb", [128, 128], BF16)
        b_selb = Buf("selb")
        rk = [sb.alloc(f"rk{q}", [128, 128], F32) for q in range(4)]
        b_rk = [Buf(f"rk{q}") for q in range(4)]
        cnt = sb.alloc("cnt", [128, 8], F32)
        b_cnt = Buf("cnt")
        P.op(ACT, "copy", dict(out=selb[:], in_=selm[:]), reads=[b_selm], writes=[b_selb])
        pw, pbw = bank()
        P.op(PE, "matmul", dict(out=pw[:, :128], lhsT=Lm[:], rhs=selb[:], start=True, stop=True),
             reads=[b_selb, b_cm], writes=[pbw])
        pt_, pbt = bank()
        P.op(PE, "matmul", dict(out=pt_[:, :128], lhsT=ones[:], rhs=selb[:], start=True, stop=True),
             reads=[b_selb, b_c], writes=[pbt])
        P.op(DVE, "tensor_copy", dict(out=rk[1][:], in_=pt_[:, :128]), reads=[pbt], writes=[b_rk[1]])
        P.op(DVE, "memset", dict(ap=rk[2][:, 0:8], constant=0.0), writes=[b_rk[2]])
        for i in range(1, 16):
            P.op(DVE, "tensor_tensor", dict(out=rk[2][:, i * 8:(i + 1) * 8], in0=rk[2][:, (i - 1) * 8:i * 8],
                                            in1=rk[1][:, (i - 1) * 8:i * 8], op=ALU.add),
                 reads=[b_rk[1], b_rk[2]], writes=[b_rk[2]])
        P.op(DVE, "tensor_tensor", dict(out=cnt[:], in0=rk[2][:, 120:128], in1=rk[1][:, 120:128], op=ALU.add),
             reads=[b_rk[1], b_rk[2]], writes=[b_cnt])
        P.op(DVE, "tensor_tensor", dict(out=rk[0][:], in0=pw[:, :128], in1=rk[2][:], op=ALU.add),
             reads=[pbw, b_rk[2]], writes=[b_rk[0]])
        P.op(DVE, "scalar_tensor_tensor", dict(out=rk[3][:], in0=rk[0][:], scalar=1.0, in1=selm[:],
                                               op0=ALU.add, op1=ALU.mult), reads=[b_rk[0], b_selm], writes=[b_rk[3]])
        P.op(DVE, "tensor_scalar", dict(out=rk[3][:], in0=rk[3][:], scalar1=-1.0, scalar2=None, op0=ALU.add),
             reads=[b_rk[3]], writes=[b_rk[3]])
        P.dma(SP, gates_o[:, :], gates[:].rearrange("p a b -> p (a b)"), reads=[b_gates], writes=[b_out])
        P.dma(SP, rank_o[:, :], rk[3][:], reads=[b_rk[3]], writes=[b_out])
        P.dma(SP, cnt_o[:, :], cnt[:], reads=[b_cnt], writes=[b_out])
        P.final_wait(SP, [b_out, b_x1s, b_xbd])
        with nc.Block() as block:
            P.emit(block)
        return nc

    P.barrier()
    sb.reset(m_const)
    if stage == "all":
        HT = 1024
        x1T = sb.alloc("x1T", [128, 16, HT], BF16)
        b_x1Th = Buf("x1Th")
        yacc = sb.alloc("yacc", [128, HT // 128, D], F32)
        b_yacc = [Buf(f"yacc{t}") for t in range(HT // 128)]
        GF = 256
        NW = 2
        wgb = [sb.alloc(f"wgb{r}", [128, 16, GF], BF16) for r in range(NW)]
        wub = [sb.alloc(f"wub{r}", [128, 16, GF], BF16) for r in range(NW)]
        wdb = [sb.alloc(f"wdb{r}", [128, GF // 128, D], BF16) for r in range(NW)]
        b_w = [Buf(f"wffn{r}") for r in range(NW)]
        sg = [sb.alloc(f"sg{r}", [128, 512], F32) for r in range(2)]
        b_sg = [Buf(f"sg{r}") for r in range(2)]
        hT = [sb.alloc(f"hT{r}", [128, GF // 128, 512], BF16) for r in range(2)]
        b_hT = [Buf(f"hT{r}") for r in range(2)]
        xt = [sb.alloc(f"xtG{r}", [128, D], F32) for r in range(2)]
        b_xt = [Buf(f"xtG{r}") for r in range(2)]
        lnp = sb.alloc("lnpG", [128, 2, D], F32)
        b_lnp = Buf("lnpG")
        P.dma(SP, lnp[:], ln_d[:, 2:4, :], writes=[b_lnp])
        junk = sb.alloc("junkG", [128, D], BF16)
        b_junk = Buf("junkG")
        ls, b_ls = small_pool("lsG", 16)
        n_exp = 8 if moe else 1
        ngrp = F // GF
        wctr = 0
        sgc = 0
        hc = 0
        for half in range(2):
            t0 = half * HT
            P.dma(SP, x1T[:], x1T_d[:, :, t0:t0 + HT], reads=[b_x1T], writes=[b_x1Th])
            for e in range(n_exp):
                for g in range(ngrp):
                    r = wctr % NW
                    wctr += 1
                    f0 = g * GF
                    P.dma(POOL, wgb[r][:], wg_d[e].rearrange("(k p) f -> p k f", p=128)[:, :, f0:f0 + GF], writes=[b_w[r]])
                    P.dma(POOL, wub[r][:], wu_d[e].rearrange("(k p) f -> p k f", p=128)[:, :, f0:f0 + GF], writes=[b_w[r]])
                    P.dma(POOL, wdb[r][:], wd_d[e, f0:f0 + GF, :].rearrange("(c p) d -> p c d", p=128), writes=[b_w[r]])
                    for tb in range(HT // 512):
                        tsl = slice(tb * 512, (tb + 1) * 512)
                        hr = hc % 2
                        hc += 1
                        for c in range(GF // 128):
                            pg, pbg = bank()
                            mm_acc(P, pg, pbg, [(wgb[r][:, k, c * 128:(c + 1) * 128], x1T[:, k, tsl]) for k in range(16)],
                                   [b_w[r], b_x1Th])
                            pu, pbu = bank()
                            mm_acc(P, pu, pbu, [(wub[r][:, k, c * 128:(c + 1) * 128], x1T[:, k, tsl]) for k in range(16)],
                                   [b_w[r], b_x1Th])
                            sr = sgc % 2
                            sgc += 1
                            P.op(ACT, "activation", dict(out=sg[sr][:], in_=pg, func=AF.Silu), reads=[pbg], writes=[b_sg[sr]])
                            P.op(DVE, "tensor_tensor", dict(out=hT[hr][:, c, :], in0=pu, in1=sg[sr][:], op=ALU.mult),
                                 reads=[pbu, b_sg[sr]], writes=[b_hT[hr]])
                        for t in range(4):
                            tt = tb * 4 + t
                            for cb in range(4):
                                pa, pb = bank()
                                mm_acc(P, pa, pb, [(hT[hr][:, c, t * 128:(t + 1) * 128], wdb[r][:, c, cb * 512:(cb + 1) * 512])
                                                   for c in range(GF // 128)], [b_hT[hr], b_w[r]])
                                dst = yacc[:, tt, cb * 512:(cb + 1) * 512]
                                gsc = gates[:, half * (HT // 128) + tt, e:e + 1]
                                if e == 0 and g == 0:
                                    if moe:
                                        P.op(DVE, "tensor_scalar", dict(out=dst, in0=pa, scalar1=gsc, scalar2=None, op0=ALU.mult),
                                             reads=[pb, b_gates], writes=[b_yacc[tt]])
                                    else:
                                        P.op(DVE, "tensor_copy", dict(out=dst, in_=pa), reads=[pb], writes=[b_yacc[tt]])
                                elif moe:
                                    P.op(DVE, "scalar_tensor_tensor", dict(out=dst, in0=pa, scalar=gsc, in1=dst,
                                                                           op0=ALU.mult, op1=ALU.add),
                                         reads=[pb, b_yacc[tt], b_gates], writes=[b_yacc[tt]])
                                else:
                                    P.op(DVE, "tensor_tensor", dict(out=dst, in0=pa, in1=dst, op=ALU.add),
                                         reads=[pb, b_yacc[tt]], writes=[b_yacc[tt]])
            for tt in range(HT // 128):
                r = tt % 2
                row0 = t0 + tt * 128
                P.dma(SP, xt[r][:], x1s[row0:row0 + 128, :], reads=[b_x1s], writes=[b_xt[r]])
                P.op(DVE, "scalar_tensor_tensor", dict(out=yacc[:, tt, :], in0=xt[r][:], scalar=ALPHA, in1=yacc[:, tt, :],
                                                       op0=ALU.mult, op1=ALU.add),
                     reads=[b_xt[r], b_yacc[tt]], writes=[b_yacc[tt]])
                layer_norm(yacc[:, tt, :], b_yacc[tt], 2, xt[r][:], b_xt[r], r)
                P.dma(SP, y_out[row0:row0 + 128, :], xt[r][:], reads=[b_xt[r]], writes=[b_out])

    if stage == "b":
        CAPMAX = MOE_CAPMAX
        GF = 256
        NW = 2
        ngrp = F // GF
        ls, b_ls = small_pool("lsG", 16)
        iot = sb.alloc("iot", [128, CAPMAX], F32)
        iot2 = sb.alloc("iot2", [128, CAPMAX], F32)
        b_cm = Buf("moe_consts")
        b_iot2 = Buf("iot2")
        P.dma(SP, iot[:], iota_d[:, :], writes=[b_cm])
        rankm = sb.alloc("rankm", [128, 128], F32)
        b_rankm = Buf("rankm")
        P.dma(SP, rankm[:], rank_i[:, :], writes=[b_rankm])
        P.dma(SP, gates[:].rearrange("p a b -> p (a b)"), gates_i[:, :], writes=[b_gates])
        NSL = 3
        Sel = [sb.alloc(f"Sel{r}", [128, CAPMAX], BF16) for r in range(NSL)]
        b_Sel = [Buf(f"Sel{r}") for r in range(NSL)]
        xeT = sb.alloc("xeT", [128, 16 * CAPMAX], BF16)
        b_xeT = Buf("xeT")
        yeacc = sb.alloc("yeacc", [128, CAPMAX // 128, D], F32)
        b_ye = [Buf(f"ye{q}") for q in range(CAPMAX // 128)]
        m_w = sb.mark()
        lnp = sb.alloc("lnpG", [128, 2, D], F32)
        junk = sb.alloc("junkG", [128, D], BF16)
        b_lnp = Buf("lnpG")
        b_junk = Buf("junkG")
        sb.reset(m_w)
        wgb = [sb.alloc(f"wgb{r}", [128, 16, GF], BF16) for r in range(NW)]
        wub = [sb.alloc(f"wub{r}", [128, 16, GF], BF16) for r in range(NW)]
        wdb = [sb.alloc(f"wdb{r}", [128, GF // 128, D], BF16) for r in range(NW)]
        b_w = [Buf(f"wffn{r}") for r in range(NW)]
        NX = 2
        xtok = [sb.alloc(f"xtok{r}", [128, D], BF16) for r in range(NX)]
        b_xtok = [Buf(f"xtok{r}") for r in range(NX)]
        sg = [sb.alloc(f"sg{r}", [128, 512], F32) for r in range(2)]
        b_sg = [Buf(f"sg{r}") for r in range(2)]
        hT = [sb.alloc(f"hT{r}", [128, GF // 128, CAPMAX], BF16) for r in range(2)]
        b_hT = [Buf(f"hT{r}") for r in range(2)]
        selT = [sb.alloc(f"selT{r}", [128, CAPMAX // 128, 128], BF16) for r in range(2)]
        b_selT = [Buf(f"selT{r}") for r in range(2)]
        ybuf = [sb.alloc(f"ybuf{r}", [128, D], F32) for r in range(2)]
        b_ybuf = [Buf(f"ybuf{r}") for r in range(2)]
        b_yd = [Buf(f"yacc_d{i}") for i in range(16)]
        ls_x = sb.alloc("lsx0", [128, D], F32)
        b_lsx = Buf("lsx0")
        segs = []
        for e in range(8):
            c0 = 0
            while c0 < caps[e]:
                segs.append((e, c0, min(CAPMAX, caps[e] - c0)))
                c0 += CAPMAX
        assert segs
        xctr = 0
        wctr = 0
        sgc = 0
        hc = 0
        yc = 0
        selctr = [0]
        for si, (e, base, cp) in enumerate(segs):
            first_seg = (si == 0)
            last_seg = (si == len(segs) - 1)
            NS = cp // 128
            cblk = [(0, min(cp, 512))] + ([(512, cp)] if cp > 512 else [])
            xe = xeT[:, 0:16 * cp].rearrange("p (k c) -> p k c", c=cp)
            ye_bf = xeT[:, 0:NS * D].rearrange("p (s d) -> p s d", d=D)
            P.op(DVE, "tensor_scalar", dict(out=iot2[:, :cp], in0=iot[:, :cp], scalar1=float(base), scalar2=None, op0=ALU.add),
                 reads=[b_cm], writes=[b_iot2])

            def build_sel(i):
                selctr[0] += 1
                r_ = selctr[0] % NSL
                P.op(DVE, "tensor_scalar", dict(out=Sel[r_][:, :cp], in0=iot2[:, :cp],
                                                scalar1=rankm[:, i * 8 + e:i * 8 + e + 1], scalar2=None, op0=ALU.is_equal),
                     reads=[b_iot2, b_rankm], writes=[b_Sel[r_]])
                return Sel[r_], b_Sel[r_]
            nb = len(cblk)
            kgsz = 8 // nb
            for kg in range(16 // kgsz):
                for i in range(16):
                    xr = xctr % NX
                    xctr += 1
                    P.dma(SP, xtok[xr][:], xb_d[i * 128:(i + 1) * 128, :], reads=[b_xbd], writes=[b_xtok[xr]])
                    S_, bS_ = build_sel(i)
                    for kk in range(kgsz):
                        k = kgsz * kg + kk
                        for bi_, (a0, a1) in enumerate(cblk):
                            bi = nb * kk + bi_
                            lastmm = (kk == kgsz - 1 and bi_ == nb - 1)
                            P.op(PE, "matmul", dict(out=ps[:, bi, :a1 - a0], lhsT=xtok[xr][:, k * 128:(k + 1) * 128],
                                                    rhs=S_[:, a0:a1], start=(i == 0), stop=(i == 15)),
                                 reads=[b_xtok[xr], bS_], writes=[psb[bi]], sig=(i == 15 or lastmm))
                for kk in range(kgsz):
                    k = kgsz * kg + kk
                    for bi_, (a0, a1) in enumerate(cblk):
                        bi = nb * kk + bi_
                        if bi % 2 == 0:
                            P.op(ACT, "copy", dict(out=xe[:, k, a0:a1], in_=ps[:, bi, :a1 - a0]),
                                 reads=[psb[bi]], writes=[b_xeT])
                        else:
                            P.op(DVE, "tensor_copy", dict(out=xe[:, k, a0:a1], in_=ps[:, bi, :a1 - a0]),
                                 reads=[psb[bi]], writes=[b_xeT])
            for g in range(ngrp):
                r = wctr % NW
                wctr += 1
                f0 = g * GF
                P.dma(POOL, wgb[r][:], wg_d[e].rearrange("(k p) f -> p k f", p=128)[:, :, f0:f0 + GF], writes=[b_w[r]])
                P.dma(POOL, wub[r][:], wu_d[e].rearrange("(k p) f -> p k f", p=128)[:, :, f0:f0 + GF], writes=[b_w[r]])
                P.dma(POOL, wdb[r][:], wd_d[e, f0:f0 + GF, :].rearrange("(c p) d -> p c d", p=128), writes=[b_w[r]])
                hr = hc % 2
                hc += 1
                for c in range(GF // 128):
                    for (a0, a1) in cblk:
                        w_ = a1 - a0
                        pg, pbg = bank()
                        mm_acc(P, pg[:, :w_], pbg, [(wgb[r][:, k, c * 128:(c + 1) * 128], xe[:, k, a0:a1]) for k in range(16)],
                               [b_w[r], b_xeT])
                        pu, pbu = bank()
                        mm_acc(P, pu[:, :w_], pbu, [(wub[r][:, k, c * 128:(c + 1) * 128], xe[:, k, a0:a1]) for k in range(16)],
                               [b_w[r], b_xeT])
                        sr = sgc % 2
                        sgc += 1
                        P.op(ACT, "activation", dict(out=sg[sr][:, :w_], in_=pg[:, :w_], func=AF.Silu), reads=[pbg], writes=[b_sg[sr]])
                        P.op(DVE, "tensor_tensor", dict(out=hT[hr][:, c, a0:a1], in0=pu[:, :w_], in1=sg[sr][:, :w_], op=ALU.mult),
                             reads=[pbu, b_sg[sr]], writes=[b_hT[hr]])
                for s_ in range(NS):
                    for cb in range(4):
                        pa, pb = bank()
                        mm_acc(P, pa, pb, [(hT[hr][:, c, s_ * 128:(s_ + 1) * 128], wdb[r][:, c, cb * 512:(cb + 1) * 512])
                                           for c in range(GF // 128)], [b_hT[hr], b_w[r]])
                        dst = yeacc[:, s_, cb * 512:(cb + 1) * 512]
                        if g == 0:
                            P.op(DVE, "tensor_copy", dict(out=dst, in_=pa), reads=[pb], writes=[b_ye[s_]])
                        else:
                            P.op(DVE, "tensor_tensor", dict(out=dst, in0=pa, in1=dst, op=ALU.add),
                                 reads=[pb, b_ye[s_]], writes=[b_ye[s_]])
            for s_ in range(NS):
                P.op(ACT, "copy", dict(out=ye_bf[:, s_, :], in_=yeacc[:, s_, :]), reads=[b_ye[s_]], writes=[b_xeT])
            if last_seg:
                P.dma(SP, lnp[:], ln_d[:, 2:4, :], writes=[b_lnp, b_w[0], b_w[1], b_junk])
            for i in range(16):
                sr = i % 2
                S_, bS_ = build_sel(i)
                pq, pbq = bank()
                pT = pq.bitcast(BF16)[:, :NS * 128].rearrange("p (a b) -> p a b", b=128)
                for c in range(NS):
                    P.op(PE, "transpose", dict(out=pT[:, c, :], in_=S_[:, c * 128:(c + 1) * 128], identity=ident[:]),
                         reads=[bS_, b_c], writes=[pbq], sig=(c == NS - 1))
                P.op(ACT, "copy", dict(out=selT[sr][:, :NS, :], in_=pT), reads=[pbq], writes=[b_selT[sr]])
                yr = yc % 2
                yc += 1
                if not first_seg:
                    P.dma(SP, ybuf[yr][:], yacc_d[i * 128:(i + 1) * 128, :], reads=[b_yd[i]], writes=[b_ybuf[yr]])
                for cb in range(4):
                    pa, pb = bank()
                    mm_acc(P, pa, pb, [(selT[sr][:, c, :], ye_bf[:, c, cb * 512:(cb + 1) * 512]) for c in range(NS)],
                           [b_selT[sr], b_xeT])
                    dst = ybuf[yr][:, cb * 512:(cb + 1) * 512]
                    gsc = gates[:, i, e:e + 1]
                    if first_seg:
                        P.op(DVE, "tensor_scalar", dict(out=dst, in0=pa, scalar1=gsc, scalar2=None, op0=ALU.mult),
                             reads=[pb, b_gates], writes=[b_ybuf[yr]])
                    else:
                        P.op(DVE, "scalar_tensor_tensor", dict(out=dst, in0=pa, scalar=gsc, in1=dst,
                                                               op0=ALU.mult, op1=ALU.add),
                             reads=[pb, b_gates, b_ybuf[yr]], writes=[b_ybuf[yr]])
                if not last_seg:
                    P.dma(SP, yacc_d[i * 128:(i + 1) * 128, :], ybuf[yr][:], reads=[b_ybuf[yr]], writes=[b_yd[i]])
                else:
                    P.dma(SP, ls_x[:], x1s[i * 128:(i + 1) * 128, :], reads=[b_x1s], writes=[b_lsx])
                    P.op(DVE, "scalar_tensor_tensor", dict(out=ybuf[yr][:], in0=ls_x[:], scalar=ALPHA, in1=ybuf[yr][:],
                                                           op0=ALU.mult, op1=ALU.add),
                         reads=[b_lsx, b_ybuf[yr]], writes=[b_ybuf[yr]])
                    layer_norm(ybuf[yr][:], b_ybuf[yr], 2, ybuf[yr][:], b_ybuf[yr], i % 2)
                    P.dma(SP, y_out[i * 128:(i + 1) * 128, :], ybuf[yr][:], reads=[b_ybuf[yr]], writes=[b_out])

    P.final_wait(SP, [b_out])
    with nc.Block() as block:
        P.emit(block)
    return nc


def kernel(**inputs):
    inp = {k: np.asarray(v) for k, v in inputs.items()}
    x = inp["x"][0]
    xs = [np.ascontiguousarray(x[c * T:(c + 1) * T]) for c in range(NCORE)]
    cores = list(range(NCORE))
    for l in range(2):
        ncA = build_A()
        resA = run_bass_kernel_spmd(ncA, inputs_A(xs, l, inp), core_ids=cores).results
        moe = (l % 2 == 1)
        if not moe:
            ncB = build_B(False, 5632)
            resB = run_bass_kernel_spmd(ncB, inputs_B(xs, l, inp, resA, False), core_ids=cores).results
        else:
            ncBa = build_B(True, 7168, "a")
            resBa = run_bass_kernel_spmd(ncBa, inputs_B(xs, l, inp, resA, True), core_ids=cores).results
            cnt = np.stack([np.asarray(resBa[c]["cnt_o"])[0] for c in range(NCORE)])
            caps = [int(-(-int(round(float(cnt[:, e].max()))) // 128) * 128) for e in range(8)]
            ncBb = build_B(True, 7168, "b", caps)
            resB = run_bass_kernel_spmd(ncBb, inputs_Bb(l, inp, resBa), core_ids=cores).results
        xs = [np.asarray(resB[c]["y"], dtype=np.float32) for c in range(NCORE)]
    return np.concatenate(xs, axis=0)[None].astype(np.float32)
```

```python
import numpy as np
import concourse.bass as bass
import concourse.mybir as mybir
from concourse.bass_utils import run_bass_kernel_spmd

F32 = mybir.dt.float32
BF16 = mybir.dt.bfloat16
AF = mybir.ActivationFunctionType
ALU = mybir.AluOpType
AX = mybir.AxisListType

PE, ACT, DVE, POOL, SP = "tensor", "scalar", "vector", "gpsimd", "sync"
ENGS = (PE, ACT, DVE, POOL, SP)
SEM_ROLL = 30000

T = 2048
D = 2048
HT = 1024
NC_IN = 3712
KVROWS = 16 * 128 + 64
ALPHA = 4 ** 0.25
LN_EPS = 1e-5
RMS_EPS = 1e-6
NEG = -1e30
NCORE = 8
S = 16384
DEBUG_COUNTS = False
MOE_CAPMAX = 1024


class Buf:
    __slots__ = ("name", "w", "r", "dsem")

    def __init__(self, name=""):
        self.name = name
        self.w = None
        self.r = {}
        self.dsem = None


class Prog:
    def __init__(self, nc):
        self.nc = nc
        self.streams = {e: [] for e in ENGS}
        self.esems = {e: None for e in ENGS}
        self.waited = {}
        self.all_sems = []

    def _new_sem(self, name):
        h = self.nc.alloc_semaphore(name)
        rec = [h, 0]
        self.all_sems.append(rec)
        return rec

    def _eng_sem(self, eng):
        rec = self.esems[eng]
        if rec is None or rec[1] >= SEM_ROLL:
            rec = self._new_sem(f"e_{eng}_{len(self.all_sems)}")
            self.esems[eng] = rec
        return rec

    def _collect(self, eng, reads, writes, is_dma):
        waits = {}

        def need(tok, same_ok):
            if tok is None:
                return
            rec, val, src, dma = tok
            if dma:
                val = rec[1]
            elif src == eng and not is_dma:
                if eng == PE or same_ok:
                    return
            k = id(rec)
            if waits.get(k, (None, -1))[1] < val:
                waits[k] = (rec, val)

        for b in reads:
            need(b.w, False)
        for b in writes:
            need(b.w, True)
            for t in b.r.values():
                need(t, True)
        out = []
        for k, (rec, val) in waits.items():
            key = (eng, k)
            if self.waited.get(key, -1) >= val:
                continue
            self.waited[key] = val
            out.append((rec[0], val))
        return out

    def op(self, eng, name, args, reads=(), writes=(), sig=True, own_sem_inc=None):
        def fn(e, name=name, args=args):
            return getattr(e, name)(**args)
        waits = self._collect(eng, reads, writes, False)
        tok = None
        if own_sem_inc is not None:
            rec = self._new_sem(f"own_{len(self.all_sems)}")
            rec[1] += own_sem_inc
            tok = (rec, rec[1], eng, True)
            self.streams[eng].append((waits, fn, (rec[0], own_sem_inc)))
        elif sig:
            rec = self._eng_sem(eng)
            rec[1] += 1
            tok = (rec, rec[1], eng, False)
            self.streams[eng].append((waits, fn, (rec[0], 1)))
        else:
            self.streams[eng].append((waits, fn, None))
        if tok is not None:
            for b in writes:
                b.w = tok
                b.r = {}
            for b in reads:
                b.r[id(rec)] = tok
        return tok

    def dma(self, eng, out, in_, reads=(), writes=(), sembuf=None, **kw):
        waits = self._collect(eng, reads, writes, True)
        sb = sembuf if sembuf is not None else (writes[0] if writes else reads[0])
        if sb.dsem is None:
            sb.dsem = self._new_sem(f"d_{sb.name}_{len(self.all_sems)}")
        rec = sb.dsem
        rec[1] += 16
        tok = (rec, rec[1], eng, True)

        def fn(e, out=out, in_=in_, kw=kw):
            return e.dma_start(out=out, in_=in_, **kw)
        self.streams[eng].append((waits, fn, (rec[0], 16)))
        for b in writes:
            b.w = tok
            b.r = {}
        for b in reads:
            b.r[id(rec)] = tok
        return tok

    def barrier(self):
        for eng in ENGS:
            waits = []
            for rec in self.all_sems:
                if rec[1] > 0 and self.waited.get((eng, id(rec)), -1) < rec[1]:
                    self.waited[(eng, id(rec))] = rec[1]
                    waits.append((rec[0], rec[1]))
            if waits:
                self.streams[eng].append((waits, None, None))

    def final_wait(self, eng, bufs):
        waits = self._collect(eng, bufs, (), True)
        self.streams[eng].append((waits, None, None))

    def emit(self, block):
        nc = self.nc

        def make(eng):
            stream = self.streams[eng]

            def body(e):
                for waits, fn, inc in stream:
                    for h, v in waits:
                        e.wait_ge(h, v)
                    if fn is not None:
                        ins = fn(e)
                        if inc is not None:
                            ins.then_inc(inc[0], inc[1])
            return body
        block.tensor(make(PE))
        block.scalar(make(ACT))
        block.vector(make(DVE))
        block.gpsimd(make(POOL))
        block.sync(make(SP))


SB_BASE = 16512
SB_END = 229376
_DT_BYTES = {F32: 4, BF16: 2}


class SBAlloc:
    def __init__(self, nc):
        self.nc = nc
        self.off = SB_BASE
        self.n = 0

    def alloc(self, name, shape, dtype):
        size = 1
        for d in shape[1:]:
            size *= d
        size *= _DT_BYTES[dtype]
        size = (size + 31) // 32 * 32
        assert self.off + size <= SB_END, f"SBUF overflow at {name}: {self.off + size}"
        t = self.nc.alloc_sbuf_tensor_at(f"{name}_{self.n}", list(shape), dtype, offset=self.off)
        self.n += 1
        self.off += size
        return t

    def mark(self):
        return self.off

    def reset(self, m):
        self.off = m


def mm_acc(P, out_ap, out_buf, pairs, reads):
    n = len(pairs)
    for i, (l, r) in enumerate(pairs):
        P.op(PE, "matmul", dict(out=out_ap, lhsT=l, rhs=r, start=(i == 0), stop=(i == n - 1)),
             reads=reads, writes=[out_buf], sig=(i == n - 1))


import ml_dtypes

BF = ml_dtypes.bfloat16
ROPE_THETA = 10000.0


def rope_np(dim):
    inv = np.power(np.float32(ROPE_THETA), -(np.arange(0, dim, 2, dtype=np.float32) / np.float32(dim))).astype(np.float32)
    ang = np.arange(S, dtype=np.float32)[:, None] * inv[None, :]
    return np.cos(ang).astype(np.float32), np.sin(ang).astype(np.float32)


def rope_tables_fm(dim):
    c, s = rope_np(dim)
    cf = np.concatenate([c, c], axis=1).T
    sf = np.concatenate([-s, s], axis=1).T
    return np.ascontiguousarray(cf), np.ascontiguousarray(sf)


def perm_half(w, dim):
    n = w.shape[1] // dim
    idx = np.concatenate([(np.arange(dim) + dim // 2) % dim + b * dim for b in range(n)])
    return w[:, idx]


def prep_w_in(w):
    c_q = w[:, 0:512]
    c_kv = w[:, 512:768]
    k_pe = w[:, 768:832]
    q_s = w[:, 832:1856]
    k_s = w[:, 1856:2112]
    v_s = w[:, 2112:2368]
    cols = [c_q, c_kv, k_pe, perm_half(k_pe, 64)]
    qsp = perm_half(q_s, 128)
    for h in range(8):
        cols += [q_s[:, h * 128:(h + 1) * 128], qsp[:, h * 128:(h + 1) * 128]]
    ksp = perm_half(k_s, 128)
    for h in range(2):
        cols += [k_s[:, h * 128:(h + 1) * 128], ksp[:, h * 128:(h + 1) * 128]]
    cols.append(v_s)
    out = np.ascontiguousarray(np.concatenate(cols, axis=1))
    assert out.shape[1] == 3712
    return out


def prep_w_qb(w):
    cols = []
    for h in range(8):
        blk = w[:, h * 192:(h + 1) * 192]
        pe = blk[:, 128:192]
        cols += [blk[:, :128], pe, perm_half(pe, 64)]
    return np.ascontiguousarray(np.concatenate(cols, axis=1))


def prep_w_kvb(w):
    ks = [w[:, h * 256:h * 256 + 128] for h in range(8)]
    vs = [w[:, h * 256 + 128:(h + 1) * 256] for h in range(8)]
    return np.ascontiguousarray(np.concatenate(ks + vs, axis=1))


def fm_vec(g, nchunk):
    return np.ascontiguousarray(g.reshape(nchunk, 128).T)


_TABS = {}


def tables():
    if not _TABS:
        _TABS["m"] = rope_tables_fm(64)
        _TABS["s"] = rope_tables_fm(128)
    return _TABS


def inputs_A(xs, l, inp):
    tb = tables()
    w_in = prep_w_in(inp["w_in"][l])
    w_qb = prep_w_qb(inp["w_qb"][l])
    w_kvb = prep_w_kvb(inp["w_kvb"][l])
    g_cq = fm_vec(inp["g_cq"][l], 4)
    g_ckv = fm_vec(inp["g_ckv"][l], 2)
    maps = []
    for c in range(NCORE):
        sl = slice(c * T, (c + 1) * T)
        maps.append({
            "x": np.ascontiguousarray(xs[c]), "w_in": w_in, "w_qb": w_qb, "w_kvb": w_kvb,
            "g_cq": g_cq, "g_ckv": g_ckv,
            "cos_m": np.ascontiguousarray(tb["m"][0][:, sl]), "sin_m": np.ascontiguousarray(tb["m"][1][:, sl]),
            "cos_s": np.ascontiguousarray(tb["s"][0][:, sl]), "sin_s": np.ascontiguousarray(tb["s"][1][:, sl]),
        })
    return maps


def swa_masks(core):
    qi = np.arange(128)[:, None]
    ki = np.arange(128)[None, :]
    prev = np.where(qi <= ki, 0.0, NEG).astype(np.float32)
    mid = np.zeros((128, 128), np.float32)
    nxt = np.where(ki <= qi, 0.0, NEG).astype(np.float32)
    full = np.concatenate([prev, mid, nxt], axis=1)
    allneg = np.full((128, 128), NEG, np.float32)
    first = full.copy()
    last = full.copy()
    if core == 0:
        first[:, :128] = allneg
    if core == NCORE - 1:
        last[:, 256:] = allneg
    return np.ascontiguousarray(np.stack([first, full, last], axis=1))


def inputs_Bb(l, inp, resBa):
    j = l // 2
    ln = np.stack([inp["ln1_g"][l], inp["ln1_b"][l], inp["ln2_g"][l], inp["ln2_b"][l]], 0)
    ln_bc = np.ascontiguousarray(np.broadcast_to(ln[None], (128, 4, 2048))).astype(np.float32)
    iota_row = np.ascontiguousarray(np.broadcast_to(np.arange(MOE_CAPMAX, dtype=np.float32)[None, :], (128, MOE_CAPMAX)))
    maps = []
    for c in range(NCORE):
        maps.append({"ln_bc": ln_bc, "iota_row": iota_row,
                     "wg": inp["moe_wg"][j], "wu": inp["moe_wu"][j], "wd": inp["moe_wd"][j],
                     "x1s_in": resBa[c]["x1s"], "xb_in": resBa[c]["xb_d"],
                     "gates_in": resBa[c]["gates_o"], "rank_in": resBa[c]["rank_o"]})
    return maps


def inputs_B(xs, l, inp, resA, moe):
    kv_all = np.ascontiguousarray(np.concatenate([resA[c]["kv_out"] for c in range(NCORE)], axis=0))
    sink_bc = np.ascontiguousarray(np.broadcast_to(inp["sink"][l][None, :], (128, 8))).astype(np.float32)
    g_mla = fm_vec(inp["g_out_mla"][l], 8)
    g_swa = np.ascontiguousarray(np.broadcast_to(inp["g_out_swa"][l][None, :], (128, 1024))).astype(np.float32)
    ln = np.stack([inp["ln1_g"][l], inp["ln1_b"][l], inp["ln2_g"][l], inp["ln2_b"][l]], 0)
    ln_bc = np.ascontiguousarray(np.broadcast_to(ln[None], (128, 4, 2048))).astype(np.float32)
    j = l // 2
    maps = []
    for c in range(NCORE):
        ks = resA[c]["ks_out"]
        vs = resA[c]["vs_out"]
        kprev = resA[c - 1]["ks_out"][:, -128:] if c > 0 else np.zeros((256, 128), ks.dtype)
        knext = resA[c + 1]["ks_out"][:, :128] if c < NCORE - 1 else np.zeros((256, 128), ks.dtype)
        vprev = resA[c - 1]["vs_out"][-128:] if c > 0 else np.zeros((128, 256), vs.dtype)
        vnext = resA[c + 1]["vs_out"][:128] if c < NCORE - 1 else np.zeros((128, 256), vs.dtype)
        m = {
            "x": np.ascontiguousarray(xs[c]), "kv_all": kv_all,
            "qn": resA[c]["qn_out"], "qpe": resA[c]["qpe_out"], "qs": resA[c]["qs_out"],
            "ks_ext": np.ascontiguousarray(np.concatenate([kprev, ks, knext], axis=1)),
            "vs_ext": np.ascontiguousarray(np.concatenate([vprev, vs, vnext], axis=0)),
            "masks": swa_masks(c), "sink_bc": sink_bc, "g_mla": g_mla, "g_swa_bc": g_swa,
            "w_out": inp["w_out"][l], "ln_bc": ln_bc,
        }
        if moe:
            lmat = (np.arange(128)[:, None] < np.arange(128)[None, :]).astype(np.float32).astype(BF)
            m.update({"w_router": inp["router_w"][j], "lmat": lmat})
        else:
            m.update({"wg": inp["dense_wg"][j], "wu": inp["dense_wu"][j], "wd": inp["dense_wd"][j]})
        maps.append(m)
    return maps


def build_A():
    nc = bass.Bass("TRN2", target_bir_lowering=False)
    x = nc.dram_tensor("x", [T, D], F32, kind="ExternalInput").ap()
    w_in = nc.dram_tensor("w_in", [D, NC_IN], F32, kind="ExternalInput").ap()
    w_qb = nc.dram_tensor("w_qb", [512, 2048], F32, kind="ExternalInput").ap()
    w_kvb = nc.dram_tensor("w_kvb", [256, 2048], F32, kind="ExternalInput").ap()
    g_cq = nc.dram_tensor("g_cq", [128, 4], F32, kind="ExternalInput").ap()
    g_ckv = nc.dram_tensor("g_ckv", [128, 2], F32, kind="ExternalInput").ap()
    cos_m = nc.dram_tensor("cos_m", [64, T], F32, kind="ExternalInput").ap()
    sin_m = nc.dram_tensor("sin_m", [64, T], F32, kind="ExternalInput").ap()
    cos_s = nc.dram_tensor("cos_s", [128, T], F32, kind="ExternalInput").ap()
    sin_s = nc.dram_tensor("sin_s", [128, T], F32, kind="ExternalInput").ap()
    kv_out = nc.dram_tensor("kv_out", [16 * 128 + 64, T], BF16, kind="ExternalOutput").ap()
    qn_out = nc.dram_tensor("qn_out", [8 * 128, T], BF16, kind="ExternalOutput").ap()
    qpe_out = nc.dram_tensor("qpe_out", [8 * 64, T], BF16, kind="ExternalOutput").ap()
    qs_out = nc.dram_tensor("qs_out", [8 * 128, T], BF16, kind="ExternalOutput").ap()
    ks_out = nc.dram_tensor("ks_out", [2 * 128, T], BF16, kind="ExternalOutput").ap()
    vs_out = nc.dram_tensor("vs_out", [T, 256], BF16, kind="ExternalOutput").ap()

    P = Prog(nc)
    sb = SBAlloc(nc)
    ps = nc.alloc_psum_tensor("ps", [128, 8, 512], F32)
    psb = [Buf(f"ps{i}") for i in range(8)]
    pctr = [0]

    def bank():
        i = pctr[0] % 8
        pctr[0] += 1
        return ps[:, i, :], psb[i]

    b_out = Buf("out")

    ident = sb.alloc("ident", [128, 128], BF16)
    ones = sb.alloc("ones", [128, 128], BF16)
    gq = sb.alloc("gq", [128, 4], F32)
    gkv = sb.alloc("gkv", [128, 2], F32)
    b_c = Buf("consts")
    P.op(POOL, "memset", dict(ap=ident[:], constant=0.0), writes=[b_c])
    P.op(POOL, "affine_select", dict(out=ident[:], in_=ident[:], pattern=[[-1, 128]],
                                     compare_op=ALU.not_equal, fill=1.0, base=0,
                                     channel_multiplier=1), reads=[b_c], writes=[b_c])
    P.op(POOL, "memset", dict(ap=ones[:], constant=1.0), reads=[b_c], writes=[b_c])
    b_g = Buf("g")
    P.dma(SP, gq[:], g_cq[:, :], writes=[b_g])
    P.dma(SP, gkv[:], g_ckv[:, :], writes=[b_g])

    cosm = sb.alloc("cosm", [64, HT], F32)
    sinm = sb.alloc("sinm", [64, HT], F32)
    coss = sb.alloc("coss", [128, HT], F32)
    sins = sb.alloc("sins", [128, HT], F32)
    cqn = sb.alloc("cqn", [128, 4, HT], BF16)
    ckvn = sb.alloc("ckvn", [128, 2, HT], BF16)
    kpeT = sb.alloc("kpeT", [64, HT], BF16)
    b_tab = Buf("tab")
    b_cqn = [Buf(f"cqn{i}") for i in range(2)]
    b_ckvn = [Buf(f"ckvn{i}") for i in range(2)]
    b_kpe = Buf("kpe")
    m_phase = sb.mark()

    w_in_v = w_in.rearrange("(k p) c -> p k c", p=128)
    groups = [(0, 512), (512, 896)] + [(896 + 512 * i, 896 + 512 * (i + 1)) for i in range(4)] + \
             [(2944, 3456), (3456, 3712)]

    for half in range(2):
        t0 = half * HT
        sb.reset(m_phase)
        xT = sb.alloc("xT", [128, 16, HT], BF16)
        b_xT = [Buf(f"xT{i}") for i in range(HT // 128)]
        xt = [sb.alloc(f"xt{r}", [128, D], BF16) for r in range(2)]
        b_xt = [Buf(f"xt{r}") for r in range(2)]
        wring = [sb.alloc(f"wr{r}", [128, 16, 512], BF16) for r in range(2)]
        b_wr = [Buf(f"wr{r}") for r in range(2)]
        sq = sb.alloc("sq", [128, 4, 512], BF16)
        b_sq = Buf("sq")
        rstd = sb.alloc("rstd", [128, 512], F32)
        b_rstd = Buf("rstd")
        tmp = [sb.alloc(f"tmp{r}", [128, 512], F32) for r in range(4)]
        b_tmp = [Buf(f"tmp{r}") for r in range(4)]
        stg = [sb.alloc(f"stg{r}", [128, HT], BF16) for r in range(4)]
        b_stg = [Buf(f"stg{r}") for r in range(4)]
        vstg = sb.alloc("vstg", [128, HT // 128, 256], BF16)
        b_vstg = Buf("vstg")

        P.dma(SP, cosm[:], cos_m[:, t0:t0 + HT], writes=[b_tab])
        P.dma(SP, sinm[:], sin_m[:, t0:t0 + HT], writes=[b_tab])
        P.dma(SP, coss[:], cos_s[:, t0:t0 + HT], writes=[b_tab])
        P.dma(SP, sins[:], sin_s[:, t0:t0 + HT], writes=[b_tab])

        for i in range(HT // 128):
            r = i % 2
            P.dma(POOL, xt[r][:], x[t0 + i * 128:t0 + (i + 1) * 128, :], writes=[b_xt[r]])
            j = (pctr[0] // 2) % 4
            pctr[0] += 2
            pT = ps[:, 2 * j:2 * j + 2, :].bitcast(BF16).rearrange("p a (b c) -> p (a b) c", c=128)
            pbufs = [psb[2 * j], psb[2 * j + 1]]
            for k in range(16):
                P.op(PE, "transpose", dict(out=pT[:, k, :], in_=xt[r][:, k * 128:(k + 1) * 128], identity=ident[:]),
                     reads=[b_xt[r], b_c], writes=pbufs, sig=(k == 15))
            eng = ACT if i % 2 == 0 else DVE
            if eng == ACT:
                P.op(ACT, "copy", dict(out=xT[:, :, i * 128:(i + 1) * 128], in_=pT[:, :, :]),
                     reads=pbufs, writes=[b_xT[i]])
            else:
                P.op(DVE, "tensor_copy", dict(out=xT[:, :, i * 128:(i + 1) * 128], in_=pT[:, :, :]),
                     reads=pbufs, writes=[b_xT[i]])

        def rms_block(chunks, nfeat, gvec, dst, dstbuf, tb):
            n = len(chunks)
            for c, (pa, pb) in enumerate(chunks):
                P.op(ACT, "activation", dict(out=sq[:, c, :], in_=pa, func=AF.Square),
                     reads=[pb], writes=[b_sq])
            sa, sbf = bank()
            mm_acc(P, sa, sbf, [(ones[:], sq[:, c, :]) for c in range(n)], [b_sq, b_c])
            P.op(DVE, "tensor_scalar", dict(out=rstd[:], in0=sa, scalar1=1.0 / nfeat, scalar2=RMS_EPS,
                                            op0=ALU.mult, op1=ALU.add), reads=[sbf], writes=[b_rstd])
            P.op(ACT, "activation", dict(out=rstd[:], in_=rstd[:], func=AF.Sqrt), reads=[b_rstd], writes=[b_rstd])
            P.op(DVE, "reciprocal", dict(out=rstd[:], in_=rstd[:]), reads=[b_rstd], writes=[b_rstd])
            for c, (pa, pb) in enumerate(chunks):
                P.op(DVE, "scalar_tensor_tensor", dict(
                    out=dst[:, c, tb * 512:(tb + 1) * 512], in0=pa, scalar=gvec[:, c:c + 1], in1=rstd[:],
                    op0=ALU.mult, op1=ALU.mult), reads=[pb, b_rstd, b_g], writes=[dstbuf])

        tctr = [0]

        def rope_block(pa, pb_a, pu, pb_u, cosT, sinT, nrow, dst_ap, dst_buf, tb):
            i0 = tctr[0] % 4
            i1 = (tctr[0] + 1) % 4
            tctr[0] += 2
            cs = slice(tb * 512, (tb + 1) * 512)
            P.op(DVE, "tensor_tensor", dict(out=tmp[i0][:nrow, :], in0=pa, in1=cosT[:nrow, cs], op=ALU.mult),
                 reads=[pb_a, b_tab], writes=[b_tmp[i0]])
            P.op(DVE, "tensor_tensor", dict(out=tmp[i1][:nrow, :], in0=pu, in1=sinT[:nrow, cs], op=ALU.mult),
                 reads=[pb_u, b_tab], writes=[b_tmp[i1]])
            P.op(POOL, "tensor_tensor", dict(out=dst_ap, in0=tmp[i0][:nrow, :], in1=tmp[i1][:nrow, :], op=ALU.add),
                 reads=[b_tmp[i0], b_tmp[i1]], writes=[dst_buf])

        sctr = [0]
        for gi, (c0, c1) in enumerate(groups):
            r = gi % 2
            ncol = c1 - c0
            P.dma(POOL, wring[r][:, :, :ncol], w_in_v[:, :, c0:c1], writes=[b_wr[r]])
            W = wring[r]
            if gi == 7:
                for i in range(HT // 128):
                    pa, pb = bank()
                    mm_acc(P, pa[:, :256], pb,
                           [(xT[:, k, i * 128:(i + 1) * 128], W[:, k, 0:256]) for k in range(16)],
                           [b_xT[i], b_wr[r]])
                    P.op(ACT, "copy", dict(out=vstg[:, i, :], in_=pa[:, :256]),
                         reads=[pb], writes=[b_vstg])
                P.dma(SP, vs_out[t0:t0 + HT, :].rearrange("(i p) c -> p i c", p=128), vstg[:],
                      reads=[b_vstg], writes=[b_out])
                continue
            if gi >= 2:
                sidx = [sctr[0] % 4, (sctr[0] + 1) % 4]
                sctr[0] += 2
            for tb in range(HT // 512):
                xb = [b_xT[4 * tb + q] for q in range(4)]
                cs = slice(tb * 512, (tb + 1) * 512)

                def chunk(col0, ncols_):
                    pa, pb = bank()
                    mm_acc(P, pa[:ncols_, :], pb,
                           [(W[:, k, col0:col0 + ncols_], xT[:, k, cs]) for k in range(16)],
                           xb + [b_wr[r]])
                    return pa, pb
                if gi == 0:
                    chunks = [chunk(c * 128, 128) for c in range(4)]
                    rms_block(chunks, 512, gq, cqn, b_cqn[tb], tb)
                elif gi == 1:
                    chunks = [chunk(c * 128, 128) for c in range(2)]
                    rms_block(chunks, 256, gkv, ckvn, b_ckvn[tb], tb)
                    pa, pb = chunk(256, 64)
                    pu, pbu = chunk(320, 64)
                    rope_block(pa[:64, :], pb, pu[:64, :], pbu, cosm, sinm, 64, kpeT[:, cs], b_kpe, tb)
                else:
                    for hh in range(2):
                        pa, pb = chunk(hh * 256, 128)
                        pu, pbu = chunk(hh * 256 + 128, 128)
                        rope_block(pa, pb, pu, pbu, coss, sins, 128, stg[sidx[hh]][:, cs], b_stg[sidx[hh]], tb)
            if gi >= 2:
                for hh in range(2):
                    if gi <= 5:
                        h = (gi - 2) * 2 + hh
                        dst = qs_out[h * 128:(h + 1) * 128, t0:t0 + HT]
                    else:
                        dst = ks_out[hh * 128:(hh + 1) * 128, t0:t0 + HT]
                    P.dma(SP, dst, stg[sidx[hh]][:], reads=[b_stg[sidx[hh]]], writes=[b_out])

        P.dma(SP, kv_out[16 * 128:16 * 128 + 64, t0:t0 + HT], kpeT[:], reads=[b_kpe], writes=[b_out])

        P.barrier()
        sb.reset(m_phase)
        wqb = sb.alloc("wqb", [128, 4, 2048], BF16)
        wkvb = sb.alloc("wkvb", [128, 2, 2048], BF16)
        b_wqb, b_wkvb = Buf("wqb"), Buf("wkvb")
        qn = sb.alloc("qn", [128, 8, HT], BF16)
        qpe = sb.alloc("qpe", [64, 8, HT], BF16)
        KT = sb.alloc("KT", [128, 8, HT], BF16)
        Vt = sb.alloc("Vt", [128, HT // 128, 1024], BF16)
        b_qn, b_qpe, b_KT, b_Vt = Buf("qn"), Buf("qpe"), Buf("KT"), Buf("Vt")
        tmp = [sb.alloc(f"tmpb{r}", [128, 512], F32) for r in range(4)]
        b_tmp = [Buf(f"tmpb{r}") for r in range(4)]
        P.dma(POOL, wqb[:], w_qb.rearrange("(k p) c -> p k c", p=128), writes=[b_wqb])
        P.dma(POOL, wkvb[:], w_kvb.rearrange("(k p) c -> p k c", p=128), writes=[b_wkvb])
        for h in range(8):
            for tb in range(HT // 512):
                cs = slice(tb * 512, (tb + 1) * 512)
                pa, pb = bank()
                mm_acc(P, pa, pb, [(wqb[:, c, 256 * h:256 * h + 128], cqn[:, c, cs]) for c in range(4)],
                       [b_wqb, b_cqn[tb]])
                P.op(ACT, "copy", dict(out=qn[:, h, cs], in_=pa), reads=[pb], writes=[b_qn])
                pa, pb = bank()
                mm_acc(P, pa[:64, :], pb, [(wqb[:, c, 256 * h + 128:256 * h + 192], cqn[:, c, cs]) for c in range(4)],
                       [b_wqb, b_cqn[tb]])
                pu, pbu = bank()
                mm_acc(P, pu[:64, :], pbu, [(wqb[:, c, 256 * h + 192:256 * h + 256], cqn[:, c, cs]) for c in range(4)],
                       [b_wqb, b_cqn[tb]])
                rope_block(pa[:64, :], pb, pu[:64, :], pbu, cosm, sinm, 64, qpe[:, h, cs], b_qpe, tb)
                pa, pb = bank()
                mm_acc(P, pa, pb, [(wkvb[:, c, 128 * h:128 * h + 128], ckvn[:, c, cs]) for c in range(2)],
                       [b_wkvb, b_ckvn[tb]])
                P.op(ACT, "copy", dict(out=KT[:, h, cs], in_=pa), reads=[pb], writes=[b_KT])
        for i in range(HT // 128):
            tb = i // 4
            for hf in range(2):
                pa, pb = bank()
                mm_acc(P, pa, pb,
                       [(ckvn[:, c, i * 128:(i + 1) * 128], wkvb[:, c, 1024 + hf * 512:1024 + (hf + 1) * 512])
                        for c in range(2)], [b_wkvb, b_ckvn[tb]])
                P.op(DVE, "tensor_copy", dict(out=Vt[:, i, hf * 512:(hf + 1) * 512], in_=pa),
                     reads=[pb], writes=[b_Vt])
        P.dma(SP, qn_out.rearrange("(h p) t -> p h t", p=128)[:, :, t0:t0 + HT], qn[:], reads=[b_qn], writes=[b_out])
        P.dma(SP, qpe_out.rearrange("(h p) t -> p h t", p=64)[:, :, t0:t0 + HT], qpe[:], reads=[b_qpe], writes=[b_out])
        for h in range(8):
            P.dma(SP, kv_out[(2 * h) * 128:(2 * h + 1) * 128, t0:t0 + HT], KT[:, h, :], reads=[b_KT], writes=[b_out])
            dst = kv_out[(2 * h + 1) * 128:(2 * h + 2) * 128, :].rearrange("p (i d) -> p i d", d=128)
            P.dma(SP, dst[:, half * (HT // 128):(half + 1) * (HT // 128), :], Vt[:, :, h * 128:(h + 1) * 128],
                  reads=[b_Vt], writes=[b_out])
        P.barrier()

    P.final_wait(SP, [b_out])
    with nc.Block() as block:
        P.emit(block)
    return nc


def build_B(moe, F, stage="all", caps=None):
    nc = bass.Bass("TRN2", target_bir_lowering=False)

    def din(name, shape, dt=F32):
        return nc.dram_tensor(name, list(shape), dt, kind="ExternalInput").ap()
    ln_d = din("ln_bc", [128, 4, D])
    if stage != "b":
        x = din("x", [T, D])
        kv_all = din("kv_all", [8 * KVROWS, T], BF16)
        qn_d = din("qn", [1024, T], BF16)
        qpe_d = din("qpe", [512, T], BF16)
        qs_d = din("qs", [1024, T], BF16)
        ks_d = din("ks_ext", [256, T + 256], BF16)
        vs_d = din("vs_ext", [T + 256, 256], BF16)
        masks_d = din("masks", [128, 3, 384])
        sink_d = din("sink_bc", [128, 8])
        gmla_d = din("g_mla", [128, 8])
        gswa_d = din("g_swa_bc", [128, 1024])
        wout_d = din("w_out", [D, D])
    if stage == "a":
        wr_d = din("w_router", [D, 8])
        lmat_d = din("lmat", [128, 128], BF16)
    if stage == "b":
        iota_d = din("iota_row", [128, MOE_CAPMAX])
        wg_d = din("wg", [8, D, F])
        wu_d = din("wu", [8, D, F])
        wd_d = din("wd", [8, F, D])
        x1s = din("x1s_in", [T, D])
        xb_d = din("xb_in", [T, D], BF16)
        gates_i = din("gates_in", [128, 128])
        rank_i = din("rank_in", [128, 128])
    if stage == "all":
        wg_d = din("wg", [1, D, F])
        wu_d = din("wu", [1, D, F])
        wd_d = din("wd", [1, F, D])
    if stage == "a":
        x1s = nc.dram_tensor("x1s", [T, D], F32, kind="ExternalOutput").ap()
        xb_d = nc.dram_tensor("xb_d", [T, D], BF16, kind="ExternalOutput").ap()
        gates_o = nc.dram_tensor("gates_o", [128, 128], F32, kind="ExternalOutput").ap()
        rank_o = nc.dram_tensor("rank_o", [128, 128], F32, kind="ExternalOutput").ap()
        cnt_o = nc.dram_tensor("cnt_o", [128, 8], F32, kind="ExternalOutput").ap()
    else:
        y_out = nc.dram_tensor("y", [T, D], F32, kind="ExternalOutput").ap()
    if stage == "all":
        x1s = nc.dram_tensor("x1s", [T, D], F32).ap()
        xb_d = nc.dram_tensor("xb_d", [T, D], BF16).ap()
    x1T_d = nc.dram_tensor("x1T_d", [128, 16, T], BF16).ap()
    yacc_d = nc.dram_tensor("yacc_d", [T, D], F32).ap()
    b_xbd = Buf("xb_d")

    P = Prog(nc)
    sb = SBAlloc(nc)
    ps = nc.alloc_psum_tensor("ps", [128, 8, 512], F32)
    psb = [Buf(f"ps{i}") for i in range(8)]
    pctr = [0]
    prange = [0, 8]

    def bank():
        lo, hi = prange
        i = lo + pctr[0] % (hi - lo)
        pctr[0] += 1
        return ps[:, i, :], psb[i]

    b_out = Buf("out")
    b_x1s = Buf("x1s")
    b_x1T = Buf("x1T_d")

    ident = sb.alloc("ident", [128, 128], BF16)
    ones = sb.alloc("ones", [128, 128], BF16)
    sink = sb.alloc("sink", [128, 8], F32)
    gmla = sb.alloc("gmla", [128, 8], F32)
    b_c = Buf("consts")
    P.op(POOL, "memset", dict(ap=ident[:], constant=0.0), writes=[b_c])
    P.op(POOL, "affine_select", dict(out=ident[:], in_=ident[:], pattern=[[-1, 128]],
                                     compare_op=ALU.not_equal, fill=1.0, base=0,
                                     channel_multiplier=1), reads=[b_c], writes=[b_c])
    P.op(POOL, "memset", dict(ap=ones[:], constant=1.0), reads=[b_c], writes=[b_c])
    b_p = Buf("params")
    if stage != "b":
        P.dma(SP, sink[:], sink_d[:, :], writes=[b_p])
        P.dma(SP, gmla[:], gmla_d[:, :], writes=[b_p])
    gates = sb.alloc("gates", [128, 16, 8], F32)
    b_gates = Buf("gates")
    selm = sb.alloc("selm", [128, 128], F32)
    b_selm = Buf("selm")
    if stage == "a":
        wr32 = sb.alloc("wr32", [128, 16, 8], F32)
        wrh = sb.alloc("wrh", [128, 16, 8], BF16)
        wrl = sb.alloc("wrl", [128, 16, 8], BF16)
        b_wr = Buf("wr")
        P.dma(SP, wr32[:], wr_d.rearrange("(k p) e -> p k e", p=128), writes=[b_wr])
        P.op(ACT, "copy", dict(out=wrh[:], in_=wr32[:]), reads=[b_wr], writes=[b_wr])
        P.op(DVE, "tensor_tensor", dict(out=wrl[:], in0=wr32[:], in1=wrh[:], op=ALU.subtract), reads=[b_wr], writes=[b_wr])
    m_const = sb.mark()

    mT = sb.alloc("mT", [128, 16, T], BF16)
    b_mT = [Buf(f"mT{i}") for i in range(16)]
    m_mT = sb.mark()

    def small_pool(prefix, n, width=1):
        ts = [sb.alloc(f"{prefix}{i}", [128, width], F32) for i in range(n)]
        bs = [Buf(f"{prefix}{i}") for i in range(n)]
        return ts, bs

    def layer_norm(src, b_src, gi, dst, b_dst, slot):
        s = [ls[8 * slot + q] for q in range(8)]
        bs = [b_ls[8 * slot + q] for q in range(8)]
        P.op(ACT, "activation", dict(out=junk[:], in_=src, func=AF.Identity, accum_out=s[0][:]),
             reads=[b_src], writes=[b_junk, bs[0]])
        P.op(ACT, "activation", dict(out=junk[:], in_=src, func=AF.Square, accum_out=s[1][:]),
             reads=[b_src], writes=[b_junk, bs[1]])
        P.op(DVE, "tensor_scalar", dict(out=s[2][:], in0=s[0][:], scalar1=1.0 / D, scalar2=None, op0=ALU.mult),
             reads=[bs[0]], writes=[bs[2]])
        P.op(DVE, "tensor_tensor", dict(out=s[3][:], in0=s[2][:], in1=s[2][:], op=ALU.mult),
             reads=[bs[2]], writes=[bs[3]])
        P.op(DVE, "scalar_tensor_tensor", dict(out=s[4][:], in0=s[1][:], scalar=1.0 / D, in1=s[3][:],
                                               op0=ALU.mult, op1=ALU.subtract),
             reads=[bs[1], bs[3]], writes=[bs[4]])
        P.op(DVE, "tensor_scalar", dict(out=s[4][:], in0=s[4][:], scalar1=LN_EPS, scalar2=None, op0=ALU.add),
             reads=[bs[4]], writes=[bs[4]])
        P.op(ACT, "activation", dict(out=s[5][:], in_=s[4][:], func=AF.Sqrt), reads=[bs[4]], writes=[bs[5]])
        P.op(DVE, "reciprocal", dict(out=s[6][:], in_=s[5][:]), reads=[bs[5]], writes=[bs[6]])
        P.op(DVE, "scalar_tensor_tensor", dict(out=s[7][:], in0=s[2][:], scalar=-1.0, in1=s[6][:],
                                               op0=ALU.mult, op1=ALU.mult),
             reads=[bs[2], bs[6]], writes=[bs[7]])
        P.op(ACT, "activation", dict(out=dst, in_=src, func=AF.Identity, scale=s[6][:], bias=s[7][:]),
             reads=[b_src, bs[6], bs[7]], writes=[b_dst])
        P.op(DVE, "tensor_tensor", dict(out=dst, in0=dst, in1=lnp[:, 0, :], op=ALU.mult),
             reads=[b_dst, b_lnp], writes=[b_dst])
        P.op(POOL, "tensor_tensor", dict(out=dst, in0=dst, in1=lnp[:, 1, :], op=ALU.add),
             reads=[b_dst, b_lnp], writes=[b_dst])

    if stage != "b":
        qsT = sb.alloc("qsT", [128, 8, T], BF16)
        ksT = sb.alloc("ksT", [128, 2, T + 256], BF16)
        vsx = sb.alloc("vsx", [128, 18, 256], BF16)
        msk = sb.alloc("msk", [128, 3, 384], F32)
        gswa = sb.alloc("gswa", [128, 1024], F32)
        b_swa_in = Buf("swa_in")
        P.dma(SP, qsT[:], qs_d.rearrange("(h p) t -> p h t", p=128), writes=[b_swa_in])
        P.dma(SP, ksT[:], ks_d.rearrange("(h p) t -> p h t", p=128), writes=[b_swa_in])
        P.dma(SP, vsx[:], vs_d.rearrange("(i p) c -> p i c", p=128), writes=[b_swa_in])
        P.dma(SP, msk[:], masks_d[:, :, :], writes=[b_swa_in])
        P.dma(SP, gswa[:], gswa_d[:, :], writes=[b_swa_in])
        NR = 3
        Sm = [sb.alloc(f"Sm{r}", [128, 384], F32) for r in range(NR)]
        b_Sm = [Buf(f"Sm{r}") for r in range(NR)]
        Pb = [sb.alloc(f"Pb{r}", [128, 384], BF16) for r in range(NR)]
        b_Pb = [Buf(f"Pb{r}") for r in range(NR)]
        PTs = [sb.alloc(f"PTs{r}", [128, 3, 128], BF16) for r in range(NR)]
        b_PTs = [Buf(f"PTs{r}") for r in range(NR)]
        st, b_st = small_pool("st", 8 * NR)
        otile = [sb.alloc(f"otile{r}", [128, 1024], F32) for r in range(2)]
        b_otile = [Buf(f"otile{r}") for r in range(2)]
        obf = [sb.alloc(f"obf{r}", [128, 1024], BF16) for r in range(2)]
        b_obf = [Buf(f"obf{r}") for r in range(2)]
        junk = sb.alloc("junk", [128, 2048], BF16)
        b_junk = Buf("junk")
        st2, b_st2 = small_pool("st2", 4)
        scale_s = 128 ** -0.5
        it = 0
        for i in range(16):
            mi = 0 if i == 0 else (2 if i == 15 else 1)
            ot, b_ot = otile[i % 2], b_otile[i % 2]
            for h in range(8):
                kvh = h // 4
                r = it % NR
                it += 1
                s = [st[8 * r + q] for q in range(8)]
                bs = [b_st[8 * r + q] for q in range(8)]
                pa, pb = bank()
                P.op(PE, "matmul", dict(out=pa[:, :384], lhsT=qsT[:, h, i * 128:(i + 1) * 128],
                                        rhs=ksT[:, kvh, i * 128:i * 128 + 384], start=True, stop=True),
                     reads=[b_swa_in], writes=[pb])
                P.op(DVE, "scalar_tensor_tensor", dict(out=Sm[r][:], in0=pa[:, :384], scalar=scale_s, in1=msk[:, mi, :],
                                                       op0=ALU.mult, op1=ALU.add), reads=[pb, b_swa_in], writes=[b_Sm[r]])
                P.op(DVE, "tensor_reduce", dict(out=s[0][:], in_=Sm[r][:], axis=AX.X, op=ALU.max),
                     reads=[b_Sm[r]], writes=[bs[0]])
                P.op(DVE, "tensor_tensor", dict(out=s[1][:], in0=s[0][:], in1=sink[:, h:h + 1], op=ALU.max),
                     reads=[bs[0], b_p], writes=[bs[1]])
                P.op(DVE, "tensor_scalar", dict(out=s[2][:], in0=s[1][:], scalar1=-1.0, scalar2=None, op0=ALU.mult),
                     reads=[bs[1]], writes=[bs[2]])
                P.op(ACT, "activation", dict(out=Pb[r][:], in_=Sm[r][:], func=AF.Exp, bias=s[2][:], accum_out=s[3][:]),
                     reads=[b_Sm[r], bs[2]], writes=[b_Pb[r], bs[3]])
                P.op(ACT, "activation", dict(out=s[4][:], in_=sink[:, h:h + 1], func=AF.Exp, bias=s[2][:]),
                     reads=[bs[2], b_p], writes=[bs[4]])
                P.op(DVE, "tensor_tensor", dict(out=s[5][:], in0=s[3][:], in1=s[4][:], op=ALU.add),
                     reads=[bs[3], bs[4]], writes=[bs[5]])
                P.op(DVE, "reciprocal", dict(out=s[6][:], in_=s[5][:]), reads=[bs[5]], writes=[bs[6]])
                pa2, pb2 = bank()
                pT = pa2.bitcast(BF16)[:, :384].rearrange("p (a b) -> p a b", b=128)
                for j in range(3):
                    P.op(PE, "transpose", dict(out=pT[:, j, :], in_=Pb[r][:, j * 128:(j + 1) * 128], identity=ident[:]),
                         reads=[b_Pb[r], b_c], writes=[pb2], sig=(j == 2))
                P.op(ACT, "copy", dict(out=PTs[r][:], in_=pT), reads=[pb2], writes=[b_PTs[r]])
                pa3, pb3 = bank()
                mm_acc(P, pa3[:, :128], pb3,
                       [(PTs[r][:, j, :], vsx[:, i + j, kvh * 128:(kvh + 1) * 128]) for j in range(3)],
                       [b_PTs[r], b_swa_in])
                P.op(DVE, "tensor_scalar", dict(out=ot[:, h * 128:(h + 1) * 128], in0=pa3[:, :128], scalar1=s[6][:],
                                                scalar2=None, op0=ALU.mult), reads=[pb3, bs[6]], writes=[b_ot])
            q4 = [st2[q] for q in range(4)]
            bq = [b_st2[q] for q in range(4)]
            P.op(ACT, "activation", dict(out=junk[:, :1024], in_=ot[:], func=AF.Square, accum_out=q4[0][:]),
                 reads=[b_ot], writes=[b_junk, bq[0]])
            P.op(DVE, "tensor_scalar", dict(out=q4[1][:], in0=q4[0][:], scalar1=1.0 / 1024, scalar2=RMS_EPS,
                                            op0=ALU.mult, op1=ALU.add), reads=[bq[0]], writes=[bq[1]])
            P.op(ACT, "activation", dict(out=q4[2][:], in_=q4[1][:], func=AF.Sqrt), reads=[bq[1]], writes=[bq[2]])
            P.op(DVE, "reciprocal", dict(out=q4[3][:], in_=q4[2][:]), reads=[bq[2]], writes=[bq[3]])
            ob, b_ob = obf[i % 2], b_obf[i % 2]
            P.op(DVE, "scalar_tensor_tensor", dict(out=ob[:], in0=ot[:], scalar=q4[3][:], in1=gswa[:],
                                                   op0=ALU.mult, op1=ALU.mult), reads=[b_ot, bq[3], b_swa_in], writes=[b_ob])
            pa4, pb4 = bank()
            pT = pa4.bitcast(BF16).rearrange("p (a b) -> p a b", b=128)
            for c in range(8):
                P.op(PE, "transpose", dict(out=pT[:, c, :], in_=ob[:, c * 128:(c + 1) * 128], identity=ident[:]),
                     reads=[b_ob, b_c], writes=[pb4], sig=(c == 7))
            P.op(ACT, "copy", dict(out=mT[:, 8:16, i * 128:(i + 1) * 128], in_=pT), reads=[pb4], writes=[b_mT[i]])

        P.barrier()
        sb.reset(m_mT)
        QP = 1024
        qn = sb.alloc("qn", [128, 8, QP], BF16)
        qpe = sb.alloc("qpe", [64, 8, QP], BF16)
        b_q = Buf("q")
        NK = 3
        Kc = [sb.alloc(f"Kc{r}", [128, T], BF16) for r in range(NK)]
        Vc = [sb.alloc(f"Vc{r}", [128, 16, 128], BF16) for r in range(NK)]
        Pc = [sb.alloc(f"Pc{r}", [64, T], BF16) for r in range(NK)]
        b_kv = [Buf(f"kvc{r}") for r in range(NK)]
        NP = 3
        PT = [sb.alloc(f"PT{r}", [128, 2, 512], BF16) for r in range(NP)]
        b_PT = [Buf(f"PT{r}") for r in range(NP)]
        oT = sb.alloc("oT", [128, 8, QP], F32)
        b_oT = Buf("oT")
        rs = [sb.alloc(f"rs{r}", [128, 512], F32) for r in range(2)]
        b_rs = [Buf(f"rs{r}") for r in range(2)]
        sq = sb.alloc("sqm", [128, 8, 512], BF16)
        b_sq = Buf("sqm")
        rstd = sb.alloc("rstdm", [128, 512], F32)
        b_rstd = Buf("rstdm")
        scale_m = 192 ** -0.5
        prange[0], prange[1] = 4, 8
        pctr[0] = 0
        cctr = 0
        pctr_pt = 0
        ones32 = sb.alloc("ones32", [128, 128], F32)
        b_o32 = Buf("ones32")
        P.op(POOL, "memset", dict(ap=ones32[:], constant=1.0), writes=[b_o32])
        accs = sb.alloc("accs", [128, 2, 512], F32)
        b_acc = Buf("accs")
        iters = [(qp, h, rk, kt) for qp in range(2) for h in range(8) for rk in range(8) for kt in range(16)]
        NIT = len(iters)
        SKEW = 1
        st_ = {}
        chunk_slot = {}
        ctrs = {"c": 0, "pt": 0, "sb": 0}

        def emit_S(j):
            qp, h, rk, kt = iters[j]
            q0 = qp * QP
            if h == 0 and rk == 0 and kt == 0:
                P.dma(SP, qn[:], qn_d.rearrange("(h p) t -> p h t", p=128)[:, :, q0:q0 + QP], writes=[b_q])
                P.dma(SP, qpe[:], qpe_d.rearrange("(h p) t -> p h t", p=64)[:, :, q0:q0 + QP], writes=[b_q])
            if kt == 0:
                r = ctrs["c"] % NK
                ctrs["c"] += 1
                chunk_slot[(qp, h, rk)] = r
                base = rk * KVROWS
                P.dma(SP, Kc[r][:], kv_all[base + 2 * h * 128:base + (2 * h + 1) * 128, :], writes=[b_kv[r]])
                P.dma(SP, Vc[r][:], kv_all[base + (2 * h + 1) * 128:base + (2 * h + 2) * 128, :]
                      .rearrange("p (i d) -> p i d", d=128), writes=[b_kv[r]])
                P.dma(SP, Pc[r][:], kv_all[base + 2048:base + 2048 + 64, :], writes=[b_kv[r]])
            r = chunk_slot[(qp, h, rk)]
            b0 = 4 + 2 * (ctrs["sb"] % 2)
            ctrs["sb"] += 1
            for qb in range(2):
                qsl = slice(qb * 512, (qb + 1) * 512)
                P.op(PE, "matmul", dict(out=ps[:, b0 + qb, :], lhsT=Kc[r][:, kt * 128:(kt + 1) * 128], rhs=qn[:, h, qsl],
                                        start=True, stop=False), reads=[b_kv[r], b_q], writes=[psb[b0 + qb]], sig=False)
                P.op(PE, "matmul", dict(out=ps[:, b0 + qb, :], lhsT=Pc[r][:, kt * 128:(kt + 1) * 128], rhs=qpe[:, h, qsl],
                                        start=False, stop=True), reads=[b_kv[r], b_q], writes=[psb[b0], psb[b0 + 1]],
                     sig=(qb == 1))
            pr = ctrs["pt"] % NP
            ctrs["pt"] += 1
            P.op(ACT, "activation", dict(out=PT[pr][:], in_=ps[:, b0:b0 + 2, :], func=AF.Exp, scale=scale_m),
                 reads=[psb[b0], psb[b0 + 1]], writes=[b_PT[pr]])
            st_[j] = (r, pr)

        def emit_PV(j):
            qp, h, rk, kt = iters[j]
            q0 = qp * QP
            r, pr = st_.pop(j)
            first = (rk == 0 and kt == 0)
            last = (rk == 7 and kt == 15)
            for qb in range(2):
                if kt % 2 == 0:
                    P.op(PE, "matmul", dict(out=ps[:, 2 + qb, :], lhsT=ones[:], rhs=PT[pr][:, qb, :], start=first, stop=False),
                         reads=[b_PT[pr], b_c], writes=[psb[2 + qb]], sig=False)
                P.op(PE, "matmul", dict(out=ps[:, qb, :], lhsT=Vc[r][:, kt, :], rhs=PT[pr][:, qb, :], start=first, stop=last),
                     reads=[b_PT[pr], b_kv[r]], writes=[psb[qb]], sig=(qb == 1 or last))
            if kt % 2 == 1:
                if rk == 0 and kt == 1:
                    P.op(DVE, "tensor_copy", dict(out=accs[:], in_=PT[pr][:]), reads=[b_PT[pr]], writes=[b_acc])
                else:
                    P.op(DVE, "tensor_tensor", dict(out=accs[:], in0=accs[:], in1=PT[pr][:], op=ALU.add),
                         reads=[b_PT[pr], b_acc], writes=[b_acc])
            if not last:
                return
            for qb in range(2):
                qsl = slice(qb * 512, (qb + 1) * 512)
                P.op(PE, "matmul", dict(out=ps[:, 2 + qb, :], lhsT=ones32[:], rhs=accs[:, qb, :], start=False, stop=True),
                     reads=[b_acc, b_o32], writes=[psb[2 + qb]])
                P.op(DVE, "reciprocal", dict(out=rs[qb][:], in_=ps[:, 2 + qb, :]), reads=[psb[2 + qb]], writes=[b_rs[qb]])
                P.op(DVE, "tensor_tensor", dict(out=oT[:, h, qsl], in0=ps[:, qb, :], in1=rs[qb][:], op=ALU.mult),
                     reads=[psb[qb], b_rs[qb]], writes=[b_oT])
            if h != 7:
                return
            for qb2 in range(2):
                qs2 = slice(qb2 * 512, (qb2 + 1) * 512)
                for h2 in range(8):
                    P.op(ACT, "activation", dict(out=sq[:, h2, :], in_=oT[:, h2, qs2], func=AF.Square),
                         reads=[b_oT], writes=[b_sq])
                sa, sbf = ps[:, 4 + 2 * qb2, :], psb[4 + 2 * qb2]
                mm_acc(P, sa, sbf, [(ones[:], sq[:, h2, :]) for h2 in range(8)], [b_sq, b_c])
                P.op(DVE, "tensor_scalar", dict(out=rstd[:], in0=sa, scalar1=1.0 / 1024, scalar2=RMS_EPS,
                                                op0=ALU.mult, op1=ALU.add), reads=[sbf], writes=[b_rstd])
                P.op(ACT, "activation", dict(out=rstd[:], in_=rstd[:], func=AF.Sqrt), reads=[b_rstd], writes=[b_rstd])
                P.op(DVE, "reciprocal", dict(out=rstd[:], in_=rstd[:]), reads=[b_rstd], writes=[b_rstd])
                tiles = [b_mT[(q0 + qb2 * 512) // 128 + q] for q in range(4)]
                for h2 in range(8):
                    P.op(DVE, "scalar_tensor_tensor", dict(
                        out=mT[:, h2, q0 + qb2 * 512:q0 + (qb2 + 1) * 512], in0=oT[:, h2, qs2], scalar=gmla[:, h2:h2 + 1],
                        in1=rstd[:], op0=ALU.mult, op1=ALU.mult), reads=[b_oT, b_rstd, b_p], writes=tiles)

        for j in range(NIT + SKEW):
            if j < NIT:
                emit_S(j)
            if j >= SKEW:
                emit_PV(j - SKEW)
        prange[0], prange[1] = 0, 8

        P.barrier()
        sb.reset(m_mT)
        wout = sb.alloc("wout", [128, 16, D], BF16)
        b_wout = Buf("wout")
        wv = wout_d.rearrange("(k p) c -> p k c", p=128)
        for cb in range(4):
            P.dma(POOL, wout[:, :, cb * 512:(cb + 1) * 512], wv[:, :, cb * 512:(cb + 1) * 512], writes=[b_wout])
        lnp = sb.alloc("lnpF", [128, 2, D], F32)
        b_lnp = Buf("lnpF")
        P.dma(SP, lnp[:], ln_d[:, 0:2, :], writes=[b_lnp])
        xt = [sb.alloc("xt0", [128, D], F32)] * 2
        b_xt = [Buf("xt0")] * 2
        yp = [sb.alloc(f"yp{r}", [128, D], F32) for r in range(2)]
        b_yp = [Buf(f"yp{r}") for r in range(2)]
        xo = yp
        b_xo = b_yp
        xb = [sb.alloc(f"xb{r}", [128, D], BF16) for r in range(2)]
        b_xb = [Buf(f"xb{r}") for r in range(2)]
        xTs = [sb.alloc(f"xTs{r}", [128, 16, 128], BF16) for r in range(2)]
        b_xTs = [Buf(f"xTs{r}") for r in range(2)]
        if moe:
            xlb = [sb.alloc("xlb0", [128, D], BF16)] * 2
            b_xlb = [Buf("xlb0")] * 2
            xlTs = [sb.alloc("xlTs0", [128, 16, 128], BF16)] * 2
            b_xlTs = [Buf("xlTs0")] * 2
            gs, b_gs = small_pool("gs", 12, 8)
        junk = sb.alloc("junkF", [128, D], BF16)
        b_junk = Buf("junkF")
        ls, b_ls = small_pool("ls", 16)

        for i in range(16):
            r = i % 2
            P.dma(SP, xt[r][:], x[i * 128:(i + 1) * 128, :], writes=[b_xt[r]])
            for cb in range(4):
                pa, pb = bank()
                mm_acc(P, pa, pb, [(mT[:, k, i * 128:(i + 1) * 128], wout[:, k, cb * 512:(cb + 1) * 512])
                                   for k in range(16)], [b_mT[i], b_wout])
                P.op(DVE, "scalar_tensor_tensor", dict(out=yp[r][:, cb * 512:(cb + 1) * 512],
                                                       in0=xt[r][:, cb * 512:(cb + 1) * 512], scalar=ALPHA, in1=pa,
                                                       op0=ALU.mult, op1=ALU.add),
                     reads=[pb, b_xt[r]], writes=[b_yp[r]])
            layer_norm(yp[r][:], b_yp[r], 0, xo[r][:], b_xo[r], r)
            P.dma(SP, x1s[i * 128:(i + 1) * 128, :], xo[r][:], reads=[b_xo[r]], writes=[b_x1s])
            P.op(ACT, "copy", dict(out=xb[r][:], in_=xo[r][:]), reads=[b_xo[r]], writes=[b_xb[r]])
            j = (pctr[0] // 2) % 4
            pctr[0] += 2
            pT = ps[:, 2 * j:2 * j + 2, :].bitcast(BF16).rearrange("p a (b c) -> p (a b) c", c=128)
            pbufs = [psb[2 * j], psb[2 * j + 1]]
            for k in range(16):
                P.op(PE, "transpose", dict(out=pT[:, k, :], in_=xb[r][:, k * 128:(k + 1) * 128], identity=ident[:]),
                     reads=[b_xb[r], b_c], writes=pbufs, sig=(k == 15))
            P.op(DVE, "tensor_copy", dict(out=xTs[r][:], in_=pT), reads=pbufs, writes=[b_xTs[r]])
            P.dma(SP, x1T_d[:, :, i * 128:(i + 1) * 128], xTs[r][:], reads=[b_xTs[r]], writes=[b_x1T])
            if moe:
                P.op(DVE, "tensor_tensor", dict(out=xlb[r][:], in0=xo[r][:], in1=xb[r][:], op=ALU.subtract),
                     reads=[b_xo[r], b_xb[r]], writes=[b_xlb[r]])
                j = (pctr[0] // 2) % 4
                pctr[0] += 2
                pT2 = ps[:, 2 * j:2 * j + 2, :].bitcast(BF16).rearrange("p a (b c) -> p (a b) c", c=128)
                pbufs2 = [psb[2 * j], psb[2 * j + 1]]
                for k in range(16):
                    P.op(PE, "transpose", dict(out=pT2[:, k, :], in_=xlb[r][:, k * 128:(k + 1) * 128], identity=ident[:]),
                         reads=[b_xlb[r], b_c], writes=pbufs2, sig=(k == 15))
                P.op(ACT, "copy", dict(out=xlTs[r][:], in_=pT2), reads=pbufs2, writes=[b_xlTs[r]])
                pl, pbl = bank()
                pairs = []
                for k in range(16):
                    pairs += [(xTs[r][:, k, :], wrh[:, k, :]), (xTs[r][:, k, :], wrl[:, k, :]), (xlTs[r][:, k, :], wrh[:, k, :])]
                mm_acc(P, pl[:, :8], pbl, pairs, [b_xTs[r], b_xlTs[r], b_wr])
                g = [gs[q] for q in range(12)]
                bg = [b_gs[q] for q in range(12)]
                P.op(DVE, "tensor_copy", dict(out=g[0][:], in_=pl[:, :8]), reads=[pbl], writes=[bg[0]])
                P.op(DVE, "tensor_reduce", dict(out=g[1][:, 0:1], in_=g[0][:], axis=AX.X, op=ALU.max), reads=[bg[0]], writes=[bg[1]])
                P.op(DVE, "tensor_scalar", dict(out=g[2][:], in0=g[0][:], scalar1=g[1][:, 0:1], scalar2=None, op0=ALU.is_equal),
                     reads=[bg[0], bg[1]], writes=[bg[2]])
                P.op(DVE, "scalar_tensor_tensor", dict(out=g[3][:], in0=g[2][:], scalar=NEG, in1=g[0][:], op0=ALU.mult, op1=ALU.add),
                     reads=[bg[2], bg[0]], writes=[bg[3]])
                P.op(DVE, "tensor_reduce", dict(out=g[4][:, 0:1], in_=g[3][:], axis=AX.X, op=ALU.max), reads=[bg[3]], writes=[bg[4]])
                P.op(DVE, "tensor_scalar", dict(out=g[5][:], in0=g[3][:], scalar1=g[4][:, 0:1], scalar2=None, op0=ALU.is_equal),
                     reads=[bg[3], bg[4]], writes=[bg[5]])
                P.op(DVE, "tensor_tensor", dict(out=g[6][:, 0:1], in0=g[4][:, 0:1], in1=g[1][:, 0:1], op=ALU.subtract),
                     reads=[bg[4], bg[1]], writes=[bg[6]])
                P.op(ACT, "activation", dict(out=g[7][:, 0:1], in_=g[6][:, 0:1], func=AF.Exp), reads=[bg[6]], writes=[bg[7]])
                P.op(DVE, "tensor_scalar", dict(out=g[8][:, 0:1], in0=g[7][:, 0:1], scalar1=1.0, scalar2=None, op0=ALU.add),
                     reads=[bg[7]], writes=[bg[8]])
                P.op(DVE, "reciprocal", dict(out=g[9][:, 0:1], in_=g[8][:, 0:1]), reads=[bg[8]], writes=[bg[9]])
                P.op(DVE, "tensor_tensor", dict(out=g[10][:, 0:1], in0=g[7][:, 0:1], in1=g[9][:, 0:1], op=ALU.mult),
                     reads=[bg[7], bg[9]], writes=[bg[10]])
                P.op(DVE, "tensor_scalar", dict(out=g[11][:], in0=g[5][:], scalar1=g[10][:, 0:1], scalar2=None, op0=ALU.mult),
                     reads=[bg[5], bg[10]], writes=[bg[11]])
                P.op(DVE, "scalar_tensor_tensor", dict(out=gates[:, i, :], in0=g[2][:], scalar=g[9][:, 0:1], in1=g[11][:],
                                                       op0=ALU.mult, op1=ALU.add),
                     reads=[bg[2], bg[9], bg[11]], writes=[b_gates])
                P.op(DVE, "tensor_tensor", dict(out=selm[:, i * 8:(i + 1) * 8], in0=g[2][:], in1=g[5][:], op=ALU.add),
                     reads=[bg[2], bg[5]], writes=[b_selm])
                P.dma(SP, xb_d[i * 128:(i + 1) * 128, :], xb[r][:], reads=[b_xb[r]], writes=[b_xbd])

    if stage == "a":
        P.barrier()
        sb.reset(m_const)
        Lm = sb.alloc("Lm", [128, 128], BF16)
        b_cm = Buf("moe_consts")
        P.dma(SP, Lm[:], lmat_d[:, :], writes=[b_cm])
        selb = sb.alloc("selb", [128, 128], BF16)
        b_selb = Buf("selb")
        rk = [sb.alloc(f"rk{q}", [128, 128], F32) for q in range(4)]
        b_rk = [Buf(f"rk{q}") for q in range(4)]
        cnt = sb.alloc("cnt", [128, 8], F32)
        b_cnt = Buf("cnt")
        P.op(ACT, "copy", dict(out=selb[:], in_=selm[:]), reads=[b_selm], writes=[b_selb])
        pw, pbw = bank()
        P.op(PE, "matmul", dict(out=pw[:, :128], lhsT=Lm[:], rhs=selb[:], start=True, stop=True),
             reads=[b_selb, b_cm], writes=[pbw])
        pt_, pbt = bank()
        P.op(PE, "matmul", dict(out=pt_[:, :128], lhsT=ones[:], rhs=selb[:], start=True, stop=True),
             reads=[b_selb, b_c], writes=[pbt])
        P.op(DVE, "tensor_copy", dict(out=rk[1][:], in_=pt_[:, :128]), reads=[pbt], writes=[b_rk[1]])
        P.op(DVE, "memset", dict(ap=rk[2][:, 0:8], constant=0.0), writes=[b_rk[2]])
        for i in range(1, 16):
            P.op(DVE, "tensor_tensor", dict(out=rk[2][:, i * 8:(i + 1) * 8], in0=rk[2][:, (i - 1) * 8:i * 8],
                                            in1=rk[1][:, (i - 1) * 8:i * 8], op=ALU.add),
                 reads=[b_rk[1], b_rk[2]], writes=[b_rk[2]])
        P.op(DVE, "tensor_tensor", dict(out=cnt[:], in0=rk[2][:, 120:128], in1=rk[1][:, 120:128], op=ALU.add),
             reads=[b_rk[1], b_rk[2]], writes=[b_cnt])
        P.op(DVE, "tensor_tensor", dict(out=rk[0][:], in0=pw[:, :128], in1=rk[2][:], op=ALU.add),
             reads=[pbw, b_rk[2]], writes=[b_rk[0]])
        P.op(DVE, "scalar_tensor_tensor", dict(out=rk[3][:], in0=rk[0][:], scalar=1.0, in1=selm[:],
                                               op0=ALU.add, op1=ALU.mult), reads=[b_rk[0], b_selm], writes=[b_rk[3]])
        P.op(DVE, "tensor_scalar", dict(out=rk[3][:], in0=rk[3][:], scalar1=-1.0, scalar2=None, op0=ALU.add),
             reads=[b_rk[3]], writes=[b_rk[3]])
        P.dma(SP, gates_o[:, :], gates[:].rearrange("p a b -> p (a b)"), reads=[b_gates], writes=[b_out])
        P.dma(SP, rank_o[:, :], rk[3][:], reads=[b_rk[3]], writes=[b_out])
        P.dma(SP, cnt_o[:, :], cnt[:], reads=[b_cnt], writes=[b_out])
        P.final_wait(SP, [b_out, b_x1s, b_xbd])
        with nc.Block() as block:
            P.emit(block)
        return nc

    P.barrier()
    sb.reset(m_const)
    if stage == "all":
        HT = 1024
        x1T = sb.alloc("x1T", [128, 16, HT], BF16)
        b_x1Th = Buf("x1Th")
        yacc = sb.alloc("yacc", [128, HT // 128, D], F32)
        b_yacc = [Buf(f"yacc{t}") for t in range(HT // 128)]
        GF = 256
        NW = 2
        wgb = [sb.alloc(f"wgb{r}", [128, 16, GF], BF16) for r in range(NW)]
        wub = [sb.alloc(f"wub{r}", [128, 16, GF], BF16) for r in range(NW)]
        wdb = [sb.alloc(f"wdb{r}", [128, GF // 128, D], BF16) for r in range(NW)]
        b_w = [Buf(f"wffn{r}") for r in range(NW)]
        sg = [sb.alloc(f"sg{r}", [128, 512], F32) for r in range(2)]
        b_sg = [Buf(f"sg{r}") for r in range(2)]
        hT = [sb.alloc(f"hT{r}", [128, GF // 128, 512], BF16) for r in range(2)]
        b_hT = [Buf(f"hT{r}") for r in range(2)]
        xt = [sb.alloc(f"xtG{r}", [128, D], F32) for r in range(2)]
        b_xt = [Buf(f"xtG{r}") for r in range(2)]
        lnp = sb.alloc("lnpG", [128, 2, D], F32)
        b_lnp = Buf("lnpG")
        P.dma(SP, lnp[:], ln_d[:, 2:4, :], writes=[b_lnp])
        junk = sb.alloc("junkG", [128, D], BF16)
        b_junk = Buf("junkG")
        ls, b_ls = small_pool("lsG", 16)
        n_exp = 8 if moe else 1
        ngrp = F // GF
        wctr = 0
        sgc = 0
        hc = 0
        for half in range(2):
            t0 = half * HT
            P.dma(SP, x1T[:], x1T_d[:, :, t0:t0 + HT], reads=[b_x1T], writes=[b_x1Th])
            for e in range(n_exp):
                for g in range(ngrp):
                    r = wctr % NW
                    wctr += 1
                    f0 = g * GF
                    P.dma(POOL, wgb[r][:], wg_d[e].rearrange("(k p) f -> p k f", p=128)[:, :, f0:f0 + GF], writes=[b_w[r]])
                    P.dma(POOL, wub[r][:], wu_d[e].rearrange("(k p) f -> p k f", p=128)[:, :, f0:f0 + GF], writes=[b_w[r]])
                    P.dma(POOL, wdb[r][:], wd_d[e, f0:f0 + GF, :].rearrange("(c p) d -> p c d", p=128), writes=[b_w[r]])
                    for tb in range(HT // 512):
                        tsl = slice(tb * 512, (tb + 1) * 512)
                        hr = hc % 2
                        hc += 1
                        for c in range(GF // 128):
                            pg, pbg = bank()
                            mm_acc(P, pg, pbg, [(wgb[r][:, k, c * 128:(c + 1) * 128], x1T[:, k, tsl]) for k in range(16)],
                                   [b_w[r], b_x1Th])
                            pu, pbu = bank()
                            mm_acc(P, pu, pbu, [(wub[r][:, k, c * 128:(c + 1) * 128], x1T[:, k, tsl]) for k in range(16)],
                                   [b_w[r], b_x1Th])
                            sr = sgc % 2
                            sgc += 1
                            P.op(ACT, "activation", dict(out=sg[sr][:], in_=pg, func=AF.Silu), reads=[pbg], writes=[b_sg[sr]])
                            P.op(DVE, "tensor_tensor", dict(out=hT[hr][:, c, :], in0=pu, in1=sg[sr][:], op=ALU.mult),
                                 reads=[pbu, b_sg[sr]], writes=[b_hT[hr]])
                        for t in range(4):
                            tt = tb * 4 + t
                            for cb in range(4):
                                pa, pb = bank()
                                mm_acc(P, pa, pb, [(hT[hr][:, c, t * 128:(t + 1) * 128], wdb[r][:, c, cb * 512:(cb + 1) * 512])
                                                   for c in range(GF // 128)], [b_hT[hr], b_w[r]])
                                dst = yacc[:, tt, cb * 512:(cb + 1) * 512]
                                gsc = gates[:, half * (HT // 128) + tt, e:e + 1]
                                if e == 0 and g == 0:
                                    if moe:
                                        P.op(DVE, "tensor_scalar", dict(out=dst, in0=pa, scalar1=gsc, scalar2=None, op0=ALU.mult),
                                             reads=[pb, b_gates], writes=[b_yacc[tt]])
                                    else:
                                        P.op(DVE, "tensor_copy", dict(out=dst, in_=pa), reads=[pb], writes=[b_yacc[tt]])
                                elif moe:
                                    P.op(DVE, "scalar_tensor_tensor", dict(out=dst, in0=pa, scalar=gsc, in1=dst,
                                                                           op0=ALU.mult, op1=ALU.add),
                                         reads=[pb, b_yacc[tt], b_gates], writes=[b_yacc[tt]])
                                else:
                                    P.op(DVE, "tensor_tensor", dict(out=dst, in0=pa, in1=dst, op=ALU.add),
                                         reads=[pb, b_yacc[tt]], writes=[b_yacc[tt]])
            for tt in range(HT // 128):
                r = tt % 2
                row0 = t0 + tt * 128
                P.dma(SP, xt[r][:], x1s[row0:row0 + 128, :], reads=[b_x1s], writes=[b_xt[r]])
                P.op(DVE, "scalar_tensor_tensor", dict(out=yacc[:, tt, :], in0=xt[r][:], scalar=ALPHA, in1=yacc[:, tt, :],
                                                       op0=ALU.mult, op1=ALU.add),
                     reads=[b_xt[r], b_yacc[tt]], writes=[b_yacc[tt]])
                layer_norm(yacc[:, tt, :], b_yacc[tt], 2, xt[r][:], b_xt[r], r)
                P.dma(SP, y_out[row0:row0 + 128, :], xt[r][:], reads=[b_xt[r]], writes=[b_out])

    if stage == "b":
        CAPMAX = MOE_CAPMAX
        GF = 256
        NW = 2
        ngrp = F // GF
        ls, b_ls = small_pool("lsG", 16)
        iot = sb.alloc("iot", [128, CAPMAX], F32)
        iot2 = sb.alloc("iot2", [128, CAPMAX], F32)
        b_cm = Buf("moe_consts")
        b_iot2 = Buf("iot2")
        P.dma(SP, iot[:], iota_d[:, :], writes=[b_cm])
        rankm = sb.alloc("rankm", [128, 128], F32)
        b_rankm = Buf("rankm")
        P.dma(SP, rankm[:], rank_i[:, :], writes=[b_rankm])
        P.dma(SP, gates[:].rearrange("p a b -> p (a b)"), gates_i[:, :], writes=[b_gates])
        NSL = 2
        Sel = [sb.alloc(f"Sel{r}", [128, CAPMAX], BF16) for r in range(NSL)]
        b_Sel = [Buf(f"Sel{r}") for r in range(NSL)]
        xeT = sb.alloc("xeT", [128, 16 * CAPMAX], BF16)
        b_xeT = Buf("xeT")
        yeacc = sb.alloc("yeacc", [128, CAPMAX // 128, D], F32)
        b_ye = [Buf(f"ye{q}") for q in range(CAPMAX // 128)]
        m_w = sb.mark()
        lnp = sb.alloc("lnpG", [128, 2, D], F32)
        junk = sb.alloc("junkG", [128, D], BF16)
        ls_x = sb.alloc("lsx0", [128, D], F32)
        b_lnp = Buf("lnpG")
        b_junk = Buf("junkG")
        b_lsx = b_junk
        sb.reset(m_w)
        wgb = [sb.alloc(f"wgb{r}", [128, 16, GF], BF16) for r in range(NW)]
        wub = [sb.alloc(f"wub{r}", [128, 16, GF], BF16) for r in range(NW)]
        wdb = [sb.alloc(f"wdb{r}", [128, GF // 128, D], BF16) for r in range(NW)]
        b_w = [Buf(f"wffn{r}") for r in range(NW)]
        NX = 2
        xtok = [sb.alloc(f"xtok{r}", [128, D], BF16) for r in range(NX)]
        b_xtok = [Buf(f"xtok{r}") for r in range(NX)]
        sg = [sb.alloc(f"sg{r}", [128, 512], F32) for r in range(2)]
        b_sg = [Buf(f"sg{r}") for r in range(2)]
        hT = [sb.alloc(f"hT{r}", [128, GF // 128, CAPMAX], BF16) for r in range(2)]
        b_hT = [Buf(f"hT{r}") for r in range(2)]
        selT = [sb.alloc(f"selT{r}", [128, CAPMAX // 128, 128], BF16) for r in range(2)]
        b_selT = [Buf(f"selT{r}") for r in range(2)]
        ybuf = [sb.alloc(f"ybuf{r}", [128, D], F32) for r in range(2)]
        b_ybuf = [Buf(f"ybuf{r}") for r in range(2)]
        b_yd = [Buf(f"yacc_d{i}") for i in range(16)]
        segs = []
        for e in range(8):
            c0 = 0
            while c0 < caps[e]:
                segs.append((e, c0, min(CAPMAX, caps[e] - c0)))
                c0 += CAPMAX
        assert segs
        xctr = 0
        wctr = 0
        sgc = 0
        hc = 0
        yc = 0
        selctr = [0]
        for si, (e, base, cp) in enumerate(segs):
            first_seg = (si == 0)
            last_seg = (si == len(segs) - 1)
            NS = cp // 128
            cblk = [(0, cp)] if cp <= 512 else [(0, cp // 2), (cp // 2, cp)]
            xe = xeT[:, 0:16 * cp].rearrange("p (k c) -> p k c", c=cp)
            ye_bf = xeT[:, 0:NS * D].rearrange("p (s d) -> p s d", d=D)
            P.op(DVE, "tensor_scalar", dict(out=iot2[:, :cp], in0=iot[:, :cp], scalar1=float(base), scalar2=None, op0=ALU.add),
                 reads=[b_cm], writes=[b_iot2])

            def build_sel(i):
                selctr[0] += 1
                r_ = selctr[0] % NSL
                P.op(DVE, "tensor_scalar", dict(out=Sel[r_][:, :cp], in0=iot2[:, :cp],
                                                scalar1=rankm[:, i * 8 + e:i * 8 + e + 1], scalar2=None, op0=ALU.is_equal),
                     reads=[b_iot2, b_rankm], writes=[b_Sel[r_]])
                return Sel[r_], b_Sel[r_]
            nb = len(cblk)
            kgsz = 8 // nb
            for kg in range(16 // kgsz):
                for i in range(16):
                    xr = xctr % NX
                    xctr += 1
                    P.dma(SP, xtok[xr][:], xb_d[i * 128:(i + 1) * 128, :], reads=[b_xbd], writes=[b_xtok[xr]])
                    S_, bS_ = build_sel(i)
                    for kk in range(kgsz):
                        k = kgsz * kg + kk
                        for bi_, (a0, a1) in enumerate(cblk):
                            bi = nb * kk + bi_
                            lastmm = (kk == kgsz - 1 and bi_ == nb - 1)
                            P.op(PE, "matmul", dict(out=ps[:, bi, :a1 - a0], lhsT=xtok[xr][:, k * 128:(k + 1) * 128],
                                                    rhs=S_[:, a0:a1], start=(i == 0), stop=(i == 15)),
                                 reads=[b_xtok[xr], bS_], writes=[psb[bi]], sig=(i == 15 or lastmm))
                for kk in range(kgsz):
                    k = kgsz * kg + kk
                    for bi_, (a0, a1) in enumerate(cblk):
                        bi = nb * kk + bi_
                        if bi % 2 == 0:
                            P.op(ACT, "copy", dict(out=xe[:, k, a0:a1], in_=ps[:, bi, :a1 - a0]),
                                 reads=[psb[bi]], writes=[b_xeT])
                        else:
                            P.op(DVE, "tensor_copy", dict(out=xe[:, k, a0:a1], in_=ps[:, bi, :a1 - a0]),
                                 reads=[psb[bi]], writes=[b_xeT])
            for g in range(ngrp):
                r = wctr % NW
                wctr += 1
                f0 = g * GF
                P.dma(POOL, wgb[r][:], wg_d[e].rearrange("(k p) f -> p k f", p=128)[:, :, f0:f0 + GF], writes=[b_w[r]])
                P.dma(POOL, wub[r][:], wu_d[e].rearrange("(k p) f -> p k f", p=128)[:, :, f0:f0 + GF], writes=[b_w[r]])
                P.dma(POOL, wdb[r][:], wd_d[e, f0:f0 + GF, :].rearrange("(c p) d -> p c d", p=128), writes=[b_w[r]])
                hr = hc % 2
                hc += 1
                for c in range(GF // 128):
                    for (a0, a1) in cblk:
                        w_ = a1 - a0
                        pg, pbg = bank()
                        mm_acc(P, pg[:, :w_], pbg, [(wgb[r][:, k, c * 128:(c + 1) * 128], xe[:, k, a0:a1]) for k in range(16)],
                               [b_w[r], b_xeT])
                        pu, pbu = bank()
                        mm_acc(P, pu[:, :w_], pbu, [(wub[r][:, k, c * 128:(c + 1) * 128], xe[:, k, a0:a1]) for k in range(16)],
                               [b_w[r], b_xeT])
                        sr = sgc % 2
                        sgc += 1
                        P.op(ACT, "activation", dict(out=sg[sr][:, :w_], in_=pg[:, :w_], func=AF.Silu), reads=[pbg], writes=[b_sg[sr]])
                        P.op(DVE, "tensor_tensor", dict(out=hT[hr][:, c, a0:a1], in0=pu[:, :w_], in1=sg[sr][:, :w_], op=ALU.mult),
                             reads=[pbu, b_sg[sr]], writes=[b_hT[hr]])
                for s_ in range(NS):
                    for cb in range(4):
                        pa, pb = bank()
                        mm_acc(P, pa, pb, [(hT[hr][:, c, s_ * 128:(s_ + 1) * 128], wdb[r][:, c, cb * 512:(cb + 1) * 512])
                                           for c in range(GF // 128)], [b_hT[hr], b_w[r]])
                        dst = yeacc[:, s_, cb * 512:(cb + 1) * 512]
                        if g == 0:
                            P.op(DVE, "tensor_copy", dict(out=dst, in_=pa), reads=[pb], writes=[b_ye[s_]])
                        else:
                            P.op(DVE, "tensor_tensor", dict(out=dst, in0=pa, in1=dst, op=ALU.add),
                                 reads=[pb, b_ye[s_]], writes=[b_ye[s_]])
            for s_ in range(NS):
                P.op(ACT, "copy", dict(out=ye_bf[:, s_, :], in_=yeacc[:, s_, :]), reads=[b_ye[s_]], writes=[b_xeT])
            if last_seg:
                P.dma(SP, lnp[:], ln_d[:, 2:4, :], writes=[b_lnp, b_w[0], b_w[1], b_junk])
            for i in range(16):
                sr = i % 2
                S_, bS_ = build_sel(i)
                pq, pbq = bank()
                pT = pq.bitcast(BF16)[:, :NS * 128].rearrange("p (a b) -> p a b", b=128)
                for c in range(NS):
                    P.op(PE, "transpose", dict(out=pT[:, c, :], in_=S_[:, c * 128:(c + 1) * 128], identity=ident[:]),
                         reads=[bS_, b_c], writes=[pbq], sig=(c == NS - 1))
                P.op(ACT, "copy", dict(out=selT[sr][:, :NS, :], in_=pT), reads=[pbq], writes=[b_selT[sr]])
                yr = yc % 2
                yc += 1
                if not first_seg:
                    P.dma(SP, ybuf[yr][:], yacc_d[i * 128:(i + 1) * 128, :], reads=[b_yd[i]], writes=[b_ybuf[yr]])
                for cb in range(4):
                    pa, pb = bank()
                    mm_acc(P, pa, pb, [(selT[sr][:, c, :], ye_bf[:, c, cb * 512:(cb + 1) * 512]) for c in range(NS)],
                           [b_selT[sr], b_xeT])
                    dst = ybuf[yr][:, cb * 512:(cb + 1) * 512]
                    gsc = gates[:, i, e:e + 1]
                    if first_seg:
                        P.op(DVE, "tensor_scalar", dict(out=dst, in0=pa, scalar1=gsc, scalar2=None, op0=ALU.mult),
                             reads=[pb, b_gates], writes=[b_ybuf[yr]])
                    else:
                        P.op(DVE, "scalar_tensor_tensor", dict(out=dst, in0=pa, scalar=gsc, in1=dst,
                                                               op0=ALU.mult, op1=ALU.add),
                             reads=[pb, b_gates, b_ybuf[yr]], writes=[b_ybuf[yr]])
                if not last_seg:
                    P.dma(SP, yacc_d[i * 128:(i + 1) * 128, :], ybuf[yr][:], reads=[b_ybuf[yr]], writes=[b_yd[i]])
                else:
                    P.dma(SP, ls_x[:], x1s[i * 128:(i + 1) * 128, :], reads=[b_x1s], writes=[b_lsx])
                    P.op(DVE, "scalar_tensor_tensor", dict(out=ybuf[yr][:], in0=ls_x[:], scalar=ALPHA, in1=ybuf[yr][:],
                                                           op0=ALU.mult, op1=ALU.add),
                         reads=[b_lsx, b_ybuf[yr]], writes=[b_ybuf[yr]])
                    layer_norm(ybuf[yr][:], b_ybuf[yr], 2, ybuf[yr][:], b_ybuf[yr], i % 2)
                    P.dma(SP, y_out[i * 128:(i + 1) * 128, :], ybuf[yr][:], reads=[b_ybuf[yr]], writes=[b_out])

    P.final_wait(SP, [b_out])
    with nc.Block() as block:
        P.emit(block)
    return nc


def kernel(**inputs):
    inp = {k: np.asarray(v) for k, v in inputs.items()}
    x = inp["x"][0]
    xs = [np.ascontiguousarray(x[c * T:(c + 1) * T]) for c in range(NCORE)]
    cores = list(range(NCORE))
    for l in range(2):
        ncA = build_A()
        resA = run_bass_kernel_spmd(ncA, inputs_A(xs, l, inp), core_ids=cores).results
        moe = (l % 2 == 1)
        if not moe:
            ncB = build_B(False, 5632)
            resB = run_bass_kernel_spmd(ncB, inputs_B(xs, l, inp, resA, False), core_ids=cores).results
        else:
            ncBa = build_B(True, 7168, "a")
            resBa = run_bass_kernel_spmd(ncBa, inputs_B(xs, l, inp, resA, True), core_ids=cores).results
            cnt = np.stack([np.asarray(resBa[c]["cnt_o"])[0] for c in range(NCORE)])
            caps = [int(-(-int(round(float(cnt[:, e].max()))) // 128) * 128) for e in range(8)]
            ncBb = build_B(True, 7168, "b", caps)
            resB = run_bass_kernel_spmd(ncBb, inputs_Bb(l, inp, resBa), core_ids=cores).results
        xs = [np.asarray(resB[c]["y"], dtype=np.float32) for c in range(NCORE)]
    return np.concatenate(xs, axis=0)[None].astype(np.float32)
```
